# Optimizing a Trainium2 kernel written in Bass

```python
import jax, jax.numpy as jnp
from jax import lax
import numpy as np

D_MODEL = 1024
BATCH = 32
SEQ = 256
DEPTH = 4
DEC_BATCH = 4
DEC_SEQ = 4096
PAST_LEN = 512

GRID_W = 64
N_MIXERS = 2
N_HGRN = (DEPTH + 1) // 2
N_CONV = DEPTH // 2
HGRN_HEADS = 8
HGRN_KEY_DIM = D_MODEL // HGRN_HEADS
HGRN_VAL_DIM = D_MODEL // HGRN_HEADS
CHUNK = 32
CONV_K = 31
D_FF = 4 * D_MODEL
N_MOD = 6
EPS = 1e-6
K_MAX = 1.0 - 1e-6

kernel_name = 'hybrid_hgrn2_conformer_dit_step'

F32 = jnp.float32


def rmsnorm(x, w):
    xf = x.astype(F32)
    y = xf * lax.rsqrt(jnp.mean(xf * xf, axis=-1, keepdims=True) + EPS)
    return (y * w.astype(F32)).astype(x.dtype)


def grid_pos_embed(n_tok, dim):
    rows = n_tok // GRID_W
    rr, cc = jnp.meshgrid(jnp.arange(rows, dtype=F32), jnp.arange(GRID_W, dtype=F32), indexing='ij')
    nf = dim // 4
    omega = 1.0 / (10000.0 ** (jnp.arange(nf, dtype=F32) / nf))
    er = rr.reshape(-1)[:, None] * omega
    ec = cc.reshape(-1)[:, None] * omega
    return jnp.concatenate([jnp.sin(er), jnp.cos(er), jnp.sin(ec), jnp.cos(ec)], axis=-1)


def chunk_gla_scan(q, k, v, log_g, s0):
    B, T, H, K = q.shape
    V = v.shape[-1]
    nc = T // CHUNK

    def to_chunks(a):
        return a.reshape(B, nc, CHUNK, H, a.shape[-1]).transpose(1, 0, 3, 2, 4)

    tril = jnp.tril(jnp.ones((CHUNK, CHUNK), dtype=bool))[:, :, None]

    def step(S, inp):
        qc, kc, vc, gc = inp
        b = jnp.cumsum(gc, axis=2)
        diff = b[:, :, :, None, :] - b[:, :, None, :, :]
        decay = jnp.where(tril, jnp.exp(jnp.where(tril, diff, 0.0)), 0.0)
        scores = jnp.einsum('bhtk,bhsk,bhtsk->bhts', qc, kc, decay)
        o = jnp.einsum('bhts,bhsv->bhtv', scores, vc) + jnp.einsum('bhtk,bhkv->bhtv', qc * jnp.exp(b), S)
        b_last = b[:, :, -1, :]
        S = jnp.exp(b_last)[..., None] * S + jnp.einsum('bhsk,bhsv->bhkv', kc * jnp.exp(b_last[:, :, None, :] - b), vc)
        return S, o

    S, o = lax.scan(step, s0, (to_chunks(q), to_chunks(k), to_chunks(v), to_chunks(log_g)))
    o = o.transpose(1, 0, 3, 2, 4).reshape(B, T, H, V)
    return o, S


def hgrn2_mixer(h, w_in, lb_f, lb_b, g_norm, w_out, s0):
    B, T, _ = h.shape
    proj = (h @ w_in).astype(F32)
    q, v, f_f, f_b, g = jnp.split(proj, 5, axis=-1)

    def heads(a):
        return a.reshape(B, T, HGRN_HEADS, -1)

    q = heads(jax.nn.silu(q) * (HGRN_KEY_DIM ** -0.5))
    v = heads(v)

    def gates(f, lb):
        lb = lb.astype(F32)
        k = jnp.minimum((1.0 - lb) * jax.nn.sigmoid(-f), K_MAX)
        log_g = jnp.log1p(-k)
        return heads(k), heads(log_g)

    k_f, lg_f = gates(f_f, lb_f)
    k_b, lg_b = gates(f_b, lb_b)
    s0 = s0.astype(F32)
    o_f, S_f = chunk_gla_scan(q, k_f, v, lg_f, s0[:, 0])
    rev = lambda a: jnp.flip(a, axis=1)
    o_b, S_b = chunk_gla_scan(rev(q), rev(k_b), rev(v), rev(lg_b), s0[:, 1])
    o = o_f + rev(o_b)
    o = o * lax.rsqrt(jnp.mean(o * o, axis=-1, keepdims=True) + EPS)
    o = o.reshape(B, T, D_MODEL) * g_norm.astype(F32) * jax.nn.silu(g)
    out = o.astype(h.dtype) @ w_out
    return out, jnp.stack([S_f, S_b], axis=1)


def conformer_conv(h, w_pw1, w_dw, b_dw, ln_g, ln_b, w_pw2):
    u = h @ w_pw1
    a, gt = jnp.split(u, 2, axis=-1)
    u = a * jax.nn.sigmoid(gt)
    u = lax.conv_general_dilated(u, w_dw[:, None, :], window_strides=(1,),
                                 padding=[(CONV_K // 2, CONV_K // 2)],
                                 dimension_numbers=('NWC', 'WIO', 'NWC'),
                                 feature_group_count=D_MODEL) + b_dw
    uf = u.astype(F32)
    mu = jnp.mean(uf, axis=-1, keepdims=True)
    var = jnp.mean(jnp.square(uf - mu), axis=-1, keepdims=True)
    uf = (uf - mu) * lax.rsqrt(var + EPS) * ln_g.astype(F32) + ln_b.astype(F32)
    return jax.nn.silu(uf).astype(h.dtype) @ w_pw2


def sqrelu_mlp(h, w1, w2):
    return jnp.square(jax.nn.relu(h @ w1)) @ w2


def setup_inputs(seed: int = 0) -> dict:
    key = jax.random.key(seed)
    ks = jax.random.split(key, 24)

    def nrm(k, shape, s):
        return jax.random.normal(k, shape, F32) * s

    D = D_MODEL
    return {
        'x_prompt': nrm(ks[0], (BATCH, SEQ, D), 1.0),
        'x_sample': nrm(ks[1], (DEC_BATCH, DEC_SEQ, D), 1.0),
        'c': nrm(ks[2], (DEC_BATCH, D), 1.0),
        'state_hgrn': nrm(ks[3], (DEC_BATCH, N_HGRN, 2, HGRN_HEADS, HGRN_KEY_DIM, HGRN_VAL_DIM), 0.5),
        'c_ctx': nrm(ks[4], (D,), 1.0),
        'w_mod': nrm(ks[5], (DEPTH, D, N_MOD * D), 0.5 * D ** -0.5),
        'b_mod': nrm(ks[6], (DEPTH, N_MOD * D), 0.02),
        'norm_mix': 1.0 + nrm(ks[7], (DEPTH, D), 0.02),
        'norm_mlp': 1.0 + nrm(ks[8], (DEPTH, D), 0.02),
        'hgrn_w_in': nrm(ks[9], (N_HGRN, D, 5 * D), D ** -0.5),
        'hgrn_lb_fwd': nrm(ks[10], (N_HGRN, D), 1.0),
        'hgrn_lb_bwd': nrm(ks[11], (N_HGRN, D), 1.0),
        'hgrn_g_norm': 1.0 + nrm(ks[12], (N_HGRN, D), 0.02),
        'hgrn_w_out': nrm(ks[13], (N_HGRN, D, D), D ** -0.5),
        'conv_w_pw1': nrm(ks[14], (N_CONV, D, 2 * D), D ** -0.5),
        'conv_w_dw': nrm(ks[15], (N_CONV, CONV_K, D), CONV_K ** -0.5),
        'conv_b_dw': nrm(ks[16], (N_CONV, D), 0.02),
        'conv_ln_g': 1.0 + nrm(ks[17], (N_CONV, D), 0.02),
        'conv_ln_b': nrm(ks[18], (N_CONV, D), 0.02),
        'conv_w_pw2': nrm(ks[19], (N_CONV, D, D), D ** -0.5),
        'mlp_w1': nrm(ks[20], (DEPTH, D, D_FF), D ** -0.5),
        'mlp_w2': nrm(ks[21], (DEPTH, D_FF, D), D_FF ** -0.5),
        'final_norm': 1.0 + nrm(ks[22], (D,), 0.02),
    }


def reference(x_prompt, x_sample, c, state_hgrn, c_ctx, w_mod, b_mod, norm_mix, norm_mlp,
              hgrn_w_in, hgrn_lb_fwd, hgrn_lb_bwd, hgrn_g_norm, hgrn_w_out,
              conv_w_pw1, conv_w_dw, conv_b_dw, conv_ln_g, conv_ln_b, conv_w_pw2,
              mlp_w1, mlp_w2, final_norm):
    def lower_bounds(p):
        p = jax.nn.softmax(p.astype(F32), axis=0)
        return jnp.maximum(jnp.cumsum(p, axis=0) - p[0], 0.0)

    lbs_f = lower_bounds(hgrn_lb_fwd)
    lbs_b = lower_bounds(hgrn_lb_bwd)

    n_lat = x_sample.shape[1]
    xp = x_prompt
    xs = x_sample + grid_pos_embed(n_lat, D_MODEL).astype(x_sample.dtype)[None]
    zero_state = jnp.zeros((x_prompt.shape[0], 2, HGRN_HEADS, HGRN_KEY_DIM, HGRN_VAL_DIM), F32)
    new_states = []

    for i in range(DEPTH):
        m_ctx = jnp.split(jax.nn.silu(c_ctx) @ w_mod[i] + b_mod[i], N_MOD, axis=-1)
        m_lat = jnp.split((jax.nn.silu(c) @ w_mod[i] + b_mod[i])[:, None, :], N_MOD, axis=-1)
        hp = rmsnorm(xp, norm_mix[i]) * (1.0 + m_ctx[1]) + m_ctx[0]
        hs = rmsnorm(xs, norm_mix[i]) * (1.0 + m_lat[1]) + m_lat[0]
        if i % N_MIXERS == 0:
            a = i // N_MIXERS
            args = (hgrn_w_in[a], lbs_f[a], lbs_b[a], hgrn_g_norm[a], hgrn_w_out[a])
            mp, st = hgrn2_mixer(hp, *args, zero_state)
            new_states.append(st.astype(x_prompt.dtype))
            ms, _ = hgrn2_mixer(hs, *args, state_hgrn[:, a])
        else:
            b = i // N_MIXERS
            args = (conv_w_pw1[b], conv_w_dw[b], conv_b_dw[b], conv_ln_g[b], conv_ln_b[b], conv_w_pw2[b])
            mp = conformer_conv(hp, *args)
            ms = conformer_conv(hs, *args)
        xp = xp + m_ctx[2] * mp
        xs = xs + m_lat[2] * ms
        hp = rmsnorm(xp, norm_mlp[i]) * (1.0 + m_ctx[4]) + m_ctx[3]
        hs = rmsnorm(xs, norm_mlp[i]) * (1.0 + m_lat[4]) + m_lat[3]
        xp = xp + m_ctx[5] * sqrelu_mlp(hp, mlp_w1[i], mlp_w2[i])
        xs = xs + m_lat[5] * sqrelu_mlp(hs, mlp_w1[i], mlp_w2[i])

    y_prompt = rmsnorm(xp, final_norm)
    y_sample = rmsnorm(xs, final_norm)
    new_state_hgrn = jnp.stack(new_states, axis=1)
    return (y_prompt, y_sample, new_state_hgrn)
```

```python
import math
import types
from contextlib import ExitStack

import numpy as np
import concourse.bass as bass
import concourse.mybir as mybir
from concourse.bass_utils import run_bass_kernel_spmd

F32 = mybir.dt.float32
BF16 = mybir.dt.bfloat16
I32 = mybir.dt.int32
AF = mybir.ActivationFunctionType
ALU = mybir.AluOpType

D = 1024
DEPTH = 4
NH = 8
CH = 32
KTAP = 31
EPS = 1e-6
KMAX = 1.0 - 1e-6
TS = 2048
TP = 1024
NDS = 24
NHW = 16

R_NMIX, R_NMLP, R_LB1, R_LB2, R_GN, R_BDW, R_LNG, R_LNB, R_FN, R_CV, R_BMOD, R_WDW = 0, 4, 8, 10, 12, 14, 16, 18, 20, 21, 23, 47


def _freeze(fn):
    if getattr(fn, "__closure__", None) is None:
        return fn
    cells = []
    for c in fn.__closure__:
        try:
            cells.append(types.CellType(c.cell_contents))
        except ValueError:
            cells.append(c)
    return types.FunctionType(fn.__code__, fn.__globals__, fn.__name__, fn.__defaults__, tuple(cells))


class Eng:
    def __init__(self, name, sem):
        self.name, self.sem, self.ops, self.n, self.seen = name, sem, [], 0, {}


class Prog:
    def __init__(self, nc, es):
        self.nc, self.es = nc, es
        self.E = {n: Eng(n, es.enter_context(nc.semaphore("s_" + n))) for n in ("pe", "act", "dve", "pool", "sp")}
        self.lastw, self.readers = {}, {}
        self.dsems = [es.enter_context(nc.semaphore("d%d" % i)) for i in range(NDS)]
        self.dcount = [0] * NDS
        self.dnext = 0
        self.dnext_sw = 0
        self.out_events = []

    def _wait(self, eng, ev):
        if ev[0] == 'c':
            _, src, seq = ev
            key, sem, val = src.name, src.sem, seq
        else:
            _, sem, val, key = ev
        if ev[0] == 'c' and eng.name == 'pe' and ev[1] is eng:
            return
        if eng.seen.get(key, 0) >= val:
            return
        eng.seen[key] = val
        eng.ops.append(lambda e, sem=sem, v=val: e.wait_ge(sem, v))

    def _deps(self, eng, reads, writes):
        for r in reads:
            if r in self.lastw:
                self._wait(eng, self.lastw[r])
        for w in writes:
            if w in self.lastw:
                self._wait(eng, self.lastw[w])
            for ev in self.readers.get(w, {}).values():
                self._wait(eng, ev)

    def _record(self, ev, key, reads, writes):
        for r in reads:
            self.readers.setdefault(r, {})[key] = ev
        for w in writes:
            self.lastw[w] = ev
            self.readers[w] = {}

    def op(self, en, fn, reads=(), writes=()):
        eng = self.E[en]
        fn = _freeze(fn)
        self._deps(eng, reads, writes)
        eng.n += 1
        ev = ('c', eng, eng.n)
        sem = eng.sem
        eng.ops.append(lambda e, fn=fn, sem=sem: fn(e).then_inc(sem, 1))
        self._record(ev, eng.name, reads, writes)
        return ev

    def dma(self, qn, out, in_, reads=(), writes=(), is_out=False):
        q = self.E[qn]
        self._deps(q, reads, writes)
        if qn == 'pool':
            i = NHW + self.dnext_sw
            self.dnext_sw = (self.dnext_sw + 1) % (NDS - NHW)
        else:
            i = self.dnext
            self.dnext = (i + 1) % NHW
        sem = self.dsems[i]
        key = ('d', i)
        if self.dcount[i] > 0:
            self._wait(q, ('d', sem, self.dcount[i], key))
        self.dcount[i] += 16
        ev = ('d', sem, self.dcount[i], key)
        q.ops.append(lambda e, o=out, a=in_, sem=sem: e.dma_start(out=o, in_=a).then_inc(sem, 16))
        self._record(ev, key, reads, writes)
        if is_out:
            self.out_events.append(ev)
        return ev

    def special(self, en, fn, sem, val, key, reads=(), writes=()):
        eng = self.E[en]
        fn = _freeze(fn)
        self._deps(eng, reads, writes)
        ev = ('d', sem, val, key)
        eng.ops.append(lambda e, fn=fn, sem=sem, val=val: fn(e).then_inc(sem, val))
        self._record(ev, key, reads, writes)
        self._wait(eng, ev)
        return ev

    def barrier(self, skip=()):
        evs = [('c', e, e.n) for e in self.E.values() if e.n > 0]
        for e in self.E.values():
            if e.name in skip:
                continue
            for ev in evs:
                self._wait(e, ev)
            for i in range(NDS):
                if self.dcount[i] > 0:
                    self._wait(e, ('d', self.dsems[i], self.dcount[i], ('d', i)))


def build_program(n_layers=DEPTH, phases=(0, 1), cc_inc=1, tiny=False):
    nc = bass.Bass("TRN2", target_bir_lowering=False)
    dt = lambda name, shape, kind="ExternalInput": nc.dram_tensor(name, list(shape), F32, kind=kind).ap()
    xs_d = dt("xs", [TS, D])
    xp_d = dt("xp", [TP, D])
    vecs_d = dt("vecs", [128, D])
    sinit_d = dt("s_init", [2, NH, 128, 128])
    pinfo_d = dt("pinfo", [128, 8])
    wmod_d = dt("w_mod", [1, 1] if tiny else [DEPTH, D, 6 * D])
    whg_d = dt("w_hg", [1, 1] if tiny else [2, NH, D, 640])
    wout_d = dt("w_out", [1, 1] if tiny else [2, D, D])
    pw1_d = dt("w_pw1", [1, 1] if tiny else [2, D, 2 * D])
    pw2_d = dt("w_pw2", [1, 1] if tiny else [2, D, D])
    w1_d = dt("w1", [1, 1] if tiny else [DEPTH, D, 4 * D])
    w2_d = dt("w2", [1, 1] if tiny else [DEPTH, 4 * D, D])
    ys_d = dt("ys", [TS, D], "ExternalOutput")
    yp_d = dt("yp", [TP, D], "ExternalOutput")
    ns_d = dt("ns", [4, 2, 2, NH, 128, 128], "ExternalOutput")
    cci_d = [[dt("cci%d_%d" % (a, h), [128, 128], "Internal") for h in range(NH)] for a in range(2)]
    cco_d = [[dt("cco%d_%d" % (a, h), [256, 128], "Internal") for h in range(NH)] for a in range(2)]
    hci_d = [nc.dram_tensor("hci%d" % b, [128, 8 * 16], BF16, kind="Internal").ap() for b in range(2)]
    hco_d = [nc.dram_tensor("hco%d" % b, [256, 8 * 16], BF16, kind="Internal").ap() for b in range(2)]

    with ExitStack() as es:
        P = Prog(nc, es)
        sb = lambda name, shape, dtp=F32: es.enter_context(nc.sbuf_tensor(name, list(shape), dtp))
        ccsems = [es.enter_context(nc.semaphore("cc%d" % i)) for i in range(20)]
        cc_next = [0]

        xT = sb("xT", [128, 8, TS])
        hT = sb("hT", [128, 8, TS], BF16)
        ringbuf = sb("ringbuf", [128, 16384], BF16)
        identF = sb("identF", [128, 128])
        identB = sb("identB", [128, 128], BF16)
        onesB = sb("onesB", [128, 128], BF16)
        M1 = sb("M1", [128, CH])
        M2 = sb("M2", [128, CH])
        cmask = sb("cmask", [128, 512])
        ones1 = sb("ones1", [128, 1])
        V = sb("V", [128, 8, 128])
        MOD = sb("MOD", [128, DEPTH, 2, 6, 8])
        WF = sb("WF", [128, DEPTH, 2, 2, 8])
        OML = sb("OML", [128, 2, 2, 8])
        pinfo = sb("pinfo_s", [128, 8])
        scT = sb("scT", [128, 8, 2], BF16)
        rstd = sb("rstd", [128, 512])
        sq = sb("sq", [128, 2, 512], BF16)
        tmpA = sb("tmpA", [128, 512])
        tmpB = sb("tmpB", [128, 512])
        WORK = sb("WORK", [128, 15872])

        ps = [es.enter_context(nc.psum_tensor("ps%d" % i, [128, 512], F32)) for i in range(7)]
        pst = es.enter_context(nc.psum_tensor("pst", [128, 1024], BF16))

        class Carver:
            def __init__(self):
                self.off = 0

            def take(self, shape, dtp=F32):
                n = int(np.prod(shape[1:]))
                words = n if dtp == F32 else (n + 1) // 2
                ap = WORK[:, self.off:self.off + words]
                self.off += words
                assert self.off <= 15872, self.off
                if dtp == BF16:
                    ap = ap.bitcast(BF16)[:, 0:n]
                if len(shape) == 3:
                    ap = ap.rearrange("p (a b) -> p a b", b=shape[2])
                return ap

        ring_i = [0]
        ring_big = [0]

        def stream(parts, big=False):
            if big:
                s = ring_big[0] % 2
                ring_big[0] += 1
                base = s * 8192
                key = 'ringH%d' % s
                wk = [key, 'ring%d' % (2 * s), 'ring%d' % (2 * s + 1)]
            else:
                s = ring_i[0] % 4
                ring_i[0] += 1
                base = s * 4096
                key = 'ring%d' % s
                wk = [key, 'ringH%d' % (s // 2)] + (['cmask16'] if s == 1 else [])
            for (c0, kk, nn, src) in parts:
                dst = ringbuf[:, base + c0:base + c0 + kk * nn].rearrange("p (k n) -> p k n", n=nn)
                P.dma('pool', dst, src, writes=wk)
            return base, key

        def wview(base, c0, kk, nn):
            return ringbuf[:, base + c0:base + c0 + kk * nn].rearrange("p (k n) -> p k n", n=nn)

        def kmajor(w2d, r0, nr, c0, ncol):
            return w2d[r0:r0 + nr, c0:c0 + ncol].rearrange("(k p) n -> p k n", p=128)

        XK = lambda sl: ('xT', sl.start // 512)
        HK = lambda sl: ('hT', sl.start // 512)
        bank_rr = [0]

        def nbank():
            b = bank_rr[0] % 3
            bank_rr[0] += 1
            return b

        def mm_group(items):
            def fn(e):
                last = None
                n = len(items)
                for i, (o, l, r, kw) in enumerate(items):
                    last = e.matmul(o, lhsT=l, rhs=r, start=(i == 0), stop=(i == n - 1), **kw)
                return last
            return fn

        P.op('pool', lambda e: e.memset(identF[:], 0.0), writes=['identF'])
        P.op('pool', lambda e: e.affine_select(out=identF[:], in_=identF[:], pattern=[[-1, 128]],
                                                compare_op=ALU.not_equal, fill=1.0, base=0, channel_multiplier=1),
             reads=['identF'], writes=['identF'])
        P.op('dve', lambda e: e.tensor_copy(out=identB[:], in_=identF[:]), reads=['identF'], writes=['identB'])
        P.op('pool', lambda e: e.memset(onesB[:], 1.0), writes=['onesB'])
        P.op('pool', lambda e: e.memset(ones1[:], 1.0), writes=['ones512'])
        P.op('pool', lambda e: e.memset(cmask[:], 1.0), writes=['cmask'])
        P.op('pool', lambda e: e.memset(cmask[:].rearrange("p (c t) -> p c t", t=CH)[:, :, 0:1], 0.0),
             reads=['cmask'], writes=['cmask'])
        cv = Carver()
        ip_f = cv.take([128, CH])
        it_f = cv.take([128, CH])
        ip_i = cv.take([128, CH]).bitcast(I32)
        it_i = cv.take([128, CH]).bitcast(I32)
        P.op('pool', lambda e: e.iota(ip_i, pattern=[[0, CH]], base=0, channel_multiplier=1), writes=['ip_i'])
        P.op('pool', lambda e: e.iota(it_i, pattern=[[1, CH]], base=0, channel_multiplier=0), writes=['it_i'])
        P.op('dve', lambda e: e.tensor_single_scalar(out=ip_i, in_=ip_i, scalar=CH - 1, op=ALU.bitwise_and),
             reads=['ip_i'], writes=['ip_i'])
        P.op('dve', lambda e: e.tensor_copy(out=ip_f, in_=ip_i), reads=['ip_i'], writes=['ip_f'])
        P.op('dve', lambda e: e.tensor_copy(out=it_f, in_=it_i), reads=['it_i'], writes=['it_f'])
        P.op('dve', lambda e: e.tensor_tensor(out=M1[:], in0=ip_f, in1=it_f, op=ALU.is_le),
             reads=['ip_f', 'it_f'], writes=['M1'])
        P.op('dve', lambda e: e.tensor_tensor(out=M2[:], in0=ip_f, in1=it_f, op=ALU.is_ge),
             reads=['ip_f', 'it_f'], writes=['M2'])

        vstage = cv.take([128, D])
        P.dma('sp', vstage, vecs_d[:, :], writes=['vstage'])
        P.dma('sp', pinfo[:], pinfo_d[:, :], writes=['pinfo'])
        for g in range(2):
            def fn(e, g=g):
                last = None
                for q in range(4):
                    fc = g * 4 + q
                    last = e.transpose(ps[g][:, q * 128:(q + 1) * 128], vstage[:, fc * 128:(fc + 1) * 128], identF[:])
                return last
            P.op('pe', fn, reads=['vstage', 'identF'], writes=['ps%d' % g])
            P.op('dve', lambda e, g=g: e.tensor_copy(out=V[:, g * 4:(g + 1) * 4, :],
                                                     in_=ps[g][:].rearrange("p (a b) -> p a b", b=128)),
                 reads=['ps%d' % g], writes=['V'])
        for d_, R_ in ((0, R_LB1), (1, R_LB2)):
            P.op('pool', lambda e, d_=d_: e.memset(OML[:, 0, d_, :], 1.0), writes=['OML'])
            P.op('dve', lambda e, R_=R_: e.tensor_tensor(out=tmpA[:, 0:8], in0=V[:, :, R_], in1=V[:, :, R_ + 1], op=ALU.subtract),
                 reads=['V'], writes=['tmpA'])
            P.op('act', lambda e, d_=d_: e.activation(out=OML[:, 1, d_, :], in_=tmpA[:, 0:8], func=AF.Sigmoid),
                 reads=['tmpA'], writes=['OML'])
        P.op('act', lambda e: e.activation(out=scT[:], in_=V[:, :, R_CV:R_CV + 2], func=AF.Silu), reads=['V'], writes=['scT'])
        for i in range(n_layers):
            psm = ps[6][:, 0:96]
            mq = {}
            for grp in range(12):
                for g2 in range(grp, min(12, grp + 3)):
                    if g2 not in mq:
                        mq[g2] = stream([(0, 8, 512, kmajor(wmod_d[i], 0, D, g2 * 512, 512))])
                s, key = mq[grp]
                w = wview(s, 0, 8, 512)
                for n in range(4):
                    j = grp * 4 + n
                    items = [(ps[6][:, 2 * j:2 * j + 2], w[:, kc, n * 128:(n + 1) * 128], scT[:, kc, :], {}) for kc in range(8)]
                    P.op('pe', mm_group(items), reads=[key, 'scT'], writes=[('psm', j)])
            for r in range(2):
                pin = psm.rearrange("p (m f r) -> p m f r", m=6, f=8)[:, :, :, r]
                bm = V[:, :, R_BMOD + i * 6:R_BMOD + i * 6 + 6].rearrange("p f m -> p m f")
                P.op('dve', lambda e, i=i, r=r, pin=pin, bm=bm: e.tensor_tensor(out=MOD[:, i, r, :, :], in0=pin, in1=bm, op=ALU.add),
                     reads=[('psm', j) for j in range(48)] + ['V'], writes=['MOD'])
                for sub, (mi, R_) in enumerate(((1, R_NMIX), (4, R_NMLP))):
                    P.op('dve', lambda e, i=i, r=r, sub=sub, mi=mi, R_=R_: e.scalar_tensor_tensor(
                        out=WF[:, i, r, sub, :], in0=MOD[:, i, r, mi, :], scalar=1.0, in1=V[:, :, R_ + i],
                        op0=ALU.add, op1=ALU.mult), reads=['MOD', 'V'], writes=['WF'])

        def stats_rstd(src_fn, T0, n, nfeat_chunks, dim, srckeys):
            ngr = (nfeat_chunks + 1) // 2
            for g in range(ngr):
                cnt = min(2, nfeat_chunks - g * 2)
                for q in range(cnt):
                    fc = g * 2 + q
                    src = src_fn(fc)
                    if q == 0:
                        P.op('act', lambda e, src=src, q=q: e.activation(out=sq[:, q, 0:n], in_=src, func=AF.Square),
                             reads=srckeys, writes=[('sq', q)])
                    else:
                        P.op('dve', lambda e, src=src, q=q: e.tensor_tensor(out=sq[:, q, 0:n], in0=src, in1=src, op=ALU.mult),
                             reads=srckeys, writes=[('sq', q)])

                def fn(e, g=g, cnt=cnt):
                    last = None
                    for q in range(cnt):
                        last = e.matmul(ps[6][:, 0:n], lhsT=onesB[:], rhs=sq[:, q, 0:n],
                                        start=(g == 0 and q == 0), stop=(g == ngr - 1 and q == cnt - 1))
                    return last
                P.op('pe', fn, reads=[('sq', q) for q in range(cnt)] + ['onesB'], writes=['ps6'])
            P.op('act', lambda e: e.activation(out=rstd[:, 0:n], in_=ps[6][:, 0:n], func=AF.Ln, scale=1.0 / dim, bias=EPS),
                 reads=['ps6'], writes=['rstd'])
            P.op('act', lambda e: e.activation(out=rstd[:, 0:n], in_=rstd[:, 0:n], func=AF.Exp, scale=-0.5), reads=['rstd'], writes=['rstd'])

        def norm_mod(i, sub, r, T):
            for tb in range(T // 512):
                sl = slice(tb * 512, (tb + 1) * 512)
                stats_rstd(lambda fc: xT[:, fc, sl], tb * 512, 512, 8, D, [XK(sl)])
                for fc in range(8):
                    tb_, tk_ = (tmpA, 'tmpA') if fc % 2 == 0 else (tmpB, 'tmpB')
                    P.op('dve', lambda e, fc=fc, tb_=tb_: e.scalar_tensor_tensor(
                        out=tb_[:], in0=xT[:, fc, sl], scalar=WF[:, i, r, sub, fc:fc + 1], in1=rstd[:],
                        op0=ALU.mult, op1=ALU.mult), reads=[XK(sl), 'WF', 'rstd'], writes=[tk_])
                    P.op('act', lambda e, fc=fc, tb_=tb_: e.activation(out=hT[:, fc, sl], in_=tb_[:], func=AF.Identity,
                                                                        bias=MOD[:, i, r, 0 if sub == 0 else 3, fc:fc + 1]),
                         reads=[tk_, 'MOD'], writes=[HK(sl)])

        def resid_evac(bank, i, r, gi, m, sl, n=512):
            P.op('dve', lambda e: e.scalar_tensor_tensor(
                out=xT[:, m, sl], in0=ps[bank][:, 0:n], scalar=MOD[:, i, r, gi, m:m + 1], in1=xT[:, m, sl],
                op0=ALU.mult, op1=ALU.add), reads=['ps%d' % bank, 'MOD', XK(sl)], writes=[XK(sl)])

        def mlp(i, r, T):
            wq = {}

            def wload(j):
                wq[j] = (stream([(0, 8, 512, kmajor(w1_d[i], 0, D, j * 512, 512))]),
                         stream([(0, 4, 1024, kmajor(w2_d[i], j * 512, 512, 0, D))]))
            wload(0)
            wload(1)
            norm_mod(i, 1, r, T)
            P.barrier(skip=('pe',))
            cvm = Carver()
            hid = [cvm.take([128, 4, 512], BF16) for _ in range(2)]
            rl = [cvm.take([128, 512], BF16) for _ in range(2)]
            NBk = T // 512
            items_ = [(j, tb) for j in range(8) for tb in range(NBk)]
            rc = [0]

            def S1(k):
                j, tb = items_[k]
                (sA, kA), _ = wq[j]
                wA = wview(sA, 0, 8, 512)
                sl = slice(tb * 512, (tb + 1) * 512)
                hb, hk = hid[k % 2], 'hid%d' % (k % 2)
                for n in range(4):
                    b = nbank()
                    its = [(ps[b][:], wA[:, kc, n * 128:(n + 1) * 128], hT[:, kc, sl], {}) for kc in range(8)]
                    P.op('pe', mm_group(its), reads=[kA, HK(sl)], writes=['ps%d' % b])
                    rb, rk = rl[rc[0] % 2], 'rl%d' % (rc[0] % 2)
                    rc[0] += 1
                    P.op('act', lambda e, b=b, rb=rb: e.activation(out=rb[:], in_=ps[b][:], func=AF.Relu),
                         reads=['ps%d' % b], writes=[rk])
                    P.op('pool', lambda e, hb=hb, n=n, rb=rb: e.tensor_tensor(out=hb[:, n, :], in0=rb[:], in1=rb[:], op=ALU.mult),
                         reads=[rk], writes=[(hk, n)])

            def S2(k):
                j, tb = items_[k]
                _, (sB, kB) = wq[j]
                wB = wview(sB, 0, 4, 1024)
                sl = slice(tb * 512, (tb + 1) * 512)
                hb, hk = hid[k % 2], 'hid%d' % (k % 2)
                for m in range(8):
                    b = nbank()
                    its = [(ps[b][:], wB[:, n, m * 128:(m + 1) * 128], hb[:, n, :], {}) for n in range(4)]
                    P.op('pe', mm_group(its), reads=[kB] + [(hk, n) for n in range(4)], writes=['ps%d' % b])
                    resid_evac(b, i, r, 5, m, sl)
            for k in range(len(items_) + 1):
                if k < len(items_):
                    S1(k)
                if k >= 1:
                    S2(k - 1)
                    jp, tbp = items_[k - 1]
                    if tbp == NBk - 1 and jp + 2 < 8:
                        wload(jp + 2)

        def hgrn(i, r, T, seqs):
            a = i // 2
            hq = {}

            def hload(h):
                hq[h] = stream([(0, 8, 640, kmajor(whg_d[a, h], 0, D, 0, 640)),
                                (5120, 1, 1024, kmajor(wout_d[a], h * 128, 128, 0, D))], big=True)
            hload(0)
            norm_mod(i, 0, r, T)
            P.barrier(skip=('pe',))
            NB = T // 512
            NT = T // 128
            NC = T // CH
            c = Carver()
            qf = c.take([128, T], BF16)
            vT = c.take([128, NT, 128], BF16)
            sg = c.take([128, T], BF16)
            o = c.take([128, T])
            q2g = c.take([128, T], BF16) if r == 1 else None
            qt = c.take([128, T], BF16)
            q16 = c.take([128, T], BF16)
            KA = c.take([128, T], BF16)
            KB = c.take([128, T], BF16)
            khT = c.take([128, NT, 128], BF16)
            vf = c.take([128, 512], BF16)
            khs = [c.take([128, 256], BF16) for _ in range(2)]
            og = vf
            kks = [c.take([128, 256]) for _ in range(2)]
            lgs = [c.take([128, 256]) for _ in range(2)]
            bbs = [c.take([128, 256]) for _ in range(2)]
            b16s = [c.take([128, 256]) for _ in range(2)]
            ebks = [c.take([128, 256]) for _ in range(2)]
            enbs = [c.take([128, 256]) for _ in range(2)]
            eL = c.take([128, NC])
            S32 = [[c.take([128, 128]) for _ in range(2)] for _ in range(len(seqs))]
            Sb = [[c.take([128, 128], BF16) for _ in range(2)] for _ in range(len(seqs))]
            Pm = [c.take([128, CH], BF16) for _ in range(4)]
            Sboth = c.take([128, 2, 128])
            Sin_b = c.take([128, 128], BF16)
            carry = c.take([128, 2])
            cmask16 = ringbuf[:, 6144:8192].bitcast(F32)[:, 0:512]
            P.op('pool', lambda e: e.memset(cmask16, 1.0), writes=['cmask16', 'ring1'])
            P.op('pool', lambda e: e.memset(cmask16.rearrange("p (c t) -> p c t", t=16)[:, :, 0:1], 0.0), reads=['cmask16'], writes=['cmask16', 'ring1'])
            scale = float(128 ** -0.5)
            H = CH // 2
            for h in range(NH):
                if h + 1 < NH:
                    hload(h + 1)
                s, key = hq.pop(h)
                w = wview(s, 0, 8, 640)
                wo = ringbuf[:, s + 5120:s + 6144]
                blocks = list(range(NB))[::-1]

                def proj(part, sl):
                    b = nbank()
                    items = [(ps[b][:], w[:, kc, part * 128:(part + 1) * 128], hT[:, kc, sl], {}) for kc in range(8)]
                    P.op('pe', mm_group(items), reads=[key, HK(sl)], writes=['ps%d' % b])
                    return b
                for tb in blocks:
                    sl = slice(tb * 512, (tb + 1) * 512)
                    b = proj(0, sl)
                    P.op('act', lambda e, b=b: e.activation(out=qf[:, sl], in_=ps[b][:], func=AF.Silu), reads=['ps%d' % b], writes=['qf'])
                    b = proj(1, sl)
                    P.op('act', lambda e, b=b: e.activation(out=vf[:], in_=ps[b][:], func=AF.Copy), reads=['ps%d' % b], writes=['vf'])
                    b = proj(4, sl)
                    P.op('act', lambda e, b=b: e.activation(out=sg[:, sl], in_=ps[b][:], func=AF.Silu), reads=['ps%d' % b], writes=['sg'])

                    def fnv(e):
                        last = None
                        for q in range(4):
                            last = e.transpose(pst[:, q * 128:(q + 1) * 128], vf[:, q * 128:(q + 1) * 128], identB[:])
                        return last
                    P.op('pe', fnv, reads=['vf', 'identB'], writes=['pst'])
                    P.op('act', lambda e, tb=tb: e.activation(out=vT[:, tb * 4:(tb + 1) * 4, :],
                                                               in_=pst[:, 0:512].rearrange("p (a b) -> p a b", b=128), func=AF.Copy),
                         reads=['pst'], writes=['vT'])
                for d in range(2):
                    rv = (lambda ap: ap) if d == 0 else (lambda ap: ap[:, ::-1])
                    li = CH - 1 if d == 0 else 0
                    ge, gl = (0, 1) if d == 0 else (1, 0)
                    l16 = H - 1 if d == 0 else 0
                    SBK = 256
                    NCS = SBK // CH
                    subs = [(tb, hb) for tb in blocks for hb in (1, 0)]
                    st_ = {}

                    def stage(k, u, s_):
                        tb, hb = u
                        sl = slice(tb * 512 + hb * SBK, tb * 512 + (hb + 1) * SBK)
                        tA = (tmpA, tmpB)[s_][:, 0:SBK]
                        tk = ('tmpA', 'tmpB')[s_]
                        kk_, lg_, bb_, ebk_, enb_, b16_, kh_ = kks[s_], lgs[s_], bbs[s_], ebks[s_], enbs[s_], b16s[s_], khs[s_]
                        K = lambda n: (n, s_)
                        if k == 0:
                            b = nbank()
                            st_[u] = b
                            items = [(ps[b][:, 0:SBK], w[:, kc, (2 + d) * 128:(3 + d) * 128], hT[:, kc, sl], {}) for kc in range(8)]
                            P.op('pe', mm_group(items), reads=[key, HK(sl)], writes=['ps%d' % b])
                        elif k == 1:
                            b = st_[u]
                            P.op('act', lambda e: e.activation(out=tA, in_=ps[b][:, 0:SBK], func=AF.Exp), reads=['ps%d' % b], writes=[tk])
                        elif k == 2:
                            P.op('act', lambda e: e.activation(out=tA, in_=tA, func=AF.Ln, bias=1.0), reads=[tk], writes=[tk])
                        elif k == 3:
                            P.op('act', lambda e: e.activation(out=tA, in_=tA, func=AF.Exp, scale=-1.0), reads=[tk], writes=[tk])
                        elif k == 4:
                            P.op('dve', lambda e: e.tensor_scalar(out=kk_, in0=tA, scalar1=OML[:, a, d, h:h + 1], scalar2=KMAX,
                                                                  op0=ALU.mult, op1=ALU.min), reads=[tk, 'OML'], writes=[K('kk')])
                        elif k == 5:
                            P.op('act', lambda e: e.activation(out=lg_, in_=kk_, func=AF.Ln, scale=-1.0, bias=1.0), reads=[K('kk')], writes=[K('lg')])
                        elif k == 6:
                            P.op('dve', lambda e: e.tensor_tensor_scan(out=rv(bb_), data0=cmask[:, 0:SBK], data1=rv(lg_), initial=0.0,
                                                                       op0=ALU.mult, op1=ALU.add), reads=['cmask', K('lg')], writes=[K('bb')])
                        elif k == 7:
                            P.op('act', lambda e: e.activation(out=ebk_, in_=bb_, func=AF.Exp), reads=[K('bb')], writes=[K('ebk')])
                        elif k == 8:
                            ebv = ebk_.rearrange("p (c t) -> p c t", t=CH)
                            c0 = (tb * 512 + hb * SBK) // CH
                            P.op('dve', lambda e: e.tensor_copy(out=eL[:, c0:c0 + NCS].rearrange("p (c o) -> p c o", o=1), in_=ebv[:, :, li:li + 1]),
                                 reads=[K('ebk')], writes=['eL'])
                            P.op('dve', lambda e: e.scalar_tensor_tensor(out=qt[:, sl], in0=qf[:, sl], scalar=scale, in1=ebk_,
                                                                         op0=ALU.mult, op1=ALU.mult), reads=['qf', K('ebk')], writes=['qt'])
                        elif k == 9:
                            bbv = bb_.rearrange("p (c t) -> p c t", t=CH)
                            P.op('dve', lambda e: e.tensor_tensor(out=tA.rearrange("p (c t) -> p c t", t=CH), in0=bbv,
                                                                  in1=bbv[:, :, li:li + 1].to_broadcast([128, NCS, CH]), op=ALU.subtract),
                                 reads=[K('bb')], writes=[tk])
                        elif k == 10:
                            P.op('act', lambda e: e.activation(out=enb_, in_=tA, func=AF.Exp, scale=-1.0), reads=[tk], writes=[K('enb')])
                        elif k == 11:
                            P.op('dve', lambda e: e.tensor_tensor(out=kh_, in0=kk_, in1=enb_, op=ALU.mult), reads=[K('kk'), K('enb')], writes=[K('kh')])
                        elif k == 12:
                            pc = 512 + s_ * SBK

                            def fnk(e):
                                last = None
                                for q in range(2):
                                    last = e.transpose(pst[:, pc + q * 128:pc + (q + 1) * 128], kh_[:, q * 128:(q + 1) * 128], identB[:])
                                return last
                            P.op('pe', fnk, reads=[K('kh'), 'identB'], writes=['pst'])
                            t2 = tb * 4 + hb * 2
                            P.op('act', lambda e: e.activation(out=khT[:, t2:t2 + 2, :],
                                                               in_=pst[:, pc:pc + SBK].rearrange("p (a b) -> p a b", b=128), func=AF.Copy),
                                 reads=['pst'], writes=['khT'])
                        elif k == 13:
                            P.op('dve', lambda e: e.tensor_tensor_scan(out=rv(b16_), data0=cmask16[:, 0:SBK], data1=rv(lg_), initial=0.0,
                                                                       op0=ALU.mult, op1=ALU.add), reads=['cmask16', K('lg')], writes=[K('b16')])
                        elif k == 14:
                            P.op('act', lambda e: e.activation(out=ebk_, in_=b16_, func=AF.Exp), reads=[K('b16')], writes=[K('ebk')])
                        elif k == 15:
                            P.op('dve', lambda e: e.scalar_tensor_tensor(out=q16[:, sl], in0=qf[:, sl], scalar=scale, in1=ebk_,
                                                                         op0=ALU.mult, op1=ALU.mult), reads=['qf', K('ebk')], writes=['q16'])
                        elif k == 16:
                            P.op('act', lambda e: e.activation(out=enb_, in_=b16_, func=AF.Exp, scale=-1.0), reads=[K('b16')], writes=[K('enb')])
                        elif k == 17:
                            P.op('dve', lambda e: e.tensor_tensor(out=KA[:, sl], in0=kk_, in1=enb_, op=ALU.mult), reads=[K('kk'), K('enb')], writes=['KA'])
                        elif k == 18:
                            KAv = KA[:, sl].rearrange("p (c g t) -> p c g t", g=2, t=H)
                            KBv = KB[:, sl].rearrange("p (c g t) -> p c g t", g=2, t=H)
                            e16 = ebk_.rearrange("p (c g t) -> p c g t", g=2, t=H)
                            P.op('dve', lambda e: e.tensor_tensor(out=KBv[:, :, ge, :], in0=KAv[:, :, ge, :],
                                                                  in1=e16[:, :, ge, l16:l16 + 1].to_broadcast([128, NCS, H]), op=ALU.mult),
                                 reads=['KA', K('ebk')], writes=['KB'])
                            P.op('pool', lambda e: e.tensor_copy(out=KBv[:, :, gl, :], in_=KAv[:, :, gl, :]), reads=['KA'], writes=['KB'])
                        elif k == 19 and d == 1 and r == 1:
                            first = (tb == NB - 1 and hb == 1)
                            init = 0.0 if first else carry[:, 0:1]
                            P.op('dve', lambda e: e.tensor_tensor_scan(out=bb_[:, ::-1], data0=ones1[:, 0:1].to_broadcast([128, SBK]), data1=lg_[:, ::-1],
                                                                       initial=init, op0=ALU.mult, op1=ALU.add),
                                 reads=['ones512', K('lg'), 'carry'], writes=[K('bb')])
                            P.op('dve', lambda e: e.tensor_copy(out=carry[:, 0:1], in_=bb_[:, 0:1]), reads=[K('bb')], writes=['carry'])
                        elif k == 20 and d == 1 and r == 1:
                            P.op('act', lambda e: e.activation(out=enb_, in_=bb_, func=AF.Exp), reads=[K('bb')], writes=[K('enb')])
                        elif k == 21 and d == 1 and r == 1:
                            P.op('dve', lambda e: e.scalar_tensor_tensor(out=q2g[:, sl], in0=qf[:, sl], scalar=scale, in1=enb_,
                                                                         op0=ALU.mult, op1=ALU.mult), reads=['qf', K('enb')], writes=['q2g'])
                    NSTG = 22
                    SP_ = NSTG // 2
                    EARLY = 6
                    nsub = len(subs)
                    for tick in range(-EARLY, SP_ * (nsub - 1) + NSTG):
                        for j_ in range(nsub):
                            k = tick - SP_ * j_
                            if k == -EARLY:
                                stage(0, subs[j_], j_ % 2)
                            elif 1 <= k < NSTG:
                                stage(k, subs[j_], j_ % 2)
                    Mk = M1 if d == 0 else M2
                    nl_ = len(seqs)
                    spl = 4 // nl_
                    steps = []
                    for (t0, L) in seqs:
                        tl = list(range(t0 // 128, (t0 + L) // 128))
                        od = [0, 1, 2, 3]
                        if d == 1:
                            tl, od = tl[::-1], od[::-1]
                        steps.append([(tt, j, k == 0, k == 3) for tt in tl for k, j in enumerate(od)])
                    NS = len(steps[0])
                    for l in range(nl_):
                        if r == 1 and d == 0:
                            P.dma('sp', S32[l][0], sinit_d[a, h], writes=[('S32', l, 0)])
                            P.op('act', lambda e, l=l: e.activation(out=Sb[l][0][:], in_=S32[l][0][:], func=AF.Copy),
                                 reads=[('S32', l, 0)], writes=[('Sb', l, 0)])
                        else:
                            P.op('pool', lambda e, l=l: e.memset(S32[l][0][:], 0.0), writes=[('S32', l, 0)])
                            P.op('pool', lambda e, l=l: e.memset(Sb[l][0][:], 0.0), writes=[('Sb', l, 0)])

                    def emitD(l, n):
                        tt, j, _, _ = steps[l][n]
                        slot = l * spl + n % spl
                        pr = slice(j * CH, (j + 1) * CH)
                        P.op('pe', lambda e, slot=slot, pr=pr, tt=tt, j=j: e.matmul(
                            ps[slot][:, 0:128], lhsT=khT[pr, tt, :], rhs=vT[pr, tt, :], start=True, stop=True,
                            tile_position=(j * CH, 0)), reads=['khT', 'vT'], writes=['ps%d' % slot])
                    look = spl - 1
                    for l in range(nl_):
                        for n in range(min(look, NS)):
                            emitD(l, n)
                    for n in range(NS):
                        for l in range(nl_):
                            tt, j, first, last = steps[l][n]
                            tslot = l * spl + (n // 4) % spl
                            if first:
                                def fnA(e, tt=tt, tslot=tslot):
                                    last_ = None
                                    for jj in range(4):
                                        c0 = tt * 128 + jj * CH
                                        for g in range(2):
                                            lhs = KA if g == ge else KB
                                            last_ = e.matmul(ps[5][jj * CH:(jj + 1) * CH, tslot * CH + g * H:tslot * CH + (g + 1) * H],
                                                             lhsT=lhs[:, c0:c0 + CH], rhs=q16[:, c0 + g * H:c0 + (g + 1) * H],
                                                             start=True, stop=True, tile_position=(0, jj * CH))
                                    return last_
                                P.op('pe', fnA, reads=['KA', 'KB', 'q16'], writes=['ps5'])
                                P.op('dve', lambda e, Mk=Mk, tslot=tslot: e.tensor_tensor(out=Pm[tslot][:], in0=ps[5][:, tslot * CH:(tslot + 1) * CH],
                                                                                          in1=Mk[:], op=ALU.mult),
                                     reads=['ps5', 'M1', 'M2'], writes=[('Pm', tslot)])
                            if n + look < NS:
                                emitD(l, n + look)
                            slot = l * spl + n % spl
                            ts_ = slice(tt * 128 + j * CH, tt * 128 + (j + 1) * CH)
                            pr = slice(j * CH, (j + 1) * CH)
                            ci = (tt * 128 + j * CH) // CH
                            oc = ps[4][:, tslot * 128 + j * CH:tslot * 128 + (j + 1) * CH]
                            cur, nxt = n % 2, (n + 1) % 2

                            def fnBC(e, pr=pr, ts_=ts_, oc=oc, tt=tt, j=j, tslot=tslot, l=l, cur=cur):
                                e.matmul(oc, lhsT=vT[pr, tt, :], rhs=Pm[tslot][pr, :], start=True, stop=False, tile_position=(j * CH, 0))
                                return e.matmul(oc, lhsT=Sb[l][cur][:], rhs=qt[:, ts_], start=False, stop=True)
                            P.op('pe', fnBC, reads=['vT', ('Pm', tslot), ('Sb', l, cur), 'qt'], writes=['ps4'])
                            ecol = eL[:, ci:ci + 1]
                            P.op('dve', lambda e, ecol=ecol, l=l, cur=cur, nxt=nxt, slot=slot: e.scalar_tensor_tensor(
                                out=S32[l][nxt][:], in0=S32[l][cur][:], scalar=ecol, in1=ps[slot][:, 0:128],
                                op0=ALU.mult, op1=ALU.add), reads=[('S32', l, cur), 'eL', 'ps%d' % slot], writes=[('S32', l, nxt)])
                            P.op('act', lambda e, l=l, nxt=nxt: e.activation(out=Sb[l][nxt][:], in_=S32[l][nxt][:], func=AF.Copy),
                                 reads=[('S32', l, nxt)], writes=[('Sb', l, nxt)])
                            if last:
                                osl = slice(tt * 128, (tt + 1) * 128)
                                pso = ps[4][:, tslot * 128:(tslot + 1) * 128]
                                if d == 0:
                                    P.op('act', lambda e, osl=osl, pso=pso: e.activation(out=o[:, osl], in_=pso, func=AF.Copy),
                                         reads=['ps4'], writes=[('o', tt)])
                                else:
                                    P.op('dve', lambda e, osl=osl, pso=pso: e.tensor_tensor(out=o[:, osl], in0=o[:, osl], in1=pso, op=ALU.add),
                                         reads=['ps4', ('o', tt)], writes=[('o', tt)])
                    for l in range(nl_):
                        fin = S32[l][NS % 2]
                        fk = ('S32', l, NS % 2)
                        if r == 0:
                            P.dma('sp', ns_d[l, a, d, h], fin, reads=[fk], is_out=True)
                        elif d == 0:
                            P.dma('sp', cci_d[a][h], fin, reads=[fk], writes=[('cci', a, h)])
                            k_ = cc_next[0]
                            cc_next[0] += 1
                            P.special('pool', lambda e, a=a, h=h: e.collective_compute(
                                "AllGather", ALU.bypass, replica_groups=[[0, 1], [2, 3], [4, 5], [6, 7]],
                                ins=[cci_d[a][h]], outs=[cco_d[a][h]]), ccsems[k_], cc_inc, ('cc', k_),
                                reads=[('cci', a, h)], writes=[('cco', a, h)])
                if r == 1:
                    P.dma('sp', Sboth, cco_d[a][h].rearrange("(r p) v -> p r v", p=128), reads=[('cco', a, h)], writes=['Sboth'])
                    P.op('dve', lambda e: e.tensor_scalar(out=Sboth[:, 0, :], in0=Sboth[:, 0, :], scalar1=pinfo[:, 6:7], scalar2=None, op0=ALU.mult),
                         reads=['Sboth', 'pinfo'], writes=['Sboth'])
                    P.op('dve', lambda e: e.scalar_tensor_tensor(out=Sin_b[:], in0=Sboth[:, 1, :], scalar=pinfo[:, 7:8], in1=Sboth[:, 0, :],
                                                                 op0=ALU.mult, op1=ALU.add), reads=['Sboth', 'pinfo'], writes=['Sin_b'])
                    for tb in range(NB):
                        sl = slice(tb * 512, (tb + 1) * 512)
                        b = nbank()
                        P.op('pe', lambda e, b=b, sl=sl: e.matmul(ps[b][:], lhsT=Sin_b[:], rhs=q2g[:, sl], start=True, stop=True),
                             reads=['Sin_b', 'q2g'], writes=['ps%d' % b])
                        ok_ = [('o', tb * 4 + q) for q in range(4)]
                        P.op('dve', lambda e, b=b, sl=sl: e.tensor_tensor(out=o[:, sl], in0=o[:, sl], in1=ps[b][:], op=ALU.add),
                             reads=['ps%d' % b] + ok_, writes=ok_)
                for tb in range(NB):
                    sl = slice(tb * 512, (tb + 1) * 512)
                    ok_ = [('o', tb * 4 + q) for q in range(4)]
                    stats_rstd(lambda fc: o[:, sl], 0, 512, 1, 128, ok_)
                    P.op('dve', lambda e, sl=sl: e.scalar_tensor_tensor(out=tmpA[:], in0=o[:, sl], scalar=V[:, h, R_GN + a:R_GN + a + 1], in1=rstd[:],
                                                                        op0=ALU.mult, op1=ALU.mult), reads=ok_ + ['V', 'rstd'], writes=['tmpA'])
                    P.op('dve', lambda e, sl=sl: e.tensor_tensor(out=og[:], in0=tmpA[:], in1=sg[:, sl], op=ALU.mult),
                         reads=['tmpA', 'sg'], writes=['vf'])
                    for m in range(8):
                        b = nbank()
                        P.op('pe', lambda e, b=b, m=m: e.matmul(ps[b][:], lhsT=wo[:, m * 128:(m + 1) * 128], rhs=og[:], start=True, stop=True),
                             reads=[key, 'vf'], writes=['ps%d' % b])
                        resid_evac(b, i, r, 2, m, sl)

        def conv(i, r, T, seqs):
            bi = i // 2
            cq = {}
            for g in range(2):
                cq[g] = (stream([(0, 8, 512, kmajor(pw1_d[bi], 0, D, g * 512, 512))]),
                         stream([(0, 8, 512, kmajor(pw1_d[bi], 0, D, D + g * 512, 512))]))
            norm_mod(i, 0, r, T)
            P.barrier(skip=('pe',))
            c = Carver()
            PADW = T + 30 * len(seqs)
            upad = c.take([128, 8, PADW], BF16)
            dgs = [c.take([128, KTAP, 128], BF16) for _ in range(2)]
            tS = c.take([128, 512])
            mu = c.take([128, 512])
            hb_st = c.take([128, 8 * 16], BF16)
            hb_in = c.take([128, 2, 8 * 16], BF16)
            P.op('pool', lambda e: e.memset(upad[:], 0.0), writes=['upad'])
            offs = [t0 + 30 * si + 15 for si, (t0, L) in enumerate(seqs)]
            for g in range(2):
                (sA, kA), (sG, kG) = cq[g]
                wA = wview(sA, 0, 8, 512)
                wG = wview(sG, 0, 8, 512)
                for n in range(4):
                    fc = g * 4 + n
                    for si, (t0, L) in enumerate(seqs):
                        for tb in range(max(1, L // 512)):
                            n_ = min(512, L)
                            sl = slice(t0 + tb * 512, t0 + tb * 512 + n_)
                            b1 = nbank()
                            P.op('pe', mm_group([(ps[b1][:, 0:n_], wG[:, kc, n * 128:(n + 1) * 128], hT[:, kc, sl], {}) for kc in range(8)]),
                                 reads=[kG, HK(sl)], writes=['ps%d' % b1])
                            P.op('act', lambda e, b1=b1, n_=n_: e.activation(out=tS[:, 0:n_], in_=ps[b1][:, 0:n_], func=AF.Sigmoid),
                                 reads=['ps%d' % b1], writes=['tS'])
                            b2 = nbank()
                            P.op('pe', mm_group([(ps[b2][:, 0:n_], wA[:, kc, n * 128:(n + 1) * 128], hT[:, kc, sl], {}) for kc in range(8)]),
                                 reads=[kA, HK(sl)], writes=['ps%d' % b2])
                            po = offs[si] + tb * 512
                            P.op('dve', lambda e, b2=b2, n_=n_, fc=fc, po=po: e.tensor_tensor(
                                out=upad[:, fc, po:po + n_], in0=ps[b2][:, 0:n_], in1=tS[:, 0:n_], op=ALU.mult),
                                reads=['ps%d' % b2, 'tS'], writes=['upad'])
            if r == 1:
                po = offs[0] + T
                P.op('dve', lambda e: e.tensor_copy(out=hb_st[:].rearrange("p (f t) -> p f t", t=16)[:, :, 0:15], in_=upad[:, :, po - 15:po][:, :, ::-1]),
                     reads=['upad'], writes=['hb_st'])
                P.dma('sp', hci_d[bi], hb_st, reads=['hb_st'], writes=[('hci', bi)])
                k_ = cc_next[0]
                cc_next[0] += 1
                P.special('pool', lambda e: e.collective_compute("AllGather", ALU.bypass, replica_groups=[[0, 1], [2, 3], [4, 5], [6, 7]],
                                                                 ins=[hci_d[bi]], outs=[hco_d[bi]]), ccsems[k_], cc_inc, ('cc', k_),
                          reads=[('hci', bi)], writes=[('hco', bi)])
                P.dma('sp', hb_in, hco_d[bi].rearrange("(r p) v -> p r v", p=128), reads=[('hco', bi)], writes=['hb_in'])
                P.op('dve', lambda e: e.tensor_scalar(out=hb_in[:, 0, :], in0=hb_in[:, 0, :], scalar1=pinfo[:, 6:7], scalar2=None, op0=ALU.mult),
                     reads=['hb_in', 'pinfo'], writes=['hb_in'])
                P.op('dve', lambda e: e.scalar_tensor_tensor(out=upad[:, :, po:po + 15], in0=hb_in[:, 1, :].rearrange("p (f t) -> p f t", t=16)[:, :, 0:15],
                                                             scalar=pinfo[:, 7:8], in1=hb_in[:, 0, :].rearrange("p (f t) -> p f t", t=16)[:, :, 0:15],
                                                             op0=ALU.mult, op1=ALU.add), reads=['hb_in', 'pinfo', 'upad'], writes=['upad'])
            def build_dg(fc):
                dg = dgs[fc % 2]
                for j in range(KTAP):
                    en = ('dve', 'pool', 'act')[j % 3]
                    sc_ = V[:, fc, R_WDW + bi * KTAP + j:R_WDW + bi * KTAP + j + 1]
                    if en == 'act':
                        P.op('act', lambda e, j=j, sc_=sc_, dg=dg: e.activation(out=dg[:, j, :], in_=identB[:], func=AF.Copy, scale=sc_),
                             reads=['identB', 'V'], writes=[('dg', fc % 2, j)])
                    else:
                        P.op(en, lambda e, j=j, sc_=sc_, dg=dg: e.tensor_scalar(out=dg[:, j, :], in0=identB[:], scalar1=sc_, scalar2=None, op0=ALU.mult),
                             reads=['identB', 'V'], writes=[('dg', fc % 2, j)])
            build_dg(0)
            for fc in range(8):
                if fc + 1 < 8:
                    build_dg(fc + 1)
                dg = dgs[fc % 2]
                for si, (t0, L) in enumerate(seqs):
                    for tb in range(max(1, L // 512)):
                        n_ = min(512, L)
                        po = offs[si] + tb * 512 - 15
                        b = nbank()
                        P.op('pe', mm_group([(ps[b][:, 0:n_], dg[:, j, :], upad[:, fc, po + j:po + j + n_], {}) for j in range(KTAP)]),
                             reads=[('dg', fc % 2, j) for j in range(KTAP)] + ['upad'], writes=['ps%d' % b])
                        sl = slice(t0 + tb * 512, t0 + tb * 512 + n_)
                        P.op('act', lambda e, b=b, n_=n_, sl=sl, fc=fc: e.activation(out=hT[:, fc, sl], in_=ps[b][:, 0:n_], func=AF.Identity,
                                                                                     bias=V[:, fc, R_BDW + bi:R_BDW + bi + 1]),
                             reads=['ps%d' % b, 'V'], writes=[HK(sl)])
            pq = [stream([(0, 8, 512, kmajor(pw2_d[bi], 0, D, g * 512, 512))]) for g in range(2)]
            def LN(tb):
                sl = slice(tb * 512, (tb + 1) * 512)
                P.op('pe', mm_group([(ps[5][:], onesB[:], hT[:, fc, sl], {}) for fc in range(8)]), reads=['onesB', HK(sl)], writes=['ps5'])
                P.op('dve', lambda e: e.tensor_scalar(out=mu[:], in0=ps[5][:], scalar1=1.0 / D, scalar2=None, op0=ALU.mult), reads=['ps5'], writes=['mu'])
                for g in range(4):
                    for q in range(2):
                        fc = g * 2 + q
                        P.op('dve', lambda e, fc=fc: e.tensor_tensor(out=tmpA[:], in0=hT[:, fc, sl], in1=mu[:], op=ALU.subtract),
                             reads=[HK(sl), 'mu'], writes=['tmpA'])
                        P.op('act', lambda e, q=q: e.activation(out=sq[:, q, :], in_=tmpA[:], func=AF.Square), reads=['tmpA'], writes=[('sq', q)])

                    def fn(e, g=g):
                        last = None
                        for q in range(2):
                            last = e.matmul(ps[6][:], lhsT=onesB[:], rhs=sq[:, q, :], start=(g == 0 and q == 0), stop=(g == 3 and q == 1))
                        return last
                    P.op('pe', fn, reads=[('sq', q) for q in range(2)] + ['onesB'], writes=['ps6'])
                P.op('act', lambda e: e.activation(out=rstd[:], in_=ps[6][:], func=AF.Ln, scale=1.0 / D, bias=EPS), reads=['ps6'], writes=['rstd'])
                P.op('act', lambda e: e.activation(out=rstd[:], in_=rstd[:], func=AF.Exp, scale=-0.5), reads=['rstd'], writes=['rstd'])
                for fc in range(8):
                    P.op('dve', lambda e, fc=fc: e.tensor_tensor(out=tmpA[:], in0=hT[:, fc, sl], in1=mu[:], op=ALU.subtract),
                         reads=[HK(sl), 'mu'], writes=['tmpA'])
                    P.op('dve', lambda e, fc=fc: e.scalar_tensor_tensor(out=tmpB[:], in0=tmpA[:], scalar=V[:, fc, R_LNG + bi:R_LNG + bi + 1], in1=rstd[:],
                                                                        op0=ALU.mult, op1=ALU.mult), reads=['tmpA', 'V', 'rstd'], writes=['tmpB'])
                    P.op('act', lambda e, fc=fc: e.activation(out=hT[:, fc, sl], in_=tmpB[:], func=AF.Silu, bias=V[:, fc, R_LNB + bi:R_LNB + bi + 1]),
                         reads=['tmpB', 'V'], writes=[HK(sl)])
            def PW2(tb):
                sl = slice(tb * 512, (tb + 1) * 512)
                for g in range(2):
                    s, key = pq[g]
                    w = wview(s, 0, 8, 512)
                    for n in range(4):
                        m = g * 4 + n
                        b = nbank()
                        P.op('pe', mm_group([(ps[b][:], w[:, kc, n * 128:(n + 1) * 128], hT[:, kc, sl], {}) for kc in range(8)]),
                             reads=[key, HK(sl)], writes=['ps%d' % b])
                        resid_evac(b, i, r, 2, m, sl)
            LN(0)
            for tb in range(T // 512):
                if tb + 1 < T // 512:
                    LN(tb + 1)
                PW2(tb)

        def run_phase(r):
            T = TP if r == 0 else TS
            x_d = xp_d if r == 0 else xs_d
            y_d = yp_d if r == 0 else ys_d
            seqs = [(k * 256, 256) for k in range(4)] if r == 0 else [(0, TS)]
            P.barrier()
            c = Carver()
            xin = [c.take([128, D]) for _ in range(2)]
            for tt in range(T // 128):
                xi = xin[tt % 2]
                xk = 'xin%d' % (tt % 2)
                P.dma('sp', xi, x_d[tt * 128:(tt + 1) * 128, :], writes=[xk])
                for g in range(2):
                    b = nbank()

                    def fn(e, g=g, b=b, xi=xi):
                        last = None
                        for q in range(4):
                            fc = g * 4 + q
                            last = e.transpose(ps[b][:, q * 128:(q + 1) * 128], xi[:, fc * 128:(fc + 1) * 128], identF[:])
                        return last
                    P.op('pe', fn, reads=[xk, 'identF'], writes=['ps%d' % b])
                    en = 'act' if g == 0 else 'dve'
                    src = ps[b][:].rearrange("p (a b) -> p a b", b=128)
                    dst = xT[:, g * 4:(g + 1) * 4, tt * 128:(tt + 1) * 128]
                    sl = slice(tt * 128, (tt + 1) * 128)
                    if en == 'act':
                        P.op('act', lambda e, src=src, dst=dst: e.activation(out=dst, in_=src, func=AF.Copy), reads=['ps%d' % b], writes=[XK(sl)])
                    else:
                        P.op('dve', lambda e, src=src, dst=dst: e.tensor_copy(out=dst, in_=src), reads=['ps%d' % b], writes=[XK(sl)])
            if r == 1:
                idx = c.take([128, 64])
                idx_i = idx.bitcast(I32)
                rowi = c.take([128, 32])
                coli = c.take([128, 64])
                ya = c.take([128, 64])
                yb = c.take([128, 64])
                yc = c.take([128, 64])
                Rtab = c.take([128, 4, 32])
                Ctab = c.take([128, 4, 64])
                om = c.take([128, 2])
                om2 = c.take([128, 2])
                pidx = c.take([128, 1])
                P.op('pool', lambda e: e.iota(idx_i, pattern=[[1, 64]], base=0, channel_multiplier=0), writes=['idx'])
                P.op('dve', lambda e: e.tensor_copy(out=coli, in_=idx_i), reads=['idx'], writes=['coli'])
                P.op('dve', lambda e: e.tensor_scalar(out=rowi, in0=coli[:, 0:32], scalar1=pinfo[:, 0:1], scalar2=pinfo[:, 2:3], op0=ALU.mult, op1=ALU.add),
                     reads=['coli', 'pinfo'], writes=['rowi'])
                P.op('dve', lambda e: e.tensor_scalar(out=coli, in0=coli, scalar1=pinfo[:, 0:1], scalar2=pinfo[:, 1:2], op0=ALU.mult, op1=ALU.add),
                     reads=['coli', 'pinfo'], writes=['coli'])
                P.op('pool', lambda e: e.iota(idx_i[:, 0:1], pattern=[[0, 1]], base=0, channel_multiplier=1), reads=['coli'], writes=['idx'])
                P.op('dve', lambda e: e.tensor_copy(out=pidx, in_=idx_i[:, 0:1]), reads=['idx'], writes=['pidx'])
                cst = math.log(10000.0) / 256.0
                for e2 in range(2):
                    P.op('act', lambda e, e2=e2: e.activation(out=om[:, e2:e2 + 1], in_=pidx, func=AF.Exp, scale=-cst, bias=-cst * 128 * e2),
                         reads=['pidx'], writes=['om'])
                P.op('dve', lambda e: e.tensor_scalar(out=om2, in0=om, scalar1=1.0 / (2 * math.pi), scalar2=None, op0=ALU.mult), reads=['om'], writes=['om2'])
                for fc in range(8):
                    src, n_, dst = (rowi, 32, Rtab[:, fc, :]) if fc < 4 else (coli, 64, Ctab[:, fc - 4, :])
                    ph = (0.0 if (fc % 4) < 2 else math.pi / 2) / (2 * math.pi)
                    a_, b_, c_ = ya[:, 0:n_], yb[:, 0:n_], yc[:, 0:n_]
                    P.op('dve', lambda e: e.tensor_scalar(out=a_, in0=src, scalar1=om2[:, fc % 2:fc % 2 + 1], scalar2=ph, op0=ALU.mult, op1=ALU.add),
                         reads=['rowi', 'coli', 'om2'], writes=['ya'])
                    P.op('dve', lambda e: e.tensor_copy(out=b_.bitcast(I32), in_=a_), reads=['ya'], writes=['yb'])
                    P.op('dve', lambda e: e.tensor_copy(out=c_, in_=b_.bitcast(I32)), reads=['yb'], writes=['yc'])
                    P.op('dve', lambda e: e.tensor_tensor(out=a_, in0=a_, in1=c_, op=ALU.subtract), reads=['ya', 'yc'], writes=['ya'])
                    P.op('act', lambda e: e.activation(out=b_, in_=a_, func=AF.Sin, scale=math.pi), reads=['ya'], writes=['yb'])
                    P.op('act', lambda e: e.activation(out=c_, in_=a_, func=AF.Sin, scale=math.pi / 2), reads=['ya'], writes=['yc'])
                    P.op('dve', lambda e: e.tensor_tensor(out=c_, in0=c_, in1=c_, op=ALU.mult), reads=['yc'], writes=['yc'])
                    P.op('dve', lambda e: e.tensor_scalar(out=c_, in0=c_, scalar1=-4.0, scalar2=2.0, op0=ALU.mult, op1=ALU.add), reads=['yc'], writes=['yc'])
                    P.op('dve', lambda e: e.tensor_tensor(out=dst, in0=b_, in1=c_, op=ALU.mult), reads=['yb', 'yc'], writes=[('ptab', fc)])
                for tb in range(TS // 512):
                    sl = slice(tb * 512, (tb + 1) * 512)
                    for fc in range(8):
                        xv = xT[:, fc, sl].rearrange("p (a b) -> p a b", b=64)
                        if fc < 4:
                            tv = Rtab[:, fc, tb * 8:(tb + 1) * 8].rearrange("p (a o) -> p a o", o=1).to_broadcast([128, 8, 64])
                        else:
                            tv = Ctab[:, fc - 4, :].rearrange("p (o b) -> p o b", o=1).to_broadcast([128, 8, 64])
                        P.op('dve', lambda e: e.tensor_tensor(out=xv, in0=xv, in1=tv, op=ALU.add), reads=[XK(sl), ('ptab', fc)], writes=[XK(sl)])
            for i in range(n_layers):
                if i % 2 == 0:
                    hgrn(i, r, T, seqs)
                else:
                    conv(i, r, T, seqs)
                mlp(i, r, T)
            P.barrier()
            c = Carver()
            yT = c.take([128, 8, 512])
            yo = [c.take([128, D]) for _ in range(2)]
            for tb in range(T // 512):
                sl = slice(tb * 512, (tb + 1) * 512)
                stats_rstd(lambda fc: xT[:, fc, sl], 0, 512, 8, D, [XK(sl)])
                for fc in range(8):
                    P.op('dve', lambda e, fc=fc: e.scalar_tensor_tensor(out=yT[:, fc, :], in0=xT[:, fc, sl], scalar=V[:, fc, R_FN:R_FN + 1], in1=rstd[:],
                                                                        op0=ALU.mult, op1=ALU.mult), reads=[XK(sl), 'V', 'rstd'], writes=[('yT', fc)])
                for q in range(4):
                    tt = tb * 4 + q
                    yb = yo[tt % 2]
                    yk = 'yo%d' % (tt % 2)
                    for g in range(2):
                        b = nbank()

                        def fn(e, g=g, b=b, q=q):
                            last = None
                            for u in range(4):
                                fc = g * 4 + u
                                last = e.transpose(ps[b][:, u * 128:(u + 1) * 128], yT[:, fc, q * 128:(q + 1) * 128], identF[:])
                            return last
                        P.op('pe', fn, reads=[('yT', fc) for fc in range(8)] + ['identF'], writes=['ps%d' % b])
                        if g == 0:
                            P.op('act', lambda e, b=b, yb=yb: e.activation(out=yb[:, 0:512], in_=ps[b][:], func=AF.Copy), reads=['ps%d' % b], writes=[(yk, 0)])
                        else:
                            P.op('dve', lambda e, b=b, yb=yb: e.tensor_copy(out=yb[:, 512:1024], in_=ps[b][:]), reads=['ps%d' % b], writes=[(yk, 1)])
                    P.dma('sp', y_d[tt * 128:(tt + 1) * 128, :], yb, reads=[(yk, 0), (yk, 1)], is_out=True)

        for r in phases:
            run_phase(r)
        P.barrier()

        block = es.enter_context(nc.Block())

        @block.tensor
        def _(e):
            for f in P.E['pe'].ops:
                f(e)

        @block.scalar
        def _(e):
            for f in P.E['act'].ops:
                f(e)

        @block.vector
        def _(e):
            for f in P.E['dve'].ops:
                f(e)

        @block.gpsimd
        def _(e):
            for f in P.E['pool'].ops:
                f(e)

        @block.sync
        def _(e):
            for f in P.E['sp'].ops:
                f(e)
    return nc


def make_in_maps(x_prompt, x_sample, c, state_hgrn, c_ctx, w_mod, b_mod, norm_mix, norm_mlp,
                 hgrn_w_in, hgrn_lb_fwd, hgrn_lb_bwd, hgrn_g_norm, hgrn_w_out,
                 conv_w_pw1, conv_w_dw, conv_b_dw, conv_ln_g, conv_ln_b, conv_w_pw2,
                 mlp_w1, mlp_w2, final_norm):
    f = lambda a: np.ascontiguousarray(np.asarray(a, dtype=np.float32))
    x_prompt, x_sample, c, state_hgrn, c_ctx = map(f, (x_prompt, x_sample, c, state_hgrn, c_ctx))
    hgrn_w_in = f(hgrn_w_in)
    w5 = hgrn_w_in.reshape(2, D, 5, NH, 128)
    whg = {}
    for half in range(2):
        order = [0, 1, 2, 3, 4] if half == 0 else [0, 1, 3, 2, 4]
        whg[half] = np.ascontiguousarray(w5[:, :, order].transpose(0, 3, 1, 2, 4).reshape(2, NH, D, 640))
    shared = dict(w_mod=f(w_mod), w_out=f(hgrn_w_out), w_pw1=f(conv_w_pw1), w_pw2=f(conv_w_pw2), w1=f(mlp_w1), w2=f(mlp_w2))
    in_maps = []
    for core in range(8):
        p, half = core // 2, core % 2
        xs = x_sample[p, half * TS:(half + 1) * TS]
        xp = x_prompt[4 * core:4 * core + 4]
        if half == 1:
            xs = xs[::-1]
            xp = xp[:, ::-1]
        vecs = np.zeros((128, D), np.float32)
        vecs[R_NMIX:R_NMIX + 4] = norm_mix
        vecs[R_NMLP:R_NMLP + 4] = norm_mlp
        lbf, lbb = (hgrn_lb_fwd, hgrn_lb_bwd) if half == 0 else (hgrn_lb_bwd, hgrn_lb_fwd)
        vecs[R_LB1:R_LB1 + 2] = lbf
        vecs[R_LB2:R_LB2 + 2] = lbb
        vecs[R_GN:R_GN + 2] = hgrn_g_norm
        vecs[R_BDW:R_BDW + 2] = conv_b_dw
        vecs[R_LNG:R_LNG + 2] = conv_ln_g
        vecs[R_LNB:R_LNB + 2] = conv_ln_b
        vecs[R_FN] = final_norm
        vecs[R_CV] = c_ctx
        vecs[R_CV + 1] = c[p]
        vecs[R_BMOD:R_BMOD + 24] = np.asarray(b_mod, np.float32).reshape(24, D)
        wdw = np.asarray(conv_w_dw, np.float32)
        if half == 1:
            wdw = wdw[:, ::-1]
        vecs[R_WDW:R_WDW + 62] = wdw.reshape(62, D)
        pinfo = np.zeros((128, 8), np.float32)
        sr = 1.0 if half == 0 else -1.0
        r0 = 0.0 if half == 0 else 63.0
        pinfo[:, 0] = sr
        pinfo[:, 1] = r0
        for tb in range(4):
            pinfo[:, 2 + tb] = r0 + sr * 8 * tb
        pinfo[:, 6] = 1.0 if half == 1 else 0.0
        pinfo[:, 7] = 1.0 if half == 0 else 0.0
        m = dict(shared)
        m.update(xs=np.ascontiguousarray(xs), xp=np.ascontiguousarray(xp.reshape(TP, D)), vecs=vecs,
                 s_init=np.ascontiguousarray(state_hgrn[p, :, half]), pinfo=pinfo, w_hg=whg[half])
        in_maps.append(m)
    return in_maps


def assemble(results):
    y_prompt = np.zeros((32, 256, D), np.float32)
    y_sample = np.zeros((4, 4096, D), np.float32)
    new_state = np.zeros((32, 2, 2, NH, 128, 128), np.float32)
    for core in range(8):
        p, half = core // 2, core % 2
        r = results[core]
        ys = np.asarray(r["ys"])
        yp = np.asarray(r["yp"]).reshape(4, 256, D)
        ns = np.asarray(r["ns"])
        if half == 1:
            ys = ys[::-1]
            yp = yp[:, ::-1]
            ns = ns[:, :, ::-1]
        y_sample[p, half * TS:(half + 1) * TS] = ys
        y_prompt[4 * core:4 * core + 4] = yp
        new_state[4 * core:4 * core + 4] = ns
    return y_prompt, y_sample, new_state


def kernel(**inputs):
    in_maps = make_in_maps(**inputs)
    nc = build_program()
    res = run_bass_kernel_spmd(nc, in_maps, core_ids=list(range(8)))
    return assemble(res.results)
```

```python
import math
import types
from contextlib import ExitStack

import numpy as np
import concourse.bass as bass
import concourse.mybir as mybir
from concourse.bass_utils import run_bass_kernel_spmd

F32 = mybir.dt.float32
BF16 = mybir.dt.bfloat16
I32 = mybir.dt.int32
AF = mybir.ActivationFunctionType
ALU = mybir.AluOpType

D = 1024
DEPTH = 4
NH = 8
CH = 32
KTAP = 31
EPS = 1e-6
KMAX = 1.0 - 1e-6
TS = 2048
TP = 1024
NDS = 24
NHW = 16

R_NMIX, R_NMLP, R_LB1, R_LB2, R_GN, R_BDW, R_LNG, R_LNB, R_FN, R_CV, R_BMOD, R_WDW = 0, 4, 8, 10, 12, 14, 16, 18, 20, 21, 23, 47


def _freeze(fn):
    if getattr(fn, "__closure__", None) is None:
        return fn
    cells = []
    for c in fn.__closure__:
        try:
            cells.append(types.CellType(c.cell_contents))
        except ValueError:
            cells.append(c)
    return types.FunctionType(fn.__code__, fn.__globals__, fn.__name__, fn.__defaults__, tuple(cells))


class Eng:
    def __init__(self, name, sem):
        self.name, self.sem, self.ops, self.n, self.seen = name, sem, [], 0, {}


class Prog:
    def __init__(self, nc, es):
        self.nc, self.es = nc, es
        self.E = {n: Eng(n, es.enter_context(nc.semaphore("s_" + n))) for n in ("pe", "act", "dve", "pool", "sp")}
        self.lastw, self.readers = {}, {}
        self.dsems = [es.enter_context(nc.semaphore("d%d" % i)) for i in range(NDS)]
        self.dcount = [0] * NDS
        self.dnext = 0
        self.dnext_sw = 0
        self.out_events = []

    def _wait(self, eng, ev):
        if ev[0] == 'c':
            _, src, seq = ev
            key, sem, val = src.name, src.sem, seq
        else:
            _, sem, val, key = ev
        if ev[0] == 'c' and eng.name == 'pe' and ev[1] is eng:
            return
        if eng.seen.get(key, 0) >= val:
            return
        eng.seen[key] = val
        eng.ops.append(lambda e, sem=sem, v=val: e.wait_ge(sem, v))

    def _deps(self, eng, reads, writes):
        for r in reads:
            if r in self.lastw:
                self._wait(eng, self.lastw[r])
        for w in writes:
            if w in self.lastw:
                self._wait(eng, self.lastw[w])
            for ev in self.readers.get(w, {}).values():
                self._wait(eng, ev)

    def _record(self, ev, key, reads, writes):
        for r in reads:
            self.readers.setdefault(r, {})[key] = ev
        for w in writes:
            self.lastw[w] = ev
            self.readers[w] = {}

    def op(self, en, fn, reads=(), writes=()):
        eng = self.E[en]
        fn = _freeze(fn)
        self._deps(eng, reads, writes)
        eng.n += 1
        ev = ('c', eng, eng.n)
        sem = eng.sem
        eng.ops.append(lambda e, fn=fn, sem=sem: fn(e).then_inc(sem, 1))
        self._record(ev, eng.name, reads, writes)
        return ev

    def dma(self, qn, out, in_, reads=(), writes=(), is_out=False):
        q = self.E[qn]
        self._deps(q, reads, writes)
        if qn == 'pool':
            i = NHW + self.dnext_sw
            self.dnext_sw = (self.dnext_sw + 1) % (NDS - NHW)
        else:
            i = self.dnext
            self.dnext = (i + 1) % NHW
        sem = self.dsems[i]
        key = ('d', i)
        if self.dcount[i] > 0:
            self._wait(q, ('d', sem, self.dcount[i], key))
        self.dcount[i] += 16
        ev = ('d', sem, self.dcount[i], key)
        q.ops.append(lambda e, o=out, a=in_, sem=sem: e.dma_start(out=o, in_=a).then_inc(sem, 16))
        self._record(ev, key, reads, writes)
        if is_out:
            self.out_events.append(ev)
        return ev

    def special(self, en, fn, sem, val, key, reads=(), writes=()):
        eng = self.E[en]
        fn = _freeze(fn)
        self._deps(eng, reads, writes)
        ev = ('d', sem, val, key)
        eng.ops.append(lambda e, fn=fn, sem=sem, val=val: fn(e).then_inc(sem, val))
        self._record(ev, key, reads, writes)
        self._wait(eng, ev)
        return ev

    def barrier(self, skip=()):
        evs = [('c', e, e.n) for e in self.E.values() if e.n > 0]
        for e in self.E.values():
            if e.name in skip:
                continue
            for ev in evs:
                self._wait(e, ev)
            for i in range(NDS):
                if self.dcount[i] > 0:
                    self._wait(e, ('d', self.dsems[i], self.dcount[i], ('d', i)))


def build_program(n_layers=DEPTH, phases=(0, 1), cc_inc=1, tiny=False):
    nc = bass.Bass("TRN2", target_bir_lowering=False)
    dt = lambda name, shape, kind="ExternalInput": nc.dram_tensor(name, list(shape), F32, kind=kind).ap()
    xs_d = dt("xs", [TS, D])
    xp_d = dt("xp", [TP, D])
    vecs_d = dt("vecs", [128, D])
    sinit_d = dt("s_init", [2, NH, 128, 128])
    pinfo_d = dt("pinfo", [128, 8])
    wmod_d = dt("w_mod", [1, 1] if tiny else [DEPTH, D, 6 * D])
    whg_d = dt("w_hg", [1, 1] if tiny else [2, NH, D, 640])
    wout_d = dt("w_out", [1, 1] if tiny else [2, D, D])
    pw1_d = dt("w_pw1", [1, 1] if tiny else [2, D, 2 * D])
    pw2_d = dt("w_pw2", [1, 1] if tiny else [2, D, D])
    w1_d = dt("w1", [1, 1] if tiny else [DEPTH, D, 4 * D])
    w2_d = dt("w2", [1, 1] if tiny else [DEPTH, 4 * D, D])
    ys_d = dt("ys", [TS, D], "ExternalOutput")
    yp_d = dt("yp", [TP, D], "ExternalOutput")
    ns_d = dt("ns", [4, 2, 2, NH, 128, 128], "ExternalOutput")
    cci_d = [[dt("cci%d_%d" % (a, h), [128, 128], "Internal") for h in range(NH)] for a in range(2)]
    cco_d = [[dt("cco%d_%d" % (a, h), [256, 128], "Internal") for h in range(NH)] for a in range(2)]
    hci_d = [nc.dram_tensor("hci%d" % b, [128, 8 * 16], BF16, kind="Internal").ap() for b in range(2)]
    hco_d = [nc.dram_tensor("hco%d" % b, [256, 8 * 16], BF16, kind="Internal").ap() for b in range(2)]

    with ExitStack() as es:
        P = Prog(nc, es)
        sb = lambda name, shape, dtp=F32: es.enter_context(nc.sbuf_tensor(name, list(shape), dtp))
        ccsems = [es.enter_context(nc.semaphore("cc%d" % i)) for i in range(20)]
        cc_next = [0]

        xT = sb("xT", [128, 8, TS])
        hT = sb("hT", [128, 8, TS], BF16)
        ringbuf = sb("ringbuf", [128, 16384], BF16)
        identF = sb("identF", [128, 128])
        identB = sb("identB", [128, 128], BF16)
        onesB = sb("onesB", [128, 128], BF16)
        M1 = sb("M1", [128, CH])
        M2 = sb("M2", [128, CH])
        cmask = sb("cmask", [128, 512])
        ones1 = sb("ones1", [128, 1])
        V = sb("V", [128, 8, 128])
        MOD = sb("MOD", [128, DEPTH, 2, 6, 8])
        WF = sb("WF", [128, DEPTH, 2, 2, 8])
        OML = sb("OML", [128, 2, 2, 8])
        pinfo = sb("pinfo_s", [128, 8])
        scT = sb("scT", [128, 8, 2], BF16)
        rstd = sb("rstd", [128, 512])
        sq = sb("sq", [128, 2, 512], BF16)
        tmpA = sb("tmpA", [128, 512])
        tmpB = sb("tmpB", [128, 512])
        WORK = sb("WORK", [128, 15872])

        ps = [es.enter_context(nc.psum_tensor("ps%d" % i, [128, 512], F32)) for i in range(7)]
        pst = es.enter_context(nc.psum_tensor("pst", [128, 1024], BF16))

        class Carver:
            def __init__(self):
                self.off = 0

            def take(self, shape, dtp=F32):
                n = int(np.prod(shape[1:]))
                words = n if dtp == F32 else (n + 1) // 2
                ap = WORK[:, self.off:self.off + words]
                self.off += words
                assert self.off <= 15872, self.off
                if dtp == BF16:
                    ap = ap.bitcast(BF16)[:, 0:n]
                if len(shape) == 3:
                    ap = ap.rearrange("p (a b) -> p a b", b=shape[2])
                return ap

        ring_i = [0]
        ring_big = [0]

        def stream(parts, big=False):
            if big:
                s = ring_big[0] % 2
                ring_big[0] += 1
                base = s * 8192
                key = 'ringH%d' % s
                wk = [key, 'ring%d' % (2 * s), 'ring%d' % (2 * s + 1)]
            else:
                s = ring_i[0] % 4
                ring_i[0] += 1
                base = s * 4096
                key = 'ring%d' % s
                wk = [key, 'ringH%d' % (s // 2)] + (['cmask16'] if s == 1 else [])
            for (c0, kk, nn, src) in parts:
                dst = ringbuf[:, base + c0:base + c0 + kk * nn].rearrange("p (k n) -> p k n", n=nn)
                P.dma('pool', dst, src, writes=wk)
            return base, key

        def wview(base, c0, kk, nn):
            return ringbuf[:, base + c0:base + c0 + kk * nn].rearrange("p (k n) -> p k n", n=nn)

        def kmajor(w2d, r0, nr, c0, ncol):
            return w2d[r0:r0 + nr, c0:c0 + ncol].rearrange("(k p) n -> p k n", p=128)

        XK = lambda sl: ('xT', sl.start // 512)
        HK = lambda sl: ('hT', sl.start // 512)
        bank_rr = [0]

        def nbank():
            b = bank_rr[0] % 3
            bank_rr[0] += 1
            return b

        def mm_group(items):
            def fn(e):
                last = None
                n = len(items)
                for i, (o, l, r, kw) in enumerate(items):
                    last = e.matmul(o, lhsT=l, rhs=r, start=(i == 0), stop=(i == n - 1), **kw)
                return last
            return fn

        P.op('pool', lambda e: e.memset(identF[:], 0.0), writes=['identF'])
        P.op('pool', lambda e: e.affine_select(out=identF[:], in_=identF[:], pattern=[[-1, 128]],
                                                compare_op=ALU.not_equal, fill=1.0, base=0, channel_multiplier=1),
             reads=['identF'], writes=['identF'])
        P.op('dve', lambda e: e.tensor_copy(out=identB[:], in_=identF[:]), reads=['identF'], writes=['identB'])
        P.op('pool', lambda e: e.memset(onesB[:], 1.0), writes=['onesB'])
        P.op('pool', lambda e: e.memset(ones1[:], 1.0), writes=['ones512'])
        P.op('pool', lambda e: e.memset(cmask[:], 1.0), writes=['cmask'])
        P.op('pool', lambda e: e.memset(cmask[:].rearrange("p (c t) -> p c t", t=CH)[:, :, 0:1], 0.0),
             reads=['cmask'], writes=['cmask'])
        cv = Carver()
        ip_f = cv.take([128, CH])
        it_f = cv.take([128, CH])
        ip_i = cv.take([128, CH]).bitcast(I32)
        it_i = cv.take([128, CH]).bitcast(I32)
        P.op('pool', lambda e: e.iota(ip_i, pattern=[[0, CH]], base=0, channel_multiplier=1), writes=['ip_i'])
        P.op('pool', lambda e: e.iota(it_i, pattern=[[1, CH]], base=0, channel_multiplier=0), writes=['it_i'])
        P.op('dve', lambda e: e.tensor_single_scalar(out=ip_i, in_=ip_i, scalar=CH - 1, op=ALU.bitwise_and),
             reads=['ip_i'], writes=['ip_i'])
        P.op('dve', lambda e: e.tensor_copy(out=ip_f, in_=ip_i), reads=['ip_i'], writes=['ip_f'])
        P.op('dve', lambda e: e.tensor_copy(out=it_f, in_=it_i), reads=['it_i'], writes=['it_f'])
        P.op('dve', lambda e: e.tensor_tensor(out=M1[:], in0=ip_f, in1=it_f, op=ALU.is_le),
             reads=['ip_f', 'it_f'], writes=['M1'])
        P.op('dve', lambda e: e.tensor_tensor(out=M2[:], in0=ip_f, in1=it_f, op=ALU.is_ge),
             reads=['ip_f', 'it_f'], writes=['M2'])

        vstage = cv.take([128, D])
        P.dma('sp', vstage, vecs_d[:, :], writes=['vstage'])
        P.dma('sp', pinfo[:], pinfo_d[:, :], writes=['pinfo'])
        for g in range(2):
            def fn(e, g=g):
                last = None
                for q in range(4):
                    fc = g * 4 + q
                    last = e.transpose(ps[g][:, q * 128:(q + 1) * 128], vstage[:, fc * 128:(fc + 1) * 128], identF[:])
                return last
            P.op('pe', fn, reads=['vstage', 'identF'], writes=['ps%d' % g])
            P.op('dve', lambda e, g=g: e.tensor_copy(out=V[:, g * 4:(g + 1) * 4, :],
                                                     in_=ps[g][:].rearrange("p (a b) -> p a b", b=128)),
                 reads=['ps%d' % g], writes=['V'])
        for d_, R_ in ((0, R_LB1), (1, R_LB2)):
            P.op('pool', lambda e, d_=d_: e.memset(OML[:, 0, d_, :], 1.0), writes=['OML'])
            P.op('dve', lambda e, R_=R_: e.tensor_tensor(out=tmpA[:, 0:8], in0=V[:, :, R_], in1=V[:, :, R_ + 1], op=ALU.subtract),
                 reads=['V'], writes=['tmpA'])
            P.op('act', lambda e, d_=d_: e.activation(out=OML[:, 1, d_, :], in_=tmpA[:, 0:8], func=AF.Sigmoid),
                 reads=['tmpA'], writes=['OML'])
        P.op('act', lambda e: e.activation(out=scT[:], in_=V[:, :, R_CV:R_CV + 2], func=AF.Silu), reads=['V'], writes=['scT'])
        for i in range(n_layers):
            psm = ps[6][:, 0:96]
            mq = {}
            for grp in range(12):
                for g2 in range(grp, min(12, grp + 3)):
                    if g2 not in mq:
                        mq[g2] = stream([(0, 8, 512, kmajor(wmod_d[i], 0, D, g2 * 512, 512))])
                s, key = mq[grp]
                w = wview(s, 0, 8, 512)
                for n in range(4):
                    j = grp * 4 + n
                    items = [(ps[6][:, 2 * j:2 * j + 2], w[:, kc, n * 128:(n + 1) * 128], scT[:, kc, :], {}) for kc in range(8)]
                    P.op('pe', mm_group(items), reads=[key, 'scT'], writes=[('psm', j)])
            for r in range(2):
                pin = psm.rearrange("p (m f r) -> p m f r", m=6, f=8)[:, :, :, r]
                bm = V[:, :, R_BMOD + i * 6:R_BMOD + i * 6 + 6].rearrange("p f m -> p m f")
                P.op('dve', lambda e, i=i, r=r, pin=pin, bm=bm: e.tensor_tensor(out=MOD[:, i, r, :, :], in0=pin, in1=bm, op=ALU.add),
                     reads=[('psm', j) for j in range(48)] + ['V'], writes=['MOD'])
                for sub, (mi, R_) in enumerate(((1, R_NMIX), (4, R_NMLP))):
                    P.op('dve', lambda e, i=i, r=r, sub=sub, mi=mi, R_=R_: e.scalar_tensor_tensor(
                        out=WF[:, i, r, sub, :], in0=MOD[:, i, r, mi, :], scalar=1.0, in1=V[:, :, R_ + i],
                        op0=ALU.add, op1=ALU.mult), reads=['MOD', 'V'], writes=['WF'])

        def stats_rstd(src_fn, T0, n, nfeat_chunks, dim, srckeys):
            ngr = (nfeat_chunks + 1) // 2
            for g in range(ngr):
                cnt = min(2, nfeat_chunks - g * 2)
                for q in range(cnt):
                    fc = g * 2 + q
                    src = src_fn(fc)
                    if q == 0:
                        P.op('act', lambda e, src=src, q=q: e.activation(out=sq[:, q, 0:n], in_=src, func=AF.Square),
                             reads=srckeys, writes=[('sq', q)])
                    else:
                        P.op('dve', lambda e, src=src, q=q: e.tensor_tensor(out=sq[:, q, 0:n], in0=src, in1=src, op=ALU.mult),
                             reads=srckeys, writes=[('sq', q)])

                def fn(e, g=g, cnt=cnt):
                    last = None
                    for q in range(cnt):
                        last = e.matmul(ps[6][:, 0:n], lhsT=onesB[:], rhs=sq[:, q, 0:n],
                                        start=(g == 0 and q == 0), stop=(g == ngr - 1 and q == cnt - 1))
                    return last
                P.op('pe', fn, reads=[('sq', q) for q in range(cnt)] + ['onesB'], writes=['ps6'])
            P.op('act', lambda e: e.activation(out=rstd[:, 0:n], in_=ps[6][:, 0:n], func=AF.Ln, scale=1.0 / dim, bias=EPS),
                 reads=['ps6'], writes=['rstd'])
            P.op('act', lambda e: e.activation(out=rstd[:, 0:n], in_=rstd[:, 0:n], func=AF.Exp, scale=-0.5), reads=['rstd'], writes=['rstd'])

        def norm_mod(i, sub, r, T):
            for tb in range(T // 512):
                sl = slice(tb * 512, (tb + 1) * 512)
                stats_rstd(lambda fc: xT[:, fc, sl], tb * 512, 512, 8, D, [XK(sl)])
                for fc in range(8):
                    tb_, tk_ = (tmpA, 'tmpA') if fc % 2 == 0 else (tmpB, 'tmpB')
                    P.op('dve', lambda e, fc=fc, tb_=tb_: e.scalar_tensor_tensor(
                        out=tb_[:], in0=xT[:, fc, sl], scalar=WF[:, i, r, sub, fc:fc + 1], in1=rstd[:],
                        op0=ALU.mult, op1=ALU.mult), reads=[XK(sl), 'WF', 'rstd'], writes=[tk_])
                    P.op('act', lambda e, fc=fc, tb_=tb_: e.activation(out=hT[:, fc, sl], in_=tb_[:], func=AF.Identity,
                                                                        bias=MOD[:, i, r, 0 if sub == 0 else 3, fc:fc + 1]),
                         reads=[tk_, 'MOD'], writes=[HK(sl)])

        def resid_evac(bank, i, r, gi, m, sl, n=512):
            P.op('dve', lambda e: e.scalar_tensor_tensor(
                out=xT[:, m, sl], in0=ps[bank][:, 0:n], scalar=MOD[:, i, r, gi, m:m + 1], in1=xT[:, m, sl],
                op0=ALU.mult, op1=ALU.add), reads=['ps%d' % bank, 'MOD', XK(sl)], writes=[XK(sl)])

        def mlp(i, r, T):
            wq = {}

            def wload(j):
                wq[j] = (stream([(0, 8, 512, kmajor(w1_d[i], 0, D, j * 512, 512))]),
                         stream([(0, 4, 1024, kmajor(w2_d[i], j * 512, 512, 0, D))]))
            wload(0)
            wload(1)
            norm_mod(i, 1, r, T)
            P.barrier(skip=('pe',))
            cvm = Carver()
            hid = [cvm.take([128, 4, 512], BF16) for _ in range(2)]
            rl = [cvm.take([128, 512], BF16) for _ in range(2)]
            NBk = T // 512
            items_ = [(j, tb) for j in range(8) for tb in range(NBk)]
            rc = [0]

            def S1(k):
                j, tb = items_[k]
                (sA, kA), _ = wq[j]
                wA = wview(sA, 0, 8, 512)
                sl = slice(tb * 512, (tb + 1) * 512)
                hb, hk = hid[k % 2], 'hid%d' % (k % 2)
                for n in range(4):
                    b = nbank()
                    its = [(ps[b][:], wA[:, kc, n * 128:(n + 1) * 128], hT[:, kc, sl], {}) for kc in range(8)]
                    P.op('pe', mm_group(its), reads=[kA, HK(sl)], writes=['ps%d' % b])
                    rb, rk = rl[rc[0] % 2], 'rl%d' % (rc[0] % 2)
                    rc[0] += 1
                    P.op('act', lambda e, b=b, rb=rb: e.activation(out=rb[:], in_=ps[b][:], func=AF.Relu),
                         reads=['ps%d' % b], writes=[rk])
                    P.op('pool', lambda e, hb=hb, n=n, rb=rb: e.tensor_tensor(out=hb[:, n, :], in0=rb[:], in1=rb[:], op=ALU.mult),
                         reads=[rk], writes=[(hk, n)])

            def S2(k):
                j, tb = items_[k]
                _, (sB, kB) = wq[j]
                wB = wview(sB, 0, 4, 1024)
                sl = slice(tb * 512, (tb + 1) * 512)
                hb, hk = hid[k % 2], 'hid%d' % (k % 2)
                for m in range(8):
                    b = nbank()
                    its = [(ps[b][:], wB[:, n, m * 128:(m + 1) * 128], hb[:, n, :], {}) for n in range(4)]
                    P.op('pe', mm_group(its), reads=[kB] + [(hk, n) for n in range(4)], writes=['ps%d' % b])
                    resid_evac(b, i, r, 5, m, sl)
            for k in range(len(items_) + 1):
                if k < len(items_):
                    S1(k)
                if k >= 1:
                    S2(k - 1)
                    jp, tbp = items_[k - 1]
                    if tbp == NBk - 1 and jp + 2 < 8:
                        wload(jp + 2)

        def hgrn(i, r, T, seqs):
            a = i // 2
            hq = {}

            def hload(h):
                hq[h] = stream([(0, 8, 640, kmajor(whg_d[a, h], 0, D, 0, 640)),
                                (5120, 1, 1024, kmajor(wout_d[a], h * 128, 128, 0, D))], big=True)
            hload(0)
            norm_mod(i, 0, r, T)
            P.barrier(skip=('pe',))
            NB = T // 512
            NT = T // 128
            NC = T // CH
            c = Carver()
            qf = c.take([128, T], BF16)
            vT = c.take([128, NT, 128], BF16)
            sg = c.take([128, T], BF16)
            o = c.take([128, T])
            q2g = c.take([128, T], BF16) if r == 1 else None
            qt = c.take([128, T], BF16)
            q16 = c.take([128, T], BF16)
            KA = c.take([128, T], BF16)
            KB = c.take([128, T], BF16)
            khT = c.take([128, NT, 128], BF16)
            vf = c.take([128, 512], BF16)
            khs = [c.take([128, 256], BF16) for _ in range(2)]
            og = vf
            kks = [c.take([128, 256]) for _ in range(2)]
            lgs = [c.take([128, 256]) for _ in range(2)]
            bbs = [c.take([128, 256]) for _ in range(2)]
            b16s = [c.take([128, 256]) for _ in range(2)]
            ebks = [c.take([128, 256]) for _ in range(2)]
            enbs = [c.take([128, 256]) for _ in range(2)]
            eL = c.take([128, NC])
            S32 = [[c.take([128, 128]) for _ in range(2)] for _ in range(len(seqs))]
            Sb = [[c.take([128, 128], BF16) for _ in range(2)] for _ in range(len(seqs))]
            Pm = [c.take([128, CH], BF16) for _ in range(4)]
            Sboth = c.take([128, 2, 128])
            Sin_b = c.take([128, 128], BF16)
            carry = c.take([128, 2])
            cmask16 = ringbuf[:, 6144:8192].bitcast(F32)[:, 0:512]
            P.op('pool', lambda e: e.memset(cmask16, 1.0), writes=['cmask16', 'ring1'])
            P.op('pool', lambda e: e.memset(cmask16.rearrange("p (c t) -> p c t", t=16)[:, :, 0:1], 0.0), reads=['cmask16'], writes=['cmask16', 'ring1'])
            scale = float(128 ** -0.5)
            H = CH // 2
            for h in range(NH):
                if h + 1 < NH:
                    hload(h + 1)
                s, key = hq.pop(h)
                w = wview(s, 0, 8, 640)
                wo = ringbuf[:, s + 5120:s + 6144]
                blocks = list(range(NB))[::-1]

                def proj(part, sl):
                    b = nbank()
                    items = [(ps[b][:], w[:, kc, part * 128:(part + 1) * 128], hT[:, kc, sl], {}) for kc in range(8)]
                    P.op('pe', mm_group(items), reads=[key, HK(sl)], writes=['ps%d' % b])
                    return b
                for tb in blocks:
                    sl = slice(tb * 512, (tb + 1) * 512)
                    b = proj(0, sl)
                    P.op('act', lambda e, b=b: e.activation(out=qf[:, sl], in_=ps[b][:], func=AF.Silu), reads=['ps%d' % b], writes=['qf'])
                    b = proj(1, sl)
                    P.op('act', lambda e, b=b: e.activation(out=vf[:], in_=ps[b][:], func=AF.Copy), reads=['ps%d' % b], writes=['vf'])
                    b = proj(4, sl)
                    P.op('act', lambda e, b=b: e.activation(out=sg[:, sl], in_=ps[b][:], func=AF.Silu), reads=['ps%d' % b], writes=['sg'])

                    def fnv(e):
                        last = None
                        for q in range(4):
                            last = e.transpose(pst[:, q * 128:(q + 1) * 128], vf[:, q * 128:(q + 1) * 128], identB[:])
                        return last
                    P.op('pe', fnv, reads=['vf', 'identB'], writes=['pst'])
                    P.op('act', lambda e, tb=tb: e.activation(out=vT[:, tb * 4:(tb + 1) * 4, :],
                                                               in_=pst[:, 0:512].rearrange("p (a b) -> p a b", b=128), func=AF.Copy),
                         reads=['pst'], writes=['vT'])
                for d in range(2):
                    rv = (lambda ap: ap) if d == 0 else (lambda ap: ap[:, ::-1])
                    li = CH - 1 if d == 0 else 0
                    ge, gl = (0, 1) if d == 0 else (1, 0)
                    l16 = H - 1 if d == 0 else 0
                    SBK = 256
                    NCS = SBK // CH
                    subs = [(tb, hb) for tb in blocks for hb in (1, 0)]
                    st_ = {}

                    def stage(k, u, s_):
                        tb, hb = u
                        sl = slice(tb * 512 + hb * SBK, tb * 512 + (hb + 1) * SBK)
                        tA = (tmpA, tmpB)[s_][:, 0:SBK]
                        tk = ('tmpA', 'tmpB')[s_]
                        kk_, lg_, bb_, ebk_, enb_, b16_, kh_ = kks[s_], lgs[s_], bbs[s_], ebks[s_], enbs[s_], b16s[s_], khs[s_]
                        K = lambda n: (n, s_)
                        if k == 0:
                            b = nbank()
                            st_[u] = b
                            items = [(ps[b][:, 0:SBK], w[:, kc, (2 + d) * 128:(3 + d) * 128], hT[:, kc, sl], {}) for kc in range(8)]
                            P.op('pe', mm_group(items), reads=[key, HK(sl)], writes=['ps%d' % b])
                        elif k == 1:
                            b = st_[u]
                            P.op('act', lambda e: e.activation(out=tA, in_=ps[b][:, 0:SBK], func=AF.Exp), reads=['ps%d' % b], writes=[tk])
                        elif k == 2:
                            P.op('act', lambda e: e.activation(out=tA, in_=tA, func=AF.Ln, bias=1.0), reads=[tk], writes=[tk])
                        elif k == 3:
                            P.op('act', lambda e: e.activation(out=tA, in_=tA, func=AF.Exp, scale=-1.0), reads=[tk], writes=[tk])
                        elif k == 4:
                            P.op('dve', lambda e: e.tensor_scalar(out=kk_, in0=tA, scalar1=OML[:, a, d, h:h + 1], scalar2=KMAX,
                                                                  op0=ALU.mult, op1=ALU.min), reads=[tk, 'OML'], writes=[K('kk')])
                        elif k == 5:
                            P.op('act', lambda e: e.activation(out=lg_, in_=kk_, func=AF.Ln, scale=-1.0, bias=1.0), reads=[K('kk')], writes=[K('lg')])
                        elif k == 6:
                            P.op('dve', lambda e: e.tensor_tensor_scan(out=rv(bb_), data0=cmask[:, 0:SBK], data1=rv(lg_), initial=0.0,
                                                                       op0=ALU.mult, op1=ALU.add), reads=['cmask', K('lg')], writes=[K('bb')])
                        elif k == 7:
                            P.op('act', lambda e: e.activation(out=ebk_, in_=bb_, func=AF.Exp), reads=[K('bb')], writes=[K('ebk')])
                        elif k == 8:
                            ebv = ebk_.rearrange("p (c t) -> p c t", t=CH)
                            c0 = (tb * 512 + hb * SBK) // CH
                            P.op('dve', lambda e: e.tensor_copy(out=eL[:, c0:c0 + NCS].rearrange("p (c o) -> p c o", o=1), in_=ebv[:, :, li:li + 1]),
                                 reads=[K('ebk')], writes=['eL'])
                            P.op('dve', lambda e: e.scalar_tensor_tensor(out=qt[:, sl], in0=qf[:, sl], scalar=scale, in1=ebk_,
                                                                         op0=ALU.mult, op1=ALU.mult), reads=['qf', K('ebk')], writes=['qt'])
                        elif k == 9:
                            bbv = bb_.rearrange("p (c t) -> p c t", t=CH)
                            P.op('dve', lambda e: e.tensor_tensor(out=tA.rearrange("p (c t) -> p c t", t=CH), in0=bbv,
                                                                  in1=bbv[:, :, li:li + 1].to_broadcast([128, NCS, CH]), op=ALU.subtract),
                                 reads=[K('bb')], writes=[tk])
                        elif k == 10:
                            P.op('act', lambda e: e.activation(out=enb_, in_=tA, func=AF.Exp, scale=-1.0), reads=[tk], writes=[K('enb')])
                        elif k == 11:
                            P.op('dve', lambda e: e.tensor_tensor(out=kh_, in0=kk_, in1=enb_, op=ALU.mult), reads=[K('kk'), K('enb')], writes=[K('kh')])
                        elif k == 12:
                            pc = 512 + s_ * SBK

                            def fnk(e):
                                last = None
                                for q in range(2):
                                    last = e.transpose(pst[:, pc + q * 128:pc + (q + 1) * 128], kh_[:, q * 128:(q + 1) * 128], identB[:])
                                return last
                            P.op('pe', fnk, reads=[K('kh'), 'identB'], writes=['pst'])
                            t2 = tb * 4 + hb * 2
                            P.op('act', lambda e: e.activation(out=khT[:, t2:t2 + 2, :],
                                                               in_=pst[:, pc:pc + SBK].rearrange("p (a b) -> p a b", b=128), func=AF.Copy),
                                 reads=['pst'], writes=['khT'])
                        elif k == 13:
                            P.op('dve', lambda e: e.tensor_tensor_scan(out=rv(b16_), data0=cmask16[:, 0:SBK], data1=rv(lg_), initial=0.0,
                                                                       op0=ALU.mult, op1=ALU.add), reads=['cmask16', K('lg')], writes=[K('b16')])
                        elif k == 14:
                            P.op('act', lambda e: e.activation(out=ebk_, in_=b16_, func=AF.Exp), reads=[K('b16')], writes=[K('ebk')])
                        elif k == 15:
                            P.op('dve', lambda e: e.scalar_tensor_tensor(out=q16[:, sl], in0=qf[:, sl], scalar=scale, in1=ebk_,
                                                                         op0=ALU.mult, op1=ALU.mult), reads=['qf', K('ebk')], writes=['q16'])
                        elif k == 16:
                            P.op('act', lambda e: e.activation(out=enb_, in_=b16_, func=AF.Exp, scale=-1.0), reads=[K('b16')], writes=[K('enb')])
                        elif k == 17:
                            P.op('dve', lambda e: e.tensor_tensor(out=KA[:, sl], in0=kk_, in1=enb_, op=ALU.mult), reads=[K('kk'), K('enb')], writes=['KA'])
                        elif k == 18:
                            KAv = KA[:, sl].rearrange("p (c g t) -> p c g t", g=2, t=H)
                            KBv = KB[:, sl].rearrange("p (c g t) -> p c g t", g=2, t=H)
                            e16 = ebk_.rearrange("p (c g t) -> p c g t", g=2, t=H)
                            P.op('dve', lambda e: e.tensor_tensor(out=KBv[:, :, ge, :], in0=KAv[:, :, ge, :],
                                                                  in1=e16[:, :, ge, l16:l16 + 1].to_broadcast([128, NCS, H]), op=ALU.mult),
                                 reads=['KA', K('ebk')], writes=['KB'])
                            P.op('pool', lambda e: e.tensor_copy(out=KBv[:, :, gl, :], in_=KAv[:, :, gl, :]), reads=['KA'], writes=['KB'])
                        elif k == 19 and d == 1 and r == 1:
                            first = (tb == NB - 1 and hb == 1)
                            init = 0.0 if first else carry[:, 0:1]
                            P.op('dve', lambda e: e.tensor_tensor_scan(out=bb_[:, ::-1], data0=ones1[:, 0:1].to_broadcast([128, SBK]), data1=lg_[:, ::-1],
                                                                       initial=init, op0=ALU.mult, op1=ALU.add),
                                 reads=['ones512', K('lg'), 'carry'], writes=[K('bb')])
                            P.op('dve', lambda e: e.tensor_copy(out=carry[:, 0:1], in_=bb_[:, 0:1]), reads=[K('bb')], writes=['carry'])
                        elif k == 20 and d == 1 and r == 1:
                            P.op('act', lambda e: e.activation(out=enb_, in_=bb_, func=AF.Exp), reads=[K('bb')], writes=[K('enb')])
                        elif k == 21 and d == 1 and r == 1:
                            P.op('dve', lambda e: e.scalar_tensor_tensor(out=q2g[:, sl], in0=qf[:, sl], scalar=scale, in1=enb_,
                                                                         op0=ALU.mult, op1=ALU.mult), reads=['qf', K('enb')], writes=['q2g'])
                    NSTG = 22
                    pairs_ = [(subs[pi], subs[pi + 1]) for pi in range(0, len(subs), 2)]
                    stage(0, pairs_[0][0], 0)
                    stage(0, pairs_[0][1], 1)
                    for pj, (uA, uB) in enumerate(pairs_):
                        for k in range(1, NSTG + 1):
                            if k < NSTG:
                                stage(k, uA, 0)
                            if k - 1 >= 1:
                                stage(k - 1, uB, 1)
                            if k == 5 and pj + 1 < len(pairs_):
                                stage(0, pairs_[pj + 1][0], 0)
                                stage(0, pairs_[pj + 1][1], 1)
                    Mk = M1 if d == 0 else M2
                    nl_ = len(seqs)
                    spl = 4 // nl_
                    steps = []
                    for (t0, L) in seqs:
                        tl = list(range(t0 // 128, (t0 + L) // 128))
                        od = [0, 1, 2, 3]
                        if d == 1:
                            tl, od = tl[::-1], od[::-1]
                        steps.append([(tt, j, k == 0, k == 3) for tt in tl for k, j in enumerate(od)])
                    NS = len(steps[0])
                    for l in range(nl_):
                        if r == 1 and d == 0:
                            P.dma('sp', S32[l][0], sinit_d[a, h], writes=[('S32', l, 0)])
                            P.op('act', lambda e, l=l: e.activation(out=Sb[l][0][:], in_=S32[l][0][:], func=AF.Copy),
                                 reads=[('S32', l, 0)], writes=[('Sb', l, 0)])
                        else:
                            P.op('pool', lambda e, l=l: e.memset(S32[l][0][:], 0.0), writes=[('S32', l, 0)])
                            P.op('pool', lambda e, l=l: e.memset(Sb[l][0][:], 0.0), writes=[('Sb', l, 0)])

                    def emitD(l, n):
                        tt, j, _, _ = steps[l][n]
                        slot = l * spl + n % spl
                        pr = slice(j * CH, (j + 1) * CH)
                        P.op('pe', lambda e, slot=slot, pr=pr, tt=tt, j=j: e.matmul(
                            ps[slot][:, 0:128], lhsT=khT[pr, tt, :], rhs=vT[pr, tt, :], start=True, stop=True,
                            tile_position=(j * CH, 0)), reads=['khT', 'vT'], writes=['ps%d' % slot])
                    look = spl - 1
                    for l in range(nl_):
                        for n in range(min(look, NS)):
                            emitD(l, n)
                    for n in range(NS):
                        for l in range(nl_):
                            tt, j, first, last = steps[l][n]
                            tslot = l * spl + (n // 4) % spl
                            if first:
                                def fnA(e, tt=tt, tslot=tslot):
                                    last_ = None
                                    for jj in range(4):
                                        c0 = tt * 128 + jj * CH
                                        for g in range(2):
                                            lhs = KA if g == ge else KB
                                            last_ = e.matmul(ps[5][jj * CH:(jj + 1) * CH, tslot * CH + g * H:tslot * CH + (g + 1) * H],
                                                             lhsT=lhs[:, c0:c0 + CH], rhs=q16[:, c0 + g * H:c0 + (g + 1) * H],
                                                             start=True, stop=True, tile_position=(0, jj * CH))
                                    return last_
                                P.op('pe', fnA, reads=['KA', 'KB', 'q16'], writes=['ps5'])
                                P.op('dve', lambda e, Mk=Mk, tslot=tslot: e.tensor_tensor(out=Pm[tslot][:], in0=ps[5][:, tslot * CH:(tslot + 1) * CH],
                                                                                          in1=Mk[:], op=ALU.mult),
                                     reads=['ps5', 'M1', 'M2'], writes=[('Pm', tslot)])
                            if n + look < NS:
                                emitD(l, n + look)
                            slot = l * spl + n % spl
                            ts_ = slice(tt * 128 + j * CH, tt * 128 + (j + 1) * CH)
                            pr = slice(j * CH, (j + 1) * CH)
                            ci = (tt * 128 + j * CH) // CH
                            oc = ps[4][:, tslot * 128 + j * CH:tslot * 128 + (j + 1) * CH]
                            cur, nxt = n % 2, (n + 1) % 2

                            def fnBC(e, pr=pr, ts_=ts_, oc=oc, tt=tt, j=j, tslot=tslot, l=l, cur=cur):
                                e.matmul(oc, lhsT=vT[pr, tt, :], rhs=Pm[tslot][pr, :], start=True, stop=False, tile_position=(j * CH, 0))
                                return e.matmul(oc, lhsT=Sb[l][cur][:], rhs=qt[:, ts_], start=False, stop=True)
                            P.op('pe', fnBC, reads=['vT', ('Pm', tslot), ('Sb', l, cur), 'qt'], writes=['ps4'])
                            ecol = eL[:, ci:ci + 1]
                            P.op('dve', lambda e, ecol=ecol, l=l, cur=cur, nxt=nxt, slot=slot: e.scalar_tensor_tensor(
                                out=S32[l][nxt][:], in0=S32[l][cur][:], scalar=ecol, in1=ps[slot][:, 0:128],
                                op0=ALU.mult, op1=ALU.add), reads=[('S32', l, cur), 'eL', 'ps%d' % slot], writes=[('S32', l, nxt)])
                            P.op('act', lambda e, l=l, nxt=nxt: e.activation(out=Sb[l][nxt][:], in_=S32[l][nxt][:], func=AF.Copy),
                                 reads=[('S32', l, nxt)], writes=[('Sb', l, nxt)])
                            if last:
                                osl = slice(tt * 128, (tt + 1) * 128)
                                pso = ps[4][:, tslot * 128:(tslot + 1) * 128]
                                if d == 0:
                                    P.op('act', lambda e, osl=osl, pso=pso: e.activation(out=o[:, osl], in_=pso, func=AF.Copy),
                                         reads=['ps4'], writes=[('o', tt)])
                                else:
                                    P.op('dve', lambda e, osl=osl, pso=pso: e.tensor_tensor(out=o[:, osl], in0=o[:, osl], in1=pso, op=ALU.add),
                                         reads=['ps4', ('o', tt)], writes=[('o', tt)])
                    for l in range(nl_):
                        fin = S32[l][NS % 2]
                        fk = ('S32', l, NS % 2)
                        if r == 0:
                            P.dma('sp', ns_d[l, a, d, h], fin, reads=[fk], is_out=True)
                        elif d == 0:
                            P.dma('sp', cci_d[a][h], fin, reads=[fk], writes=[('cci', a, h)])
                            k_ = cc_next[0]
                            cc_next[0] += 1
                            P.special('pool', lambda e, a=a, h=h: e.collective_compute(
                                "AllGather", ALU.bypass, replica_groups=[[0, 1], [2, 3], [4, 5], [6, 7]],
                                ins=[cci_d[a][h]], outs=[cco_d[a][h]]), ccsems[k_], cc_inc, ('cc', k_),
                                reads=[('cci', a, h)], writes=[('cco', a, h)])
                if r == 1:
                    P.dma('sp', Sboth, cco_d[a][h].rearrange("(r p) v -> p r v", p=128), reads=[('cco', a, h)], writes=['Sboth'])
                    P.op('dve', lambda e: e.tensor_scalar(out=Sboth[:, 0, :], in0=Sboth[:, 0, :], scalar1=pinfo[:, 6:7], scalar2=None, op0=ALU.mult),
                         reads=['Sboth', 'pinfo'], writes=['Sboth'])
                    P.op('dve', lambda e: e.scalar_tensor_tensor(out=Sin_b[:], in0=Sboth[:, 1, :], scalar=pinfo[:, 7:8], in1=Sboth[:, 0, :],
                                                                 op0=ALU.mult, op1=ALU.add), reads=['Sboth', 'pinfo'], writes=['Sin_b'])
                    for tb in range(NB):
                        sl = slice(tb * 512, (tb + 1) * 512)
                        b = nbank()
                        P.op('pe', lambda e, b=b, sl=sl: e.matmul(ps[b][:], lhsT=Sin_b[:], rhs=q2g[:, sl], start=True, stop=True),
                             reads=['Sin_b', 'q2g'], writes=['ps%d' % b])
                        ok_ = [('o', tb * 4 + q) for q in range(4)]
                        P.op('dve', lambda e, b=b, sl=sl: e.tensor_tensor(out=o[:, sl], in0=o[:, sl], in1=ps[b][:], op=ALU.add),
                             reads=['ps%d' % b] + ok_, writes=ok_)
                for tb in range(NB):
                    sl = slice(tb * 512, (tb + 1) * 512)
                    ok_ = [('o', tb * 4 + q) for q in range(4)]
                    stats_rstd(lambda fc: o[:, sl], 0, 512, 1, 128, ok_)
                    P.op('dve', lambda e, sl=sl: e.scalar_tensor_tensor(out=tmpA[:], in0=o[:, sl], scalar=V[:, h, R_GN + a:R_GN + a + 1], in1=rstd[:],
                                                                        op0=ALU.mult, op1=ALU.mult), reads=ok_ + ['V', 'rstd'], writes=['tmpA'])
                    P.op('dve', lambda e, sl=sl: e.tensor_tensor(out=og[:], in0=tmpA[:], in1=sg[:, sl], op=ALU.mult),
                         reads=['tmpA', 'sg'], writes=['vf'])
                    for m in range(8):
                        b = nbank()
                        P.op('pe', lambda e, b=b, m=m: e.matmul(ps[b][:], lhsT=wo[:, m * 128:(m + 1) * 128], rhs=og[:], start=True, stop=True),
                             reads=[key, 'vf'], writes=['ps%d' % b])
                        resid_evac(b, i, r, 2, m, sl)

        def conv(i, r, T, seqs):
            bi = i // 2
            cq = {}
            for g in range(2):
                cq[g] = (stream([(0, 8, 512, kmajor(pw1_d[bi], 0, D, g * 512, 512))]),
                         stream([(0, 8, 512, kmajor(pw1_d[bi], 0, D, D + g * 512, 512))]))
            norm_mod(i, 0, r, T)
            P.barrier(skip=('pe',))
            c = Carver()
            PADW = T + 30 * len(seqs)
            upad = c.take([128, 8, PADW], BF16)
            dgs = [c.take([128, KTAP, 128], BF16) for _ in range(2)]
            tS = c.take([128, 512])
            mu = c.take([128, 512])
            hb_st = c.take([128, 8 * 16], BF16)
            hb_in = c.take([128, 2, 8 * 16], BF16)
            P.op('pool', lambda e: e.memset(upad[:], 0.0), writes=['upad'])
            offs = [t0 + 30 * si + 15 for si, (t0, L) in enumerate(seqs)]
            for g in range(2):
                (sA, kA), (sG, kG) = cq[g]
                wA = wview(sA, 0, 8, 512)
                wG = wview(sG, 0, 8, 512)
                for n in range(4):
                    fc = g * 4 + n
                    for si, (t0, L) in enumerate(seqs):
                        for tb in range(max(1, L // 512)):
                            n_ = min(512, L)
                            sl = slice(t0 + tb * 512, t0 + tb * 512 + n_)
                            b1 = nbank()
                            P.op('pe', mm_group([(ps[b1][:, 0:n_], wG[:, kc, n * 128:(n + 1) * 128], hT[:, kc, sl], {}) for kc in range(8)]),
                                 reads=[kG, HK(sl)], writes=['ps%d' % b1])
                            P.op('act', lambda e, b1=b1, n_=n_: e.activation(out=tS[:, 0:n_], in_=ps[b1][:, 0:n_], func=AF.Sigmoid),
                                 reads=['ps%d' % b1], writes=['tS'])
                            b2 = nbank()
                            P.op('pe', mm_group([(ps[b2][:, 0:n_], wA[:, kc, n * 128:(n + 1) * 128], hT[:, kc, sl], {}) for kc in range(8)]),
                                 reads=[kA, HK(sl)], writes=['ps%d' % b2])
                            po = offs[si] + tb * 512
                            P.op('dve', lambda e, b2=b2, n_=n_, fc=fc, po=po: e.tensor_tensor(
                                out=upad[:, fc, po:po + n_], in0=ps[b2][:, 0:n_], in1=tS[:, 0:n_], op=ALU.mult),
                                reads=['ps%d' % b2, 'tS'], writes=['upad'])
            if r == 1:
                po = offs[0] + T
                P.op('dve', lambda e: e.tensor_copy(out=hb_st[:].rearrange("p (f t) -> p f t", t=16)[:, :, 0:15], in_=upad[:, :, po - 15:po][:, :, ::-1]),
                     reads=['upad'], writes=['hb_st'])
                P.dma('sp', hci_d[bi], hb_st, reads=['hb_st'], writes=[('hci', bi)])
                k_ = cc_next[0]
                cc_next[0] += 1
                P.special('pool', lambda e: e.collective_compute("AllGather", ALU.bypass, replica_groups=[[0, 1], [2, 3], [4, 5], [6, 7]],
                                                                 ins=[hci_d[bi]], outs=[hco_d[bi]]), ccsems[k_], cc_inc, ('cc', k_),
                          reads=[('hci', bi)], writes=[('hco', bi)])
                P.dma('sp', hb_in, hco_d[bi].rearrange("(r p) v -> p r v", p=128), reads=[('hco', bi)], writes=['hb_in'])
                P.op('dve', lambda e: e.tensor_scalar(out=hb_in[:, 0, :], in0=hb_in[:, 0, :], scalar1=pinfo[:, 6:7], scalar2=None, op0=ALU.mult),
                     reads=['hb_in', 'pinfo'], writes=['hb_in'])
                P.op('dve', lambda e: e.scalar_tensor_tensor(out=upad[:, :, po:po + 15], in0=hb_in[:, 1, :].rearrange("p (f t) -> p f t", t=16)[:, :, 0:15],
                                                             scalar=pinfo[:, 7:8], in1=hb_in[:, 0, :].rearrange("p (f t) -> p f t", t=16)[:, :, 0:15],
                                                             op0=ALU.mult, op1=ALU.add), reads=['hb_in', 'pinfo', 'upad'], writes=['upad'])
            def build_dg(fc):
                dg = dgs[fc % 2]
                for j in range(KTAP):
                    en = ('dve', 'pool', 'act')[j % 3]
                    sc_ = V[:, fc, R_WDW + bi * KTAP + j:R_WDW + bi * KTAP + j + 1]
                    if en == 'act':
                        P.op('act', lambda e, j=j, sc_=sc_, dg=dg: e.activation(out=dg[:, j, :], in_=identB[:], func=AF.Copy, scale=sc_),
                             reads=['identB', 'V'], writes=[('dg', fc % 2, j)])
                    else:
                        P.op(en, lambda e, j=j, sc_=sc_, dg=dg: e.tensor_scalar(out=dg[:, j, :], in0=identB[:], scalar1=sc_, scalar2=None, op0=ALU.mult),
                             reads=['identB', 'V'], writes=[('dg', fc % 2, j)])
            build_dg(0)
            for fc in range(8):
                if fc + 1 < 8:
                    build_dg(fc + 1)
                dg = dgs[fc % 2]
                for si, (t0, L) in enumerate(seqs):
                    for tb in range(max(1, L // 512)):
                        n_ = min(512, L)
                        po = offs[si] + tb * 512 - 15
                        b = nbank()
                        P.op('pe', mm_group([(ps[b][:, 0:n_], dg[:, j, :], upad[:, fc, po + j:po + j + n_], {}) for j in range(KTAP)]),
                             reads=[('dg', fc % 2, j) for j in range(KTAP)] + ['upad'], writes=['ps%d' % b])
                        sl = slice(t0 + tb * 512, t0 + tb * 512 + n_)
                        P.op('act', lambda e, b=b, n_=n_, sl=sl, fc=fc: e.activation(out=hT[:, fc, sl], in_=ps[b][:, 0:n_], func=AF.Identity,
                                                                                     bias=V[:, fc, R_BDW + bi:R_BDW + bi + 1]),
                             reads=['ps%d' % b, 'V'], writes=[HK(sl)])
            pq = [stream([(0, 8, 512, kmajor(pw2_d[bi], 0, D, g * 512, 512))]) for g in range(2)]
            def LN(tb):
                sl = slice(tb * 512, (tb + 1) * 512)
                P.op('pe', mm_group([(ps[5][:], onesB[:], hT[:, fc, sl], {}) for fc in range(8)]), reads=['onesB', HK(sl)], writes=['ps5'])
                P.op('dve', lambda e: e.tensor_scalar(out=mu[:], in0=ps[5][:], scalar1=1.0 / D, scalar2=None, op0=ALU.mult), reads=['ps5'], writes=['mu'])
                for g in range(4):
                    for q in range(2):
                        fc = g * 2 + q
                        P.op('dve', lambda e, fc=fc: e.tensor_tensor(out=tmpA[:], in0=hT[:, fc, sl], in1=mu[:], op=ALU.subtract),
                             reads=[HK(sl), 'mu'], writes=['tmpA'])
                        P.op('act', lambda e, q=q: e.activation(out=sq[:, q, :], in_=tmpA[:], func=AF.Square), reads=['tmpA'], writes=[('sq', q)])

                    def fn(e, g=g):
                        last = None
                        for q in range(2):
                            last = e.matmul(ps[6][:], lhsT=onesB[:], rhs=sq[:, q, :], start=(g == 0 and q == 0), stop=(g == 3 and q == 1))
                        return last
                    P.op('pe', fn, reads=[('sq', q) for q in range(2)] + ['onesB'], writes=['ps6'])
                P.op('act', lambda e: e.activation(out=rstd[:], in_=ps[6][:], func=AF.Ln, scale=1.0 / D, bias=EPS), reads=['ps6'], writes=['rstd'])
                P.op('act', lambda e: e.activation(out=rstd[:], in_=rstd[:], func=AF.Exp, scale=-0.5), reads=['rstd'], writes=['rstd'])
                for fc in range(8):
                    P.op('dve', lambda e, fc=fc: e.tensor_tensor(out=tmpA[:], in0=hT[:, fc, sl], in1=mu[:], op=ALU.subtract),
                         reads=[HK(sl), 'mu'], writes=['tmpA'])
                    P.op('dve', lambda e, fc=fc: e.scalar_tensor_tensor(out=tmpB[:], in0=tmpA[:], scalar=V[:, fc, R_LNG + bi:R_LNG + bi + 1], in1=rstd[:],
                                                                        op0=ALU.mult, op1=ALU.mult), reads=['tmpA', 'V', 'rstd'], writes=['tmpB'])
                    P.op('act', lambda e, fc=fc: e.activation(out=hT[:, fc, sl], in_=tmpB[:], func=AF.Silu, bias=V[:, fc, R_LNB + bi:R_LNB + bi + 1]),
                         reads=['tmpB', 'V'], writes=[HK(sl)])
            def PW2(tb):
                sl = slice(tb * 512, (tb + 1) * 512)
                for g in range(2):
                    s, key = pq[g]
                    w = wview(s, 0, 8, 512)
                    for n in range(4):
                        m = g * 4 + n
                        b = nbank()
                        P.op('pe', mm_group([(ps[b][:], w[:, kc, n * 128:(n + 1) * 128], hT[:, kc, sl], {}) for kc in range(8)]),
                             reads=[key, HK(sl)], writes=['ps%d' % b])
                        resid_evac(b, i, r, 2, m, sl)
            LN(0)
            for tb in range(T // 512):
                if tb + 1 < T // 512:
                    LN(tb + 1)
                PW2(tb)

        def run_phase(r):
            T = TP if r == 0 else TS
            x_d = xp_d if r == 0 else xs_d
            y_d = yp_d if r == 0 else ys_d
            seqs = [(k * 256, 256) for k in range(4)] if r == 0 else [(0, TS)]
            P.barrier()
            c = Carver()
            xin = [c.take([128, D]) for _ in range(2)]
            for tt in range(T // 128):
                xi = xin[tt % 2]
                xk = 'xin%d' % (tt % 2)
                P.dma('sp', xi, x_d[tt * 128:(tt + 1) * 128, :], writes=[xk])
                for g in range(2):
                    b = nbank()

                    def fn(e, g=g, b=b, xi=xi):
                        last = None
                        for q in range(4):
                            fc = g * 4 + q
                            last = e.transpose(ps[b][:, q * 128:(q + 1) * 128], xi[:, fc * 128:(fc + 1) * 128], identF[:])
                        return last
                    P.op('pe', fn, reads=[xk, 'identF'], writes=['ps%d' % b])
                    en = 'act' if g == 0 else 'dve'
                    src = ps[b][:].rearrange("p (a b) -> p a b", b=128)
                    dst = xT[:, g * 4:(g + 1) * 4, tt * 128:(tt + 1) * 128]
                    sl = slice(tt * 128, (tt + 1) * 128)
                    if en == 'act':
                        P.op('act', lambda e, src=src, dst=dst: e.activation(out=dst, in_=src, func=AF.Copy), reads=['ps%d' % b], writes=[XK(sl)])
                    else:
                        P.op('dve', lambda e, src=src, dst=dst: e.tensor_copy(out=dst, in_=src), reads=['ps%d' % b], writes=[XK(sl)])
            if r == 1:
                idx = c.take([128, 64])
                idx_i = idx.bitcast(I32)
                rowi = c.take([128, 32])
                coli = c.take([128, 64])
                ya = c.take([128, 64])
                yb = c.take([128, 64])
                yc = c.take([128, 64])
                Rtab = c.take([128, 4, 32])
                Ctab = c.take([128, 4, 64])
                om = c.take([128, 2])
                om2 = c.take([128, 2])
                pidx = c.take([128, 1])
                P.op('pool', lambda e: e.iota(idx_i, pattern=[[1, 64]], base=0, channel_multiplier=0), writes=['idx'])
                P.op('dve', lambda e: e.tensor_copy(out=coli, in_=idx_i), reads=['idx'], writes=['coli'])
                P.op('dve', lambda e: e.tensor_scalar(out=rowi, in0=coli[:, 0:32], scalar1=pinfo[:, 0:1], scalar2=pinfo[:, 2:3], op0=ALU.mult, op1=ALU.add),
                     reads=['coli', 'pinfo'], writes=['rowi'])
                P.op('dve', lambda e: e.tensor_scalar(out=coli, in0=coli, scalar1=pinfo[:, 0:1], scalar2=pinfo[:, 1:2], op0=ALU.mult, op1=ALU.add),
                     reads=['coli', 'pinfo'], writes=['coli'])
                P.op('pool', lambda e: e.iota(idx_i[:, 0:1], pattern=[[0, 1]], base=0, channel_multiplier=1), reads=['coli'], writes=['idx'])
                P.op('dve', lambda e: e.tensor_copy(out=pidx, in_=idx_i[:, 0:1]), reads=['idx'], writes=['pidx'])
                cst = math.log(10000.0) / 256.0
                for e2 in range(2):
                    P.op('act', lambda e, e2=e2: e.activation(out=om[:, e2:e2 + 1], in_=pidx, func=AF.Exp, scale=-cst, bias=-cst * 128 * e2),
                         reads=['pidx'], writes=['om'])
                P.op('dve', lambda e: e.tensor_scalar(out=om2, in0=om, scalar1=1.0 / (2 * math.pi), scalar2=None, op0=ALU.mult), reads=['om'], writes=['om2'])
                for fc in range(8):
                    src, n_, dst = (rowi, 32, Rtab[:, fc, :]) if fc < 4 else (coli, 64, Ctab[:, fc - 4, :])
                    ph = (0.0 if (fc % 4) < 2 else math.pi / 2) / (2 * math.pi)
                    a_, b_, c_ = ya[:, 0:n_], yb[:, 0:n_], yc[:, 0:n_]
                    P.op('dve', lambda e: e.tensor_scalar(out=a_, in0=src, scalar1=om2[:, fc % 2:fc % 2 + 1], scalar2=ph, op0=ALU.mult, op1=ALU.add),
                         reads=['rowi', 'coli', 'om2'], writes=['ya'])
                    P.op('dve', lambda e: e.tensor_copy(out=b_.bitcast(I32), in_=a_), reads=['ya'], writes=['yb'])
                    P.op('dve', lambda e: e.tensor_copy(out=c_, in_=b_.bitcast(I32)), reads=['yb'], writes=['yc'])
                    P.op('dve', lambda e: e.tensor_tensor(out=a_, in0=a_, in1=c_, op=ALU.subtract), reads=['ya', 'yc'], writes=['ya'])
                    P.op('act', lambda e: e.activation(out=b_, in_=a_, func=AF.Sin, scale=math.pi), reads=['ya'], writes=['yb'])
                    P.op('act', lambda e: e.activation(out=c_, in_=a_, func=AF.Sin, scale=math.pi / 2), reads=['ya'], writes=['yc'])
                    P.op('dve', lambda e: e.tensor_tensor(out=c_, in0=c_, in1=c_, op=ALU.mult), reads=['yc'], writes=['yc'])
                    P.op('dve', lambda e: e.tensor_scalar(out=c_, in0=c_, scalar1=-4.0, scalar2=2.0, op0=ALU.mult, op1=ALU.add), reads=['yc'], writes=['yc'])
                    P.op('dve', lambda e: e.tensor_tensor(out=dst, in0=b_, in1=c_, op=ALU.mult), reads=['yb', 'yc'], writes=[('ptab', fc)])
                for tb in range(TS // 512):
                    sl = slice(tb * 512, (tb + 1) * 512)
                    for fc in range(8):
                        xv = xT[:, fc, sl].rearrange("p (a b) -> p a b", b=64)
                        if fc < 4:
                            tv = Rtab[:, fc, tb * 8:(tb + 1) * 8].rearrange("p (a o) -> p a o", o=1).to_broadcast([128, 8, 64])
                        else:
                            tv = Ctab[:, fc - 4, :].rearrange("p (o b) -> p o b", o=1).to_broadcast([128, 8, 64])
                        P.op('dve', lambda e: e.tensor_tensor(out=xv, in0=xv, in1=tv, op=ALU.add), reads=[XK(sl), ('ptab', fc)], writes=[XK(sl)])
            for i in range(n_layers):
                if i % 2 == 0:
                    hgrn(i, r, T, seqs)
                else:
                    conv(i, r, T, seqs)
                mlp(i, r, T)
            P.barrier()
            c = Carver()
            yT = c.take([128, 8, 512])
            yo = [c.take([128, D]) for _ in range(2)]
            for tb in range(T // 512):
                sl = slice(tb * 512, (tb + 1) * 512)
                stats_rstd(lambda fc: xT[:, fc, sl], 0, 512, 8, D, [XK(sl)])
                for fc in range(8):
                    P.op('dve', lambda e, fc=fc: e.scalar_tensor_tensor(out=yT[:, fc, :], in0=xT[:, fc, sl], scalar=V[:, fc, R_FN:R_FN + 1], in1=rstd[:],
                                                                        op0=ALU.mult, op1=ALU.mult), reads=[XK(sl), 'V', 'rstd'], writes=[('yT', fc)])
                for q in range(4):
                    tt = tb * 4 + q
                    yb = yo[tt % 2]
                    yk = 'yo%d' % (tt % 2)
                    for g in range(2):
                        b = nbank()

                        def fn(e, g=g, b=b, q=q):
                            last = None
                            for u in range(4):
                                fc = g * 4 + u
                                last = e.transpose(ps[b][:, u * 128:(u + 1) * 128], yT[:, fc, q * 128:(q + 1) * 128], identF[:])
                            return last
                        P.op('pe', fn, reads=[('yT', fc) for fc in range(8)] + ['identF'], writes=['ps%d' % b])
                        if g == 0:
                            P.op('act', lambda e, b=b, yb=yb: e.activation(out=yb[:, 0:512], in_=ps[b][:], func=AF.Copy), reads=['ps%d' % b], writes=[(yk, 0)])
                        else:
                            P.op('dve', lambda e, b=b, yb=yb: e.tensor_copy(out=yb[:, 512:1024], in_=ps[b][:]), reads=['ps%d' % b], writes=[(yk, 1)])
                    P.dma('sp', y_d[tt * 128:(tt + 1) * 128, :], yb, reads=[(yk, 0), (yk, 1)], is_out=True)

        for r in phases:
            run_phase(r)
        P.barrier()

        block = es.enter_context(nc.Block())

        @block.tensor
        def _(e):
            for f in P.E['pe'].ops:
                f(e)

        @block.scalar
        def _(e):
            for f in P.E['act'].ops:
                f(e)

        @block.vector
        def _(e):
            for f in P.E['dve'].ops:
                f(e)

        @block.gpsimd
        def _(e):
            for f in P.E['pool'].ops:
                f(e)

        @block.sync
        def _(e):
            for f in P.E['sp'].ops:
                f(e)
    return nc


def make_in_maps(x_prompt, x_sample, c, state_hgrn, c_ctx, w_mod, b_mod, norm_mix, norm_mlp,
                 hgrn_w_in, hgrn_lb_fwd, hgrn_lb_bwd, hgrn_g_norm, hgrn_w_out,
                 conv_w_pw1, conv_w_dw, conv_b_dw, conv_ln_g, conv_ln_b, conv_w_pw2,
                 mlp_w1, mlp_w2, final_norm):
    f = lambda a: np.ascontiguousarray(np.asarray(a, dtype=np.float32))
    x_prompt, x_sample, c, state_hgrn, c_ctx = map(f, (x_prompt, x_sample, c, state_hgrn, c_ctx))
    hgrn_w_in = f(hgrn_w_in)
    w5 = hgrn_w_in.reshape(2, D, 5, NH, 128)
    whg = {}
    for half in range(2):
        order = [0, 1, 2, 3, 4] if half == 0 else [0, 1, 3, 2, 4]
        whg[half] = np.ascontiguousarray(w5[:, :, order].transpose(0, 3, 1, 2, 4).reshape(2, NH, D, 640))
    shared = dict(w_mod=f(w_mod), w_out=f(hgrn_w_out), w_pw1=f(conv_w_pw1), w_pw2=f(conv_w_pw2), w1=f(mlp_w1), w2=f(mlp_w2))
    in_maps = []
    for core in range(8):
        p, half = core // 2, core % 2
        xs = x_sample[p, half * TS:(half + 1) * TS]
        xp = x_prompt[4 * core:4 * core + 4]
        if half == 1:
            xs = xs[::-1]
            xp = xp[:, ::-1]
        vecs = np.zeros((128, D), np.float32)
        vecs[R_NMIX:R_NMIX + 4] = norm_mix
        vecs[R_NMLP:R_NMLP + 4] = norm_mlp
        lbf, lbb = (hgrn_lb_fwd, hgrn_lb_bwd) if half == 0 else (hgrn_lb_bwd, hgrn_lb_fwd)
        vecs[R_LB1:R_LB1 + 2] = lbf
        vecs[R_LB2:R_LB2 + 2] = lbb
        vecs[R_GN:R_GN + 2] = hgrn_g_norm
        vecs[R_BDW:R_BDW + 2] = conv_b_dw
        vecs[R_LNG:R_LNG + 2] = conv_ln_g
        vecs[R_LNB:R_LNB + 2] = conv_ln_b
        vecs[R_FN] = final_norm
        vecs[R_CV] = c_ctx
        vecs[R_CV + 1] = c[p]
        vecs[R_BMOD:R_BMOD + 24] = np.asarray(b_mod, np.float32).reshape(24, D)
        wdw = np.asarray(conv_w_dw, np.float32)
        if half == 1:
            wdw = wdw[:, ::-1]
        vecs[R_WDW:R_WDW + 62] = wdw.reshape(62, D)
        pinfo = np.zeros((128, 8), np.float32)
        sr = 1.0 if half == 0 else -1.0
        r0 = 0.0 if half == 0 else 63.0
        pinfo[:, 0] = sr
        pinfo[:, 1] = r0
        for tb in range(4):
            pinfo[:, 2 + tb] = r0 + sr * 8 * tb
        pinfo[:, 6] = 1.0 if half == 1 else 0.0
        pinfo[:, 7] = 1.0 if half == 0 else 0.0
        m = dict(shared)
        m.update(xs=np.ascontiguousarray(xs), xp=np.ascontiguousarray(xp.reshape(TP, D)), vecs=vecs,
                 s_init=np.ascontiguousarray(state_hgrn[p, :, half]), pinfo=pinfo, w_hg=whg[half])
        in_maps.append(m)
    return in_maps


def assemble(results):
    y_prompt = np.zeros((32, 256, D), np.float32)
    y_sample = np.zeros((4, 4096, D), np.float32)
    new_state = np.zeros((32, 2, 2, NH, 128, 128), np.float32)
    for core in range(8):
        p, half = core // 2, core % 2
        r = results[core]
        ys = np.asarray(r["ys"])
        yp = np.asarray(r["yp"]).reshape(4, 256, D)
        ns = np.asarray(r["ns"])
        if half == 1:
            ys = ys[::-1]
            yp = yp[:, ::-1]
            ns = ns[:, :, ::-1]
        y_sample[p, half * TS:(half + 1) * TS] = ys
        y_prompt[4 * core:4 * core + 4] = yp
        new_state[4 * core:4 * core + 4] = ns
    return y_prompt, y_sample, new_state


def kernel(**inputs):
    in_maps = make_in_maps(**inputs)
    nc = build_program()
    res = run_bass_kernel_spmd(nc, in_maps, core_ids=list(range(8)))
    return assemble(res.results)
```

```python
import math
import types
from contextlib import ExitStack

import numpy as np
import concourse.bass as bass
import concourse.mybir as mybir
from concourse.bass_utils import run_bass_kernel_spmd

F32 = mybir.dt.float32
BF16 = mybir.dt.bfloat16
I32 = mybir.dt.int32
AF = mybir.ActivationFunctionType
ALU = mybir.AluOpType

D = 1024
DEPTH = 4
NH = 8
CH = 32
KTAP = 31
EPS = 1e-6
KMAX = 1.0 - 1e-6
TS = 2048
TP = 1024
NDS = 24
NHW = 16

R_NMIX, R_NMLP, R_LB1, R_LB2, R_GN, R_BDW, R_LNG, R_LNB, R_FN, R_CV, R_BMOD, R_WDW = 0, 4, 8, 10, 12, 14, 16, 18, 20, 21, 23, 47


def _freeze(fn):
    if getattr(fn, "__closure__", None) is None:
        return fn
    cells = []
    for c in fn.__closure__:
        try:
            cells.append(types.CellType(c.cell_contents))
        except ValueError:
            cells.append(c)
    return types.FunctionType(fn.__code__, fn.__globals__, fn.__name__, fn.__defaults__, tuple(cells))


class Eng:
    def __init__(self, name, sem):
        self.name, self.sem, self.ops, self.n, self.seen = name, sem, [], 0, {}


class Prog:
    def __init__(self, nc, es):
        self.nc, self.es = nc, es
        self.E = {n: Eng(n, es.enter_context(nc.semaphore("s_" + n))) for n in ("pe", "act", "dve", "pool", "sp")}
        self.lastw, self.readers = {}, {}
        self.dsems = [es.enter_context(nc.semaphore("d%d" % i)) for i in range(NDS)]
        self.dcount = [0] * NDS
        self.dnext = 0
        self.dnext_sw = 0
        self.out_events = []

    def _wait(self, eng, ev):
        if ev[0] == 'c':
            _, src, seq = ev
            key, sem, val = src.name, src.sem, seq
        else:
            _, sem, val, key = ev
        if ev[0] == 'c' and eng.name == 'pe' and ev[1] is eng:
            return
        if eng.seen.get(key, 0) >= val:
            return
        eng.seen[key] = val
        eng.ops.append(lambda e, sem=sem, v=val: e.wait_ge(sem, v))

    def _deps(self, eng, reads, writes):
        for r in reads:
            if r in self.lastw:
                self._wait(eng, self.lastw[r])
        for w in writes:
            if w in self.lastw:
                self._wait(eng, self.lastw[w])
            for ev in self.readers.get(w, {}).values():
                self._wait(eng, ev)

    def _record(self, ev, key, reads, writes):
        for r in reads:
            self.readers.setdefault(r, {})[key] = ev
        for w in writes:
            self.lastw[w] = ev
            self.readers[w] = {}

    def op(self, en, fn, reads=(), writes=()):
        eng = self.E[en]
        fn = _freeze(fn)
        self._deps(eng, reads, writes)
        eng.n += 1
        ev = ('c', eng, eng.n)
        sem = eng.sem
        eng.ops.append(lambda e, fn=fn, sem=sem: fn(e).then_inc(sem, 1))
        self._record(ev, eng.name, reads, writes)
        return ev

    def dma(self, qn, out, in_, reads=(), writes=(), is_out=False):
        q = self.E[qn]
        self._deps(q, reads, writes)
        if qn == 'pool':
            i = NHW + self.dnext_sw
            self.dnext_sw = (self.dnext_sw + 1) % (NDS - NHW)
        else:
            i = self.dnext
            self.dnext = (i + 1) % NHW
        sem = self.dsems[i]
        key = ('d', i)
        if self.dcount[i] > 0:
            self._wait(q, ('d', sem, self.dcount[i], key))
        self.dcount[i] += 16
        ev = ('d', sem, self.dcount[i], key)
        q.ops.append(lambda e, o=out, a=in_, sem=sem: e.dma_start(out=o, in_=a).then_inc(sem, 16))
        self._record(ev, key, reads, writes)
        if is_out:
            self.out_events.append(ev)
        return ev

    def special(self, en, fn, sem, val, key, reads=(), writes=()):
        eng = self.E[en]
        fn = _freeze(fn)
        self._deps(eng, reads, writes)
        ev = ('d', sem, val, key)
        eng.ops.append(lambda e, fn=fn, sem=sem, val=val: fn(e).then_inc(sem, val))
        self._record(ev, key, reads, writes)
        self._wait(eng, ev)
        return ev

    def barrier(self, skip=()):
        evs = [('c', e, e.n) for e in self.E.values() if e.n > 0]
        for e in self.E.values():
            if e.name in skip:
                continue
            for ev in evs:
                self._wait(e, ev)
            for i in range(NDS):
                if self.dcount[i] > 0:
                    self._wait(e, ('d', self.dsems[i], self.dcount[i], ('d', i)))


def build_program(n_layers=DEPTH, phases=(0, 1), cc_inc=1, tiny=False):
    nc = bass.Bass("TRN2", target_bir_lowering=False)
    dt = lambda name, shape, kind="ExternalInput": nc.dram_tensor(name, list(shape), F32, kind=kind).ap()
    xs_d = dt("xs", [TS, D])
    xp_d = dt("xp", [TP, D])
    vecs_d = dt("vecs", [128, D])
    sinit_d = dt("s_init", [2, NH, 128, 128])
    pinfo_d = dt("pinfo", [128, 8])
    wmod_d = dt("w_mod", [1, 1] if tiny else [DEPTH, D, 6 * D])
    whg_d = dt("w_hg", [1, 1] if tiny else [2, NH, D, 640])
    wout_d = dt("w_out", [1, 1] if tiny else [2, D, D])
    pw1_d = dt("w_pw1", [1, 1] if tiny else [2, D, 2 * D])
    pw2_d = dt("w_pw2", [1, 1] if tiny else [2, D, D])
    w1_d = dt("w1", [1, 1] if tiny else [DEPTH, D, 4 * D])
    w2_d = dt("w2", [1, 1] if tiny else [DEPTH, 4 * D, D])
    ys_d = dt("ys", [TS, D], "ExternalOutput")
    yp_d = dt("yp", [TP, D], "ExternalOutput")
    ns_d = dt("ns", [4, 2, 2, NH, 128, 128], "ExternalOutput")
    cci_d = [[dt("cci%d_%d" % (a, h), [128, 128], "Internal") for h in range(NH)] for a in range(2)]
    cco_d = [[dt("cco%d_%d" % (a, h), [256, 128], "Internal") for h in range(NH)] for a in range(2)]
    hci_d = [nc.dram_tensor("hci%d" % b, [128, 8 * 16], BF16, kind="Internal").ap() for b in range(2)]
    hco_d = [nc.dram_tensor("hco%d" % b, [256, 8 * 16], BF16, kind="Internal").ap() for b in range(2)]

    with ExitStack() as es:
        P = Prog(nc, es)
        sb = lambda name, shape, dtp=F32: es.enter_context(nc.sbuf_tensor(name, list(shape), dtp))
        ccsems = [es.enter_context(nc.semaphore("cc%d" % i)) for i in range(20)]
        cc_next = [0]

        xT = sb("xT", [128, 8, TS])
        hT = sb("hT", [128, 8, TS], BF16)
        ringbuf = sb("ringbuf", [128, 16384], BF16)
        identF = sb("identF", [128, 128])
        identB = sb("identB", [128, 128], BF16)
        onesB = sb("onesB", [128, 128], BF16)
        M1 = sb("M1", [128, CH])
        M2 = sb("M2", [128, CH])
        cmask = sb("cmask", [128, 512])
        ones1 = sb("ones1", [128, 1])
        V = sb("V", [128, 8, 128])
        MOD = sb("MOD", [128, DEPTH, 2, 6, 8])
        WF = sb("WF", [128, DEPTH, 2, 2, 8])
        OML = sb("OML", [128, 2, 2, 8])
        pinfo = sb("pinfo_s", [128, 8])
        scT = sb("scT", [128, 8, 2], BF16)
        rstd = sb("rstd", [128, 512])
        sq = sb("sq", [128, 2, 512], BF16)
        tmpA = sb("tmpA", [128, 512])
        tmpB = sb("tmpB", [128, 512])
        WORK = sb("WORK", [128, 15872])

        ps = [es.enter_context(nc.psum_tensor("ps%d" % i, [128, 512], F32)) for i in range(7)]
        pst = es.enter_context(nc.psum_tensor("pst", [128, 1024], BF16))

        class Carver:
            def __init__(self):
                self.off = 0

            def take(self, shape, dtp=F32):
                n = int(np.prod(shape[1:]))
                words = n if dtp == F32 else (n + 1) // 2
                ap = WORK[:, self.off:self.off + words]
                self.off += words
                assert self.off <= 15872, self.off
                if dtp == BF16:
                    ap = ap.bitcast(BF16)[:, 0:n]
                if len(shape) == 3:
                    ap = ap.rearrange("p (a b) -> p a b", b=shape[2])
                return ap

        ring_i = [0]
        ring_big = [0]

        def stream(parts, big=False):
            if big:
                s = ring_big[0] % 2
                ring_big[0] += 1
                base = s * 8192
                key = 'ringH%d' % s
                wk = [key, 'ring%d' % (2 * s), 'ring%d' % (2 * s + 1)]
            else:
                s = ring_i[0] % 4
                ring_i[0] += 1
                base = s * 4096
                key = 'ring%d' % s
                wk = [key, 'ringH%d' % (s // 2)] + (['cmask16'] if s == 1 else [])
            for (c0, kk, nn, src) in parts:
                dst = ringbuf[:, base + c0:base + c0 + kk * nn].rearrange("p (k n) -> p k n", n=nn)
                P.dma('pool', dst, src, writes=wk)
            return base, key

        def wview(base, c0, kk, nn):
            return ringbuf[:, base + c0:base + c0 + kk * nn].rearrange("p (k n) -> p k n", n=nn)

        def kmajor(w2d, r0, nr, c0, ncol):
            return w2d[r0:r0 + nr, c0:c0 + ncol].rearrange("(k p) n -> p k n", p=128)

        XK = lambda sl: ('xT', sl.start // 512)
        HK = lambda sl: ('hT', sl.start // 512)
        bank_rr = [0]

        def nbank():
            b = bank_rr[0] % 3
            bank_rr[0] += 1
            return b

        def mm_group(items):
            def fn(e):
                last = None
                n = len(items)
                for i, (o, l, r, kw) in enumerate(items):
                    last = e.matmul(o, lhsT=l, rhs=r, start=(i == 0), stop=(i == n - 1), **kw)
                return last
            return fn

        P.op('pool', lambda e: e.memset(identF[:], 0.0), writes=['identF'])
        P.op('pool', lambda e: e.affine_select(out=identF[:], in_=identF[:], pattern=[[-1, 128]],
                                                compare_op=ALU.not_equal, fill=1.0, base=0, channel_multiplier=1),
             reads=['identF'], writes=['identF'])
        P.op('dve', lambda e: e.tensor_copy(out=identB[:], in_=identF[:]), reads=['identF'], writes=['identB'])
        P.op('pool', lambda e: e.memset(onesB[:], 1.0), writes=['onesB'])
        P.op('pool', lambda e: e.memset(ones1[:], 1.0), writes=['ones512'])
        P.op('pool', lambda e: e.memset(cmask[:], 1.0), writes=['cmask'])
        P.op('pool', lambda e: e.memset(cmask[:].rearrange("p (c t) -> p c t", t=CH)[:, :, 0:1], 0.0),
             reads=['cmask'], writes=['cmask'])
        cv = Carver()
        ip_f = cv.take([128, CH])
        it_f = cv.take([128, CH])
        ip_i = cv.take([128, CH]).bitcast(I32)
        it_i = cv.take([128, CH]).bitcast(I32)
        P.op('pool', lambda e: e.iota(ip_i, pattern=[[0, CH]], base=0, channel_multiplier=1), writes=['ip_i'])
        P.op('pool', lambda e: e.iota(it_i, pattern=[[1, CH]], base=0, channel_multiplier=0), writes=['it_i'])
        P.op('dve', lambda e: e.tensor_single_scalar(out=ip_i, in_=ip_i, scalar=CH - 1, op=ALU.bitwise_and),
             reads=['ip_i'], writes=['ip_i'])
        P.op('dve', lambda e: e.tensor_copy(out=ip_f, in_=ip_i), reads=['ip_i'], writes=['ip_f'])
        P.op('dve', lambda e: e.tensor_copy(out=it_f, in_=it_i), reads=['it_i'], writes=['it_f'])
        P.op('dve', lambda e: e.tensor_tensor(out=M1[:], in0=ip_f, in1=it_f, op=ALU.is_le),
             reads=['ip_f', 'it_f'], writes=['M1'])
        P.op('dve', lambda e: e.tensor_tensor(out=M2[:], in0=ip_f, in1=it_f, op=ALU.is_ge),
             reads=['ip_f', 'it_f'], writes=['M2'])

        vstage = cv.take([128, D])
        P.dma('sp', vstage, vecs_d[:, :], writes=['vstage'])
        P.dma('sp', pinfo[:], pinfo_d[:, :], writes=['pinfo'])
        for g in range(2):
            def fn(e, g=g):
                last = None
                for q in range(4):
                    fc = g * 4 + q
                    last = e.transpose(ps[g][:, q * 128:(q + 1) * 128], vstage[:, fc * 128:(fc + 1) * 128], identF[:])
                return last
            P.op('pe', fn, reads=['vstage', 'identF'], writes=['ps%d' % g])
            P.op('dve', lambda e, g=g: e.tensor_copy(out=V[:, g * 4:(g + 1) * 4, :],
                                                     in_=ps[g][:].rearrange("p (a b) -> p a b", b=128)),
                 reads=['ps%d' % g], writes=['V'])
        for d_, R_ in ((0, R_LB1), (1, R_LB2)):
            P.op('pool', lambda e, d_=d_: e.memset(OML[:, 0, d_, :], 1.0), writes=['OML'])
            P.op('dve', lambda e, R_=R_: e.tensor_tensor(out=tmpA[:, 0:8], in0=V[:, :, R_], in1=V[:, :, R_ + 1], op=ALU.subtract),
                 reads=['V'], writes=['tmpA'])
            P.op('act', lambda e, d_=d_: e.activation(out=OML[:, 1, d_, :], in_=tmpA[:, 0:8], func=AF.Sigmoid),
                 reads=['tmpA'], writes=['OML'])
        P.op('act', lambda e: e.activation(out=scT[:], in_=V[:, :, R_CV:R_CV + 2], func=AF.Silu), reads=['V'], writes=['scT'])
        for i in range(n_layers):
            psm = ps[6][:, 0:96]
            mq = {}
            for grp in range(12):
                for g2 in range(grp, min(12, grp + 3)):
                    if g2 not in mq:
                        mq[g2] = stream([(0, 8, 512, kmajor(wmod_d[i], 0, D, g2 * 512, 512))])
                s, key = mq[grp]
                w = wview(s, 0, 8, 512)
                for n in range(4):
                    j = grp * 4 + n
                    items = [(ps[6][:, 2 * j:2 * j + 2], w[:, kc, n * 128:(n + 1) * 128], scT[:, kc, :], {}) for kc in range(8)]
                    P.op('pe', mm_group(items), reads=[key, 'scT'], writes=[('psm', j)])
            for r in range(2):
                pin = psm.rearrange("p (m f r) -> p m f r", m=6, f=8)[:, :, :, r]
                bm = V[:, :, R_BMOD + i * 6:R_BMOD + i * 6 + 6].rearrange("p f m -> p m f")
                P.op('dve', lambda e, i=i, r=r, pin=pin, bm=bm: e.tensor_tensor(out=MOD[:, i, r, :, :], in0=pin, in1=bm, op=ALU.add),
                     reads=[('psm', j) for j in range(48)] + ['V'], writes=['MOD'])
                for sub, (mi, R_) in enumerate(((1, R_NMIX), (4, R_NMLP))):
                    P.op('dve', lambda e, i=i, r=r, sub=sub, mi=mi, R_=R_: e.scalar_tensor_tensor(
                        out=WF[:, i, r, sub, :], in0=MOD[:, i, r, mi, :], scalar=1.0, in1=V[:, :, R_ + i],
                        op0=ALU.add, op1=ALU.mult), reads=['MOD', 'V'], writes=['WF'])

        def stats_rstd(src_fn, T0, n, nfeat_chunks, dim, srckeys):
            ngr = (nfeat_chunks + 1) // 2
            for g in range(ngr):
                cnt = min(2, nfeat_chunks - g * 2)
                for q in range(cnt):
                    fc = g * 2 + q
                    src = src_fn(fc)
                    if q == 0:
                        P.op('act', lambda e, src=src, q=q: e.activation(out=sq[:, q, 0:n], in_=src, func=AF.Square),
                             reads=srckeys, writes=[('sq', q)])
                    else:
                        P.op('dve', lambda e, src=src, q=q: e.tensor_tensor(out=sq[:, q, 0:n], in0=src, in1=src, op=ALU.mult),
                             reads=srckeys, writes=[('sq', q)])

                def fn(e, g=g, cnt=cnt):
                    last = None
                    for q in range(cnt):
                        last = e.matmul(ps[6][:, 0:n], lhsT=onesB[:], rhs=sq[:, q, 0:n],
                                        start=(g == 0 and q == 0), stop=(g == ngr - 1 and q == cnt - 1))
                    return last
                P.op('pe', fn, reads=[('sq', q) for q in range(cnt)] + ['onesB'], writes=['ps6'])
            P.op('act', lambda e: e.activation(out=rstd[:, 0:n], in_=ps[6][:, 0:n], func=AF.Ln, scale=1.0 / dim, bias=EPS),
                 reads=['ps6'], writes=['rstd'])
            P.op('act', lambda e: e.activation(out=rstd[:, 0:n], in_=rstd[:, 0:n], func=AF.Exp, scale=-0.5), reads=['rstd'], writes=['rstd'])

        def norm_mod(i, sub, r, T):
            for tb in range(T // 512):
                sl = slice(tb * 512, (tb + 1) * 512)
                stats_rstd(lambda fc: xT[:, fc, sl], tb * 512, 512, 8, D, [XK(sl)])
                for fc in range(8):
                    tb_, tk_ = (tmpA, 'tmpA') if fc % 2 == 0 else (tmpB, 'tmpB')
                    P.op('dve', lambda e, fc=fc, tb_=tb_: e.scalar_tensor_tensor(
                        out=tb_[:], in0=xT[:, fc, sl], scalar=WF[:, i, r, sub, fc:fc + 1], in1=rstd[:],
                        op0=ALU.mult, op1=ALU.mult), reads=[XK(sl), 'WF', 'rstd'], writes=[tk_])
                    P.op('act', lambda e, fc=fc, tb_=tb_: e.activation(out=hT[:, fc, sl], in_=tb_[:], func=AF.Identity,
                                                                        bias=MOD[:, i, r, 0 if sub == 0 else 3, fc:fc + 1]),
                         reads=[tk_, 'MOD'], writes=[HK(sl)])

        def resid_evac(bank, i, r, gi, m, sl, n=512):
            P.op('dve', lambda e: e.scalar_tensor_tensor(
                out=xT[:, m, sl], in0=ps[bank][:, 0:n], scalar=MOD[:, i, r, gi, m:m + 1], in1=xT[:, m, sl],
                op0=ALU.mult, op1=ALU.add), reads=['ps%d' % bank, 'MOD', XK(sl)], writes=[XK(sl)])

        def mlp(i, r, T):
            wq = {}

            def wload(j):
                wq[j] = (stream([(0, 8, 512, kmajor(w1_d[i], 0, D, j * 512, 512))]),
                         stream([(0, 4, 1024, kmajor(w2_d[i], j * 512, 512, 0, D))]))
            wload(0)
            wload(1)
            norm_mod(i, 1, r, T)
            P.barrier(skip=('pe',))
            cvm = Carver()
            hid = [cvm.take([128, 4, 512], BF16) for _ in range(2)]
            rl = [cvm.take([128, 512], BF16) for _ in range(2)]
            NBk = T // 512
            items_ = [(j, tb) for j in range(8) for tb in range(NBk)]
            rc = [0]

            def S1(k):
                j, tb = items_[k]
                (sA, kA), _ = wq[j]
                wA = wview(sA, 0, 8, 512)
                sl = slice(tb * 512, (tb + 1) * 512)
                hb, hk = hid[k % 2], 'hid%d' % (k % 2)
                for n in range(4):
                    b = nbank()
                    its = [(ps[b][:], wA[:, kc, n * 128:(n + 1) * 128], hT[:, kc, sl], {}) for kc in range(8)]
                    P.op('pe', mm_group(its), reads=[kA, HK(sl)], writes=['ps%d' % b])
                    rb, rk = rl[rc[0] % 2], 'rl%d' % (rc[0] % 2)
                    rc[0] += 1
                    P.op('act', lambda e, b=b, rb=rb: e.activation(out=rb[:], in_=ps[b][:], func=AF.Relu),
                         reads=['ps%d' % b], writes=[rk])
                    P.op('pool', lambda e, hb=hb, n=n, rb=rb: e.tensor_tensor(out=hb[:, n, :], in0=rb[:], in1=rb[:], op=ALU.mult),
                         reads=[rk], writes=[(hk, n)])

            def S2(k):
                j, tb = items_[k]
                _, (sB, kB) = wq[j]
                wB = wview(sB, 0, 4, 1024)
                sl = slice(tb * 512, (tb + 1) * 512)
                hb, hk = hid[k % 2], 'hid%d' % (k % 2)
                for m in range(8):
                    b = nbank()
                    its = [(ps[b][:], wB[:, n, m * 128:(m + 1) * 128], hb[:, n, :], {}) for n in range(4)]
                    P.op('pe', mm_group(its), reads=[kB] + [(hk, n) for n in range(4)], writes=['ps%d' % b])
                    resid_evac(b, i, r, 5, m, sl)
            for k in range(len(items_) + 1):
                if k < len(items_):
                    S1(k)
                if k >= 1:
                    S2(k - 1)
                    jp, tbp = items_[k - 1]
                    if tbp == NBk - 1 and jp + 2 < 8:
                        wload(jp + 2)

        def hgrn(i, r, T, seqs):
            a = i // 2
            hq = {}

            def hload(h):
                hq[h] = stream([(0, 8, 640, kmajor(whg_d[a, h], 0, D, 0, 640)),
                                (5120, 1, 1024, kmajor(wout_d[a], h * 128, 128, 0, D))], big=True)
            hload(0)
            norm_mod(i, 0, r, T)
            P.barrier(skip=('pe',))
            NB = T // 512
            NT = T // 128
            NC = T // CH
            c = Carver()
            qf = c.take([128, T], BF16)
            vT = c.take([128, NT, 128], BF16)
            sg = c.take([128, T], BF16)
            o = c.take([128, T])
            q2g = c.take([128, T], BF16) if r == 1 else None
            qt = c.take([128, T], BF16)
            q16 = c.take([128, T], BF16)
            KA = c.take([128, T], BF16)
            KB = c.take([128, T], BF16)
            khT = c.take([128, NT, 128], BF16)
            vf = c.take([128, 512], BF16)
            kh_off = c.off
            khs = [c.take([128, 256], BF16) for _ in range(2)]
            og = vf
            kks = [c.take([128, 256]) for _ in range(2)]
            lgs = [c.take([128, 256]) for _ in range(2)]
            bbs = [c.take([128, 256]) for _ in range(2)]
            b16s = [c.take([128, 256]) for _ in range(2)]
            ebks = [c.take([128, 256]) for _ in range(2)]
            enbs = [c.take([128, 256]) for _ in range(2)]
            eL = c.take([128, NC])
            S32 = [[c.take([128, 128]) for _ in range(2)] for _ in range(len(seqs))]
            Sb = [[c.take([128, 128], BF16) for _ in range(2)] for _ in range(len(seqs))]
            Pm = [c.take([128, CH], BF16) for _ in range(4)]
            Sboth = c.take([128, 2, 128])
            Sin_b = c.take([128, 128], BF16)
            carry = c.take([128, 2])
            cmask16 = ringbuf[:, 6144:8192].bitcast(F32)[:, 0:512]
            P.op('pool', lambda e: e.memset(cmask16, 1.0), writes=['cmask16', 'ring1'])
            P.op('pool', lambda e: e.memset(cmask16.rearrange("p (c t) -> p c t", t=16)[:, :, 0:1], 0.0), reads=['cmask16'], writes=['cmask16', 'ring1'])
            scale = float(128 ** -0.5)
            H = CH // 2
            for h in range(NH):
                if h + 1 < NH:
                    hload(h + 1)
                s, key = hq.pop(h)
                w = wview(s, 0, 8, 640)
                wo = ringbuf[:, s + 5120:s + 6144]
                blocks = list(range(NB))[::-1]

                def proj(part, sl):
                    b = nbank()
                    items = [(ps[b][:], w[:, kc, part * 128:(part + 1) * 128], hT[:, kc, sl], {}) for kc in range(8)]
                    P.op('pe', mm_group(items), reads=[key, HK(sl)], writes=['ps%d' % b])
                    return b
                for tb in blocks:
                    sl = slice(tb * 512, (tb + 1) * 512)
                    b = proj(0, sl)
                    P.op('act', lambda e, b=b: e.activation(out=qf[:, sl], in_=ps[b][:], func=AF.Silu), reads=['ps%d' % b], writes=['qf'])
                    b = proj(1, sl)
                    P.op('act', lambda e, b=b: e.activation(out=vf[:], in_=ps[b][:], func=AF.Copy), reads=['ps%d' % b], writes=['vf'])
                    b = proj(4, sl)
                    P.op('act', lambda e, b=b: e.activation(out=sg[:, sl], in_=ps[b][:], func=AF.Silu), reads=['ps%d' % b], writes=['sg'])

                    def fnv(e):
                        last = None
                        for q in range(4):
                            last = e.transpose(pst[:, q * 128:(q + 1) * 128], vf[:, q * 128:(q + 1) * 128], identB[:])
                        return last
                    P.op('pe', fnv, reads=['vf', 'identB'], writes=['pst'])
                    P.op('act', lambda e, tb=tb: e.activation(out=vT[:, tb * 4:(tb + 1) * 4, :],
                                                               in_=pst[:, 0:512].rearrange("p (a b) -> p a b", b=128), func=AF.Copy),
                         reads=['pst'], writes=['vT'])
                for d in range(2):
                    rv = (lambda ap: ap) if d == 0 else (lambda ap: ap[:, ::-1])
                    li = CH - 1 if d == 0 else 0
                    ge, gl = (0, 1) if d == 0 else (1, 0)
                    l16 = H - 1 if d == 0 else 0
                    SBK = 256
                    NCS = SBK // CH
                    subs = [(tb, hb) for tb in blocks for hb in (1, 0)]
                    st_ = {}

                    def stage(k, u, s_):
                        tb, hb = u
                        sl = slice(tb * 512 + hb * SBK, tb * 512 + (hb + 1) * SBK)
                        tA = (tmpA, tmpB)[s_][:, 0:SBK]
                        tk = ('tmpA', 'tmpB')[s_]
                        kk_, lg_, bb_, ebk_, enb_, b16_, kh_ = kks[s_], lgs[s_], bbs[s_], ebks[s_], enbs[s_], b16s[s_], khs[s_]
                        K = lambda n: (n, s_)
                        if k == 0:
                            b = nbank()
                            st_[u] = b
                            items = [(ps[b][:, 0:SBK], w[:, kc, (2 + d) * 128:(3 + d) * 128], hT[:, kc, sl], {}) for kc in range(8)]
                            P.op('pe', mm_group(items), reads=[key, HK(sl)], writes=['ps%d' % b])
                        elif k == 1:
                            b = st_[u]
                            P.op('act', lambda e: e.activation(out=tA, in_=ps[b][:, 0:SBK], func=AF.Exp), reads=['ps%d' % b], writes=[tk])
                        elif k == 2:
                            P.op('act', lambda e: e.activation(out=tA, in_=tA, func=AF.Ln, bias=1.0), reads=[tk], writes=[tk])
                        elif k == 3:
                            P.op('act', lambda e: e.activation(out=tA, in_=tA, func=AF.Exp, scale=-1.0), reads=[tk], writes=[tk])
                        elif k == 4:
                            P.op('dve', lambda e: e.tensor_scalar(out=kk_, in0=tA, scalar1=OML[:, a, d, h:h + 1], scalar2=KMAX,
                                                                  op0=ALU.mult, op1=ALU.min), reads=[tk, 'OML'], writes=[K('kk')])
                        elif k == 5:
                            P.op('act', lambda e: e.activation(out=lg_, in_=kk_, func=AF.Ln, scale=-1.0, bias=1.0), reads=[K('kk')], writes=[K('lg')])
                        elif k == 6:
                            P.op('dve', lambda e: e.tensor_tensor_scan(out=rv(bb_), data0=cmask[:, 0:SBK], data1=rv(lg_), initial=0.0,
                                                                       op0=ALU.mult, op1=ALU.add), reads=['cmask', K('lg')], writes=[K('bb')])
                        elif k == 7:
                            P.op('act', lambda e: e.activation(out=ebk_, in_=bb_, func=AF.Exp), reads=[K('bb')], writes=[K('ebk')])
                        elif k == 8:
                            ebv = ebk_.rearrange("p (c t) -> p c t", t=CH)
                            c0 = (tb * 512 + hb * SBK) // CH
                            P.op('dve', lambda e: e.tensor_copy(out=eL[:, c0:c0 + NCS].rearrange("p (c o) -> p c o", o=1), in_=ebv[:, :, li:li + 1]),
                                 reads=[K('ebk')], writes=['eL'])
                            P.op('dve', lambda e: e.scalar_tensor_tensor(out=qt[:, sl], in0=qf[:, sl], scalar=scale, in1=ebk_,
                                                                         op0=ALU.mult, op1=ALU.mult), reads=['qf', K('ebk')], writes=['qt'])
                        elif k == 9:
                            bbv = bb_.rearrange("p (c t) -> p c t", t=CH)
                            P.op('dve', lambda e: e.tensor_tensor(out=tA.rearrange("p (c t) -> p c t", t=CH), in0=bbv,
                                                                  in1=bbv[:, :, li:li + 1].to_broadcast([128, NCS, CH]), op=ALU.subtract),
                                 reads=[K('bb')], writes=[tk])
                        elif k == 10:
                            P.op('act', lambda e: e.activation(out=enb_, in_=tA, func=AF.Exp, scale=-1.0), reads=[tk], writes=[K('enb')])
                        elif k == 11:
                            P.op('dve', lambda e: e.tensor_tensor(out=kh_, in0=kk_, in1=enb_, op=ALU.mult), reads=[K('kk'), K('enb')], writes=[K('kh')])
                        elif k == 12:
                            pc = 512 + s_ * SBK

                            def fnk(e):
                                last = None
                                for q in range(2):
                                    last = e.transpose(pst[:, pc + q * 128:pc + (q + 1) * 128], kh_[:, q * 128:(q + 1) * 128], identB[:])
                                return last
                            P.op('pe', fnk, reads=[K('kh'), 'identB'], writes=['pst'])
                            t2 = tb * 4 + hb * 2
                            P.op('act', lambda e: e.activation(out=khT[:, t2:t2 + 2, :],
                                                               in_=pst[:, pc:pc + SBK].rearrange("p (a b) -> p a b", b=128), func=AF.Copy),
                                 reads=['pst'], writes=['khT'])
                        elif k == 13:
                            P.op('dve', lambda e: e.tensor_tensor_scan(out=rv(b16_), data0=cmask16[:, 0:SBK], data1=rv(lg_), initial=0.0,
                                                                       op0=ALU.mult, op1=ALU.add), reads=['cmask16', K('lg')], writes=[K('b16')])
                        elif k == 14:
                            P.op('act', lambda e: e.activation(out=ebk_, in_=b16_, func=AF.Exp), reads=[K('b16')], writes=[K('ebk')])
                        elif k == 15:
                            P.op('dve', lambda e: e.scalar_tensor_tensor(out=q16[:, sl], in0=qf[:, sl], scalar=scale, in1=ebk_,
                                                                         op0=ALU.mult, op1=ALU.mult), reads=['qf', K('ebk')], writes=['q16'])
                        elif k == 16:
                            P.op('act', lambda e: e.activation(out=enb_, in_=b16_, func=AF.Exp, scale=-1.0), reads=[K('b16')], writes=[K('enb')])
                        elif k == 17:
                            P.op('dve', lambda e: e.tensor_tensor(out=KA[:, sl], in0=kk_, in1=enb_, op=ALU.mult), reads=[K('kk'), K('enb')], writes=['KA'])
                        elif k == 18:
                            KAv = KA[:, sl].rearrange("p (c g t) -> p c g t", g=2, t=H)
                            KBv = KB[:, sl].rearrange("p (c g t) -> p c g t", g=2, t=H)
                            e16 = ebk_.rearrange("p (c g t) -> p c g t", g=2, t=H)
                            P.op('dve', lambda e: e.tensor_tensor(out=KBv[:, :, ge, :], in0=KAv[:, :, ge, :],
                                                                  in1=e16[:, :, ge, l16:l16 + 1].to_broadcast([128, NCS, H]), op=ALU.mult),
                                 reads=['KA', K('ebk')], writes=['KB'])
                            P.op('pool', lambda e: e.tensor_copy(out=KBv[:, :, gl, :], in_=KAv[:, :, gl, :]), reads=['KA'], writes=['KB'])
                        elif k == 19 and d == 1 and r == 1:
                            first = (tb == NB - 1 and hb == 1)
                            init = 0.0 if first else carry[:, 0:1]
                            P.op('dve', lambda e: e.tensor_tensor_scan(out=bb_[:, ::-1], data0=ones1[:, 0:1].to_broadcast([128, SBK]), data1=lg_[:, ::-1],
                                                                       initial=init, op0=ALU.mult, op1=ALU.add),
                                 reads=['ones512', K('lg'), 'carry'], writes=[K('bb')])
                            P.op('dve', lambda e: e.tensor_copy(out=carry[:, 0:1], in_=bb_[:, 0:1]), reads=[K('bb')], writes=['carry'])
                        elif k == 20 and d == 1 and r == 1:
                            P.op('act', lambda e: e.activation(out=enb_, in_=bb_, func=AF.Exp), reads=[K('bb')], writes=[K('enb')])
                        elif k == 21 and d == 1 and r == 1:
                            P.op('dve', lambda e: e.scalar_tensor_tensor(out=q2g[:, sl], in0=qf[:, sl], scalar=scale, in1=enb_,
                                                                         op0=ALU.mult, op1=ALU.mult), reads=['qf', K('enb')], writes=['q2g'])
                    NSTG = 22
                    pairs_ = [(subs[pi], subs[pi + 1]) for pi in range(0, len(subs), 2)]
                    stage(0, pairs_[0][0], 0)
                    stage(0, pairs_[0][1], 1)
                    for pj, (uA, uB) in enumerate(pairs_):
                        for k in range(1, NSTG + 1):
                            if k < NSTG:
                                stage(k, uA, 0)
                            if k - 1 >= 1:
                                stage(k - 1, uB, 1)
                            if k == 5 and pj + 1 < len(pairs_):
                                stage(0, pairs_[pj + 1][0], 0)
                                stage(0, pairs_[pj + 1][1], 1)
                    Mk = M1 if d == 0 else M2
                    nl_ = len(seqs)
                    spl = 4 // nl_
                    steps = []
                    for (t0, L) in seqs:
                        tl = list(range(t0 // 128, (t0 + L) // 128))
                        od = [0, 1, 2, 3]
                        if d == 1:
                            tl, od = tl[::-1], od[::-1]
                        steps.append([(tt, j, k == 0, k == 3) for tt in tl for k, j in enumerate(od)])
                    NS = len(steps[0])
                    for l in range(nl_):
                        if r == 1 and d == 0:
                            P.dma('sp', S32[l][0], sinit_d[a, h], writes=[('S32', l, 0)])
                            P.op('act', lambda e, l=l: e.activation(out=Sb[l][0][:], in_=S32[l][0][:], func=AF.Copy),
                                 reads=[('S32', l, 0)], writes=[('Sb', l, 0)])
                        else:
                            P.op('pool', lambda e, l=l: e.memset(S32[l][0][:], 0.0), writes=[('S32', l, 0)])
                            P.op('pool', lambda e, l=l: e.memset(Sb[l][0][:], 0.0), writes=[('Sb', l, 0)])

                    def emitD(l, n):
                        tt, j, _, _ = steps[l][n]
                        slot = l * spl + n % spl
                        pr = slice(j * CH, (j + 1) * CH)
                        P.op('pe', lambda e, slot=slot, pr=pr, tt=tt, j=j: e.matmul(
                            ps[slot][:, 0:128], lhsT=khT[pr, tt, :], rhs=vT[pr, tt, :], start=True, stop=True,
                            tile_position=(j * CH, 0)), reads=['khT', 'vT'], writes=['ps%d' % slot])
                    look = spl - 1
                    for l in range(nl_):
                        for n in range(min(look, NS)):
                            emitD(l, n)
                    for n in range(NS):
                        for l in range(nl_):
                            tt, j, first, last = steps[l][n]
                            tslot = l * spl + (n // 4) % spl
                            if first:
                                def fnA(e, tt=tt, tslot=tslot):
                                    last_ = None
                                    for jj in range(4):
                                        c0 = tt * 128 + jj * CH
                                        for g in range(2):
                                            lhs = KA if g == ge else KB
                                            last_ = e.matmul(ps[5][jj * CH:(jj + 1) * CH, tslot * CH + g * H:tslot * CH + (g + 1) * H],
                                                             lhsT=lhs[:, c0:c0 + CH], rhs=q16[:, c0 + g * H:c0 + (g + 1) * H],
                                                             start=True, stop=True, tile_position=(0, jj * CH))
                                    return last_
                                P.op('pe', fnA, reads=['KA', 'KB', 'q16'], writes=['ps5'])
                                P.op('dve', lambda e, Mk=Mk, tslot=tslot: e.tensor_tensor(out=Pm[tslot][:], in0=ps[5][:, tslot * CH:(tslot + 1) * CH],
                                                                                          in1=Mk[:], op=ALU.mult),
                                     reads=['ps5', 'M1', 'M2'], writes=[('Pm', tslot)])
                            if n + look < NS:
                                emitD(l, n + look)
                            slot = l * spl + n % spl
                            ts_ = slice(tt * 128 + j * CH, tt * 128 + (j + 1) * CH)
                            pr = slice(j * CH, (j + 1) * CH)
                            ci = (tt * 128 + j * CH) // CH
                            oc = ps[4][:, tslot * 128 + j * CH:tslot * 128 + (j + 1) * CH]
                            cur, nxt = n % 2, (n + 1) % 2

                            def fnBC(e, pr=pr, ts_=ts_, oc=oc, tt=tt, j=j, tslot=tslot, l=l, cur=cur):
                                e.matmul(oc, lhsT=vT[pr, tt, :], rhs=Pm[tslot][pr, :], start=True, stop=False, tile_position=(j * CH, 0))
                                return e.matmul(oc, lhsT=Sb[l][cur][:], rhs=qt[:, ts_], start=False, stop=True)
                            P.op('pe', fnBC, reads=['vT', ('Pm', tslot), ('Sb', l, cur), 'qt'], writes=['ps4'])
                            ecol = eL[:, ci:ci + 1]
                            P.op('dve', lambda e, ecol=ecol, l=l, cur=cur, nxt=nxt, slot=slot: e.scalar_tensor_tensor(
                                out=S32[l][nxt][:], in0=S32[l][cur][:], scalar=ecol, in1=ps[slot][:, 0:128],
                                op0=ALU.mult, op1=ALU.add), reads=[('S32', l, cur), 'eL', 'ps%d' % slot], writes=[('S32', l, nxt)])
                            P.op('act', lambda e, l=l, nxt=nxt: e.activation(out=Sb[l][nxt][:], in_=S32[l][nxt][:], func=AF.Copy),
                                 reads=[('S32', l, nxt)], writes=[('Sb', l, nxt)])
                            if last:
                                osl = slice(tt * 128, (tt + 1) * 128)
                                pso = ps[4][:, tslot * 128:(tslot + 1) * 128]
                                if d == 0:
                                    P.op('act', lambda e, osl=osl, pso=pso: e.activation(out=o[:, osl], in_=pso, func=AF.Copy),
                                         reads=['ps4'], writes=[('o', tt)])
                                else:
                                    P.op('dve', lambda e, osl=osl, pso=pso: e.tensor_tensor(out=o[:, osl], in0=o[:, osl], in1=pso, op=ALU.add),
                                         reads=['ps4', ('o', tt)], writes=[('o', tt)])
                    for l in range(nl_):
                        fin = S32[l][NS % 2]
                        fk = ('S32', l, NS % 2)
                        if r == 0:
                            P.dma('sp', ns_d[l, a, d, h], fin, reads=[fk], is_out=True)
                        elif d == 0:
                            P.dma('sp', cci_d[a][h], fin, reads=[fk], writes=[('cci', a, h)])
                            k_ = cc_next[0]
                            cc_next[0] += 1
                            P.special('pool', lambda e, a=a, h=h: e.collective_compute(
                                "AllGather", ALU.bypass, replica_groups=[[0, 1], [2, 3], [4, 5], [6, 7]],
                                ins=[cci_d[a][h]], outs=[cco_d[a][h]]), ccsems[k_], cc_inc, ('cc', k_),
                                reads=[('cci', a, h)], writes=[('cco', a, h)])
                if r == 1:
                    P.dma('sp', Sboth, cco_d[a][h].rearrange("(r p) v -> p r v", p=128), reads=[('cco', a, h)], writes=['Sboth'])
                    P.op('dve', lambda e: e.tensor_scalar(out=Sboth[:, 0, :], in0=Sboth[:, 0, :], scalar1=pinfo[:, 6:7], scalar2=None, op0=ALU.mult),
                         reads=['Sboth', 'pinfo'], writes=['Sboth'])
                    P.op('dve', lambda e: e.scalar_tensor_tensor(out=Sin_b[:], in0=Sboth[:, 1, :], scalar=pinfo[:, 7:8], in1=Sboth[:, 0, :],
                                                                 op0=ALU.mult, op1=ALU.add), reads=['Sboth', 'pinfo'], writes=['Sin_b'])
                    for tb in range(NB):
                        sl = slice(tb * 512, (tb + 1) * 512)
                        b = nbank()
                        P.op('pe', lambda e, b=b, sl=sl: e.matmul(ps[b][:], lhsT=Sin_b[:], rhs=q2g[:, sl], start=True, stop=True),
                             reads=['Sin_b', 'q2g'], writes=['ps%d' % b])
                        ok_ = [('o', tb * 4 + q) for q in range(4)]
                        P.op('dve', lambda e, b=b, sl=sl: e.tensor_tensor(out=o[:, sl], in0=o[:, sl], in1=ps[b][:], op=ALU.add),
                             reads=['ps%d' % b] + ok_, writes=ok_)
                og2 = WORK[:, kh_off:kh_off + 256].bitcast(BF16)
                ogs = [(og, ['vf']), (og2, [('kh', 0), ('kh', 1)])]

                def post_prep(tb):
                    sl = slice(tb * 512, (tb + 1) * 512)
                    ok_ = [('o', tb * 4 + q) for q in range(4)]
                    ob, okeys = ogs[tb % 2]
                    stats_rstd(lambda fc: o[:, sl], 0, 512, 1, 128, ok_)
                    P.op('dve', lambda e: e.scalar_tensor_tensor(out=tmpA[:], in0=o[:, sl], scalar=V[:, h, R_GN + a:R_GN + a + 1], in1=rstd[:],
                                                                 op0=ALU.mult, op1=ALU.mult), reads=ok_ + ['V', 'rstd'], writes=['tmpA'])
                    P.op('dve', lambda e: e.tensor_tensor(out=ob, in0=tmpA[:], in1=sg[:, sl], op=ALU.mult),
                         reads=['tmpA', 'sg'], writes=okeys)

                def post_mm(tb):
                    sl = slice(tb * 512, (tb + 1) * 512)
                    ob, okeys = ogs[tb % 2]
                    for m in range(8):
                        b = nbank()
                        P.op('pe', lambda e, b=b, m=m: e.matmul(ps[b][:], lhsT=wo[:, m * 128:(m + 1) * 128], rhs=ob, start=True, stop=True),
                             reads=[key] + okeys, writes=['ps%d' % b])
                        resid_evac(b, i, r, 2, m, sl)
                post_prep(0)
                for tb in range(NB):
                    if tb + 1 < NB:
                        post_prep(tb + 1)
                    post_mm(tb)

        def conv(i, r, T, seqs):
            bi = i // 2
            cq = {}
            for g in range(2):
                cq[g] = (stream([(0, 8, 512, kmajor(pw1_d[bi], 0, D, g * 512, 512))]),
                         stream([(0, 8, 512, kmajor(pw1_d[bi], 0, D, D + g * 512, 512))]))
            norm_mod(i, 0, r, T)
            P.barrier(skip=('pe',))
            c = Carver()
            PADW = T + 30 * len(seqs)
            upad = c.take([128, 8, PADW], BF16)
            dgs = [c.take([128, KTAP, 128], BF16) for _ in range(2)]
            tS = c.take([128, 512])
            mu = c.take([128, 512])
            hb_st = c.take([128, 8 * 16], BF16)
            hb_in = c.take([128, 2, 8 * 16], BF16)
            P.op('pool', lambda e: e.memset(upad[:], 0.0), writes=['upad'])
            offs = [t0 + 30 * si + 15 for si, (t0, L) in enumerate(seqs)]
            for g in range(2):
                (sA, kA), (sG, kG) = cq[g]
                wA = wview(sA, 0, 8, 512)
                wG = wview(sG, 0, 8, 512)
                for n in range(4):
                    fc = g * 4 + n
                    for si, (t0, L) in enumerate(seqs):
                        for tb in range(max(1, L // 512)):
                            n_ = min(512, L)
                            sl = slice(t0 + tb * 512, t0 + tb * 512 + n_)
                            b1 = nbank()
                            P.op('pe', mm_group([(ps[b1][:, 0:n_], wG[:, kc, n * 128:(n + 1) * 128], hT[:, kc, sl], {}) for kc in range(8)]),
                                 reads=[kG, HK(sl)], writes=['ps%d' % b1])
                            P.op('act', lambda e, b1=b1, n_=n_: e.activation(out=tS[:, 0:n_], in_=ps[b1][:, 0:n_], func=AF.Sigmoid),
                                 reads=['ps%d' % b1], writes=['tS'])
                            b2 = nbank()
                            P.op('pe', mm_group([(ps[b2][:, 0:n_], wA[:, kc, n * 128:(n + 1) * 128], hT[:, kc, sl], {}) for kc in range(8)]),
                                 reads=[kA, HK(sl)], writes=['ps%d' % b2])
                            po = offs[si] + tb * 512
                            P.op('dve', lambda e, b2=b2, n_=n_, fc=fc, po=po: e.tensor_tensor(
                                out=upad[:, fc, po:po + n_], in0=ps[b2][:, 0:n_], in1=tS[:, 0:n_], op=ALU.mult),
                                reads=['ps%d' % b2, 'tS'], writes=['upad'])
            if r == 1:
                po = offs[0] + T
                P.op('dve', lambda e: e.tensor_copy(out=hb_st[:].rearrange("p (f t) -> p f t", t=16)[:, :, 0:15], in_=upad[:, :, po - 15:po][:, :, ::-1]),
                     reads=['upad'], writes=['hb_st'])
                P.dma('sp', hci_d[bi], hb_st, reads=['hb_st'], writes=[('hci', bi)])
                k_ = cc_next[0]
                cc_next[0] += 1
                P.special('pool', lambda e: e.collective_compute("AllGather", ALU.bypass, replica_groups=[[0, 1], [2, 3], [4, 5], [6, 7]],
                                                                 ins=[hci_d[bi]], outs=[hco_d[bi]]), ccsems[k_], cc_inc, ('cc', k_),
                          reads=[('hci', bi)], writes=[('hco', bi)])
                P.dma('sp', hb_in, hco_d[bi].rearrange("(r p) v -> p r v", p=128), reads=[('hco', bi)], writes=['hb_in'])
                P.op('dve', lambda e: e.tensor_scalar(out=hb_in[:, 0, :], in0=hb_in[:, 0, :], scalar1=pinfo[:, 6:7], scalar2=None, op0=ALU.mult),
                     reads=['hb_in', 'pinfo'], writes=['hb_in'])
                P.op('dve', lambda e: e.scalar_tensor_tensor(out=upad[:, :, po:po + 15], in0=hb_in[:, 1, :].rearrange("p (f t) -> p f t", t=16)[:, :, 0:15],
                                                             scalar=pinfo[:, 7:8], in1=hb_in[:, 0, :].rearrange("p (f t) -> p f t", t=16)[:, :, 0:15],
                                                             op0=ALU.mult, op1=ALU.add), reads=['hb_in', 'pinfo', 'upad'], writes=['upad'])
            def build_dg(fc):
                dg = dgs[fc % 2]
                for j in range(KTAP):
                    en = ('dve', 'pool', 'act')[j % 3]
                    sc_ = V[:, fc, R_WDW + bi * KTAP + j:R_WDW + bi * KTAP + j + 1]
                    if en == 'act':
                        P.op('act', lambda e, j=j, sc_=sc_, dg=dg: e.activation(out=dg[:, j, :], in_=identB[:], func=AF.Copy, scale=sc_),
                             reads=['identB', 'V'], writes=[('dg', fc % 2, j)])
                    else:
                        P.op(en, lambda e, j=j, sc_=sc_, dg=dg: e.tensor_scalar(out=dg[:, j, :], in0=identB[:], scalar1=sc_, scalar2=None, op0=ALU.mult),
                             reads=['identB', 'V'], writes=[('dg', fc % 2, j)])
            build_dg(0)
            for fc in range(8):
                if fc + 1 < 8:
                    build_dg(fc + 1)
                dg = dgs[fc % 2]
                for si, (t0, L) in enumerate(seqs):
                    for tb in range(max(1, L // 512)):
                        n_ = min(512, L)
                        po = offs[si] + tb * 512 - 15
                        b = nbank()
                        P.op('pe', mm_group([(ps[b][:, 0:n_], dg[:, j, :], upad[:, fc, po + j:po + j + n_], {}) for j in range(KTAP)]),
                             reads=[('dg', fc % 2, j) for j in range(KTAP)] + ['upad'], writes=['ps%d' % b])
                        sl = slice(t0 + tb * 512, t0 + tb * 512 + n_)
                        P.op('act', lambda e, b=b, n_=n_, sl=sl, fc=fc: e.activation(out=hT[:, fc, sl], in_=ps[b][:, 0:n_], func=AF.Identity,
                                                                                     bias=V[:, fc, R_BDW + bi:R_BDW + bi + 1]),
                             reads=['ps%d' % b, 'V'], writes=[HK(sl)])
            pq = [stream([(0, 8, 512, kmajor(pw2_d[bi], 0, D, g * 512, 512))]) for g in range(2)]
            def LN(tb):
                sl = slice(tb * 512, (tb + 1) * 512)
                P.op('pe', mm_group([(ps[5][:], onesB[:], hT[:, fc, sl], {}) for fc in range(8)]), reads=['onesB', HK(sl)], writes=['ps5'])
                P.op('dve', lambda e: e.tensor_scalar(out=mu[:], in0=ps[5][:], scalar1=1.0 / D, scalar2=None, op0=ALU.mult), reads=['ps5'], writes=['mu'])
                for g in range(4):
                    for q in range(2):
                        fc = g * 2 + q
                        P.op('dve', lambda e, fc=fc: e.tensor_tensor(out=tmpA[:], in0=hT[:, fc, sl], in1=mu[:], op=ALU.subtract),
                             reads=[HK(sl), 'mu'], writes=['tmpA'])
                        P.op('act', lambda e, q=q: e.activation(out=sq[:, q, :], in_=tmpA[:], func=AF.Square), reads=['tmpA'], writes=[('sq', q)])

                    def fn(e, g=g):
                        last = None
                        for q in range(2):
                            last = e.matmul(ps[6][:], lhsT=onesB[:], rhs=sq[:, q, :], start=(g == 0 and q == 0), stop=(g == 3 and q == 1))
                        return last
                    P.op('pe', fn, reads=[('sq', q) for q in range(2)] + ['onesB'], writes=['ps6'])
                P.op('act', lambda e: e.activation(out=rstd[:], in_=ps[6][:], func=AF.Ln, scale=1.0 / D, bias=EPS), reads=['ps6'], writes=['rstd'])
                P.op('act', lambda e: e.activation(out=rstd[:], in_=rstd[:], func=AF.Exp, scale=-0.5), reads=['rstd'], writes=['rstd'])
                for fc in range(8):
                    P.op('dve', lambda e, fc=fc: e.tensor_tensor(out=tmpA[:], in0=hT[:, fc, sl], in1=mu[:], op=ALU.subtract),
                         reads=[HK(sl), 'mu'], writes=['tmpA'])
                    P.op('dve', lambda e, fc=fc: e.scalar_tensor_tensor(out=tmpB[:], in0=tmpA[:], scalar=V[:, fc, R_LNG + bi:R_LNG + bi + 1], in1=rstd[:],
                                                                        op0=ALU.mult, op1=ALU.mult), reads=['tmpA', 'V', 'rstd'], writes=['tmpB'])
                    P.op('act', lambda e, fc=fc: e.activation(out=hT[:, fc, sl], in_=tmpB[:], func=AF.Silu, bias=V[:, fc, R_LNB + bi:R_LNB + bi + 1]),
                         reads=['tmpB', 'V'], writes=[HK(sl)])
            def PW2(tb):
                sl = slice(tb * 512, (tb + 1) * 512)
                for g in range(2):
                    s, key = pq[g]
                    w = wview(s, 0, 8, 512)
                    for n in range(4):
                        m = g * 4 + n
                        b = nbank()
                        P.op('pe', mm_group([(ps[b][:], w[:, kc, n * 128:(n + 1) * 128], hT[:, kc, sl], {}) for kc in range(8)]),
                             reads=[key, HK(sl)], writes=['ps%d' % b])
                        resid_evac(b, i, r, 2, m, sl)
            LN(0)
            for tb in range(T // 512):
                if tb + 1 < T // 512:
                    LN(tb + 1)
                PW2(tb)

        def run_phase(r):
            T = TP if r == 0 else TS
            x_d = xp_d if r == 0 else xs_d
            y_d = yp_d if r == 0 else ys_d
            seqs = [(k * 256, 256) for k in range(4)] if r == 0 else [(0, TS)]
            P.barrier()
            c = Carver()
            xin = [c.take([128, D]) for _ in range(2)]
            for tt in range(T // 128):
                xi = xin[tt % 2]
                xk = 'xin%d' % (tt % 2)
                P.dma('sp', xi, x_d[tt * 128:(tt + 1) * 128, :], writes=[xk])
                for g in range(2):
                    b = nbank()

                    def fn(e, g=g, b=b, xi=xi):
                        last = None
                        for q in range(4):
                            fc = g * 4 + q
                            last = e.transpose(ps[b][:, q * 128:(q + 1) * 128], xi[:, fc * 128:(fc + 1) * 128], identF[:])
                        return last
                    P.op('pe', fn, reads=[xk, 'identF'], writes=['ps%d' % b])
                    en = 'act' if g == 0 else 'dve'
                    src = ps[b][:].rearrange("p (a b) -> p a b", b=128)
                    dst = xT[:, g * 4:(g + 1) * 4, tt * 128:(tt + 1) * 128]
                    sl = slice(tt * 128, (tt + 1) * 128)
                    if en == 'act':
                        P.op('act', lambda e, src=src, dst=dst: e.activation(out=dst, in_=src, func=AF.Copy), reads=['ps%d' % b], writes=[XK(sl)])
                    else:
                        P.op('dve', lambda e, src=src, dst=dst: e.tensor_copy(out=dst, in_=src), reads=['ps%d' % b], writes=[XK(sl)])
            if r == 1:
                idx = c.take([128, 64])
                idx_i = idx.bitcast(I32)
                rowi = c.take([128, 32])
                coli = c.take([128, 64])
                ya = c.take([128, 64])
                yb = c.take([128, 64])
                yc = c.take([128, 64])
                Rtab = c.take([128, 4, 32])
                Ctab = c.take([128, 4, 64])
                om = c.take([128, 2])
                om2 = c.take([128, 2])
                pidx = c.take([128, 1])
                P.op('pool', lambda e: e.iota(idx_i, pattern=[[1, 64]], base=0, channel_multiplier=0), writes=['idx'])
                P.op('dve', lambda e: e.tensor_copy(out=coli, in_=idx_i), reads=['idx'], writes=['coli'])
                P.op('dve', lambda e: e.tensor_scalar(out=rowi, in0=coli[:, 0:32], scalar1=pinfo[:, 0:1], scalar2=pinfo[:, 2:3], op0=ALU.mult, op1=ALU.add),
                     reads=['coli', 'pinfo'], writes=['rowi'])
                P.op('dve', lambda e: e.tensor_scalar(out=coli, in0=coli, scalar1=pinfo[:, 0:1], scalar2=pinfo[:, 1:2], op0=ALU.mult, op1=ALU.add),
                     reads=['coli', 'pinfo'], writes=['coli'])
                P.op('pool', lambda e: e.iota(idx_i[:, 0:1], pattern=[[0, 1]], base=0, channel_multiplier=1), reads=['coli'], writes=['idx'])
                P.op('dve', lambda e: e.tensor_copy(out=pidx, in_=idx_i[:, 0:1]), reads=['idx'], writes=['pidx'])
                cst = math.log(10000.0) / 256.0
                for e2 in range(2):
                    P.op('act', lambda e, e2=e2: e.activation(out=om[:, e2:e2 + 1], in_=pidx, func=AF.Exp, scale=-cst, bias=-cst * 128 * e2),
                         reads=['pidx'], writes=['om'])
                P.op('dve', lambda e: e.tensor_scalar(out=om2, in0=om, scalar1=1.0 / (2 * math.pi), scalar2=None, op0=ALU.mult), reads=['om'], writes=['om2'])
                for fc in range(8):
                    src, n_, dst = (rowi, 32, Rtab[:, fc, :]) if fc < 4 else (coli, 64, Ctab[:, fc - 4, :])
                    ph = (0.0 if (fc % 4) < 2 else math.pi / 2) / (2 * math.pi)
                    a_, b_, c_ = ya[:, 0:n_], yb[:, 0:n_], yc[:, 0:n_]
                    P.op('dve', lambda e: e.tensor_scalar(out=a_, in0=src, scalar1=om2[:, fc % 2:fc % 2 + 1], scalar2=ph, op0=ALU.mult, op1=ALU.add),
                         reads=['rowi', 'coli', 'om2'], writes=['ya'])
                    P.op('dve', lambda e: e.tensor_copy(out=b_.bitcast(I32), in_=a_), reads=['ya'], writes=['yb'])
                    P.op('dve', lambda e: e.tensor_copy(out=c_, in_=b_.bitcast(I32)), reads=['yb'], writes=['yc'])
                    P.op('dve', lambda e: e.tensor_tensor(out=a_, in0=a_, in1=c_, op=ALU.subtract), reads=['ya', 'yc'], writes=['ya'])
                    P.op('act', lambda e: e.activation(out=b_, in_=a_, func=AF.Sin, scale=math.pi), reads=['ya'], writes=['yb'])
                    P.op('act', lambda e: e.activation(out=c_, in_=a_, func=AF.Sin, scale=math.pi / 2), reads=['ya'], writes=['yc'])
                    P.op('dve', lambda e: e.tensor_tensor(out=c_, in0=c_, in1=c_, op=ALU.mult), reads=['yc'], writes=['yc'])
                    P.op('dve', lambda e: e.tensor_scalar(out=c_, in0=c_, scalar1=-4.0, scalar2=2.0, op0=ALU.mult, op1=ALU.add), reads=['yc'], writes=['yc'])
                    P.op('dve', lambda e: e.tensor_tensor(out=dst, in0=b_, in1=c_, op=ALU.mult), reads=['yb', 'yc'], writes=[('ptab', fc)])
                for tb in range(TS // 512):
                    sl = slice(tb * 512, (tb + 1) * 512)
                    for fc in range(8):
                        xv = xT[:, fc, sl].rearrange("p (a b) -> p a b", b=64)
                        if fc < 4:
                            tv = Rtab[:, fc, tb * 8:(tb + 1) * 8].rearrange("p (a o) -> p a o", o=1).to_broadcast([128, 8, 64])
                        else:
                            tv = Ctab[:, fc - 4, :].rearrange("p (o b) -> p o b", o=1).to_broadcast([128, 8, 64])
                        P.op('dve', lambda e: e.tensor_tensor(out=xv, in0=xv, in1=tv, op=ALU.add), reads=[XK(sl), ('ptab', fc)], writes=[XK(sl)])
            for i in range(n_layers):
                if i % 2 == 0:
                    hgrn(i, r, T, seqs)
                else:
                    conv(i, r, T, seqs)
                mlp(i, r, T)
            P.barrier()
            c = Carver()
            yT = c.take([128, 8, 512])
            yo = [c.take([128, D]) for _ in range(2)]
            for tb in range(T // 512):
                sl = slice(tb * 512, (tb + 1) * 512)
                stats_rstd(lambda fc: xT[:, fc, sl], 0, 512, 8, D, [XK(sl)])
                for fc in range(8):
                    P.op('dve', lambda e, fc=fc: e.scalar_tensor_tensor(out=yT[:, fc, :], in0=xT[:, fc, sl], scalar=V[:, fc, R_FN:R_FN + 1], in1=rstd[:],
                                                                        op0=ALU.mult, op1=ALU.mult), reads=[XK(sl), 'V', 'rstd'], writes=[('yT', fc)])
                for q in range(4):
                    tt = tb * 4 + q
                    yb = yo[tt % 2]
                    yk = 'yo%d' % (tt % 2)
                    for g in range(2):
                        b = nbank()

                        def fn(e, g=g, b=b, q=q):
                            last = None
                            for u in range(4):
                                fc = g * 4 + u
                                last = e.transpose(ps[b][:, u * 128:(u + 1) * 128], yT[:, fc, q * 128:(q + 1) * 128], identF[:])
                            return last
                        P.op('pe', fn, reads=[('yT', fc) for fc in range(8)] + ['identF'], writes=['ps%d' % b])
                        if g == 0:
                            P.op('act', lambda e, b=b, yb=yb: e.activation(out=yb[:, 0:512], in_=ps[b][:], func=AF.Copy), reads=['ps%d' % b], writes=[(yk, 0)])
                        else:
                            P.op('dve', lambda e, b=b, yb=yb: e.tensor_copy(out=yb[:, 512:1024], in_=ps[b][:]), reads=['ps%d' % b], writes=[(yk, 1)])
                    P.dma('sp', y_d[tt * 128:(tt + 1) * 128, :], yb, reads=[(yk, 0), (yk, 1)], is_out=True)

        for r in phases:
            run_phase(r)
        P.barrier()

        block = es.enter_context(nc.Block())

        @block.tensor
        def _(e):
            for f in P.E['pe'].ops:
                f(e)

        @block.scalar
        def _(e):
            for f in P.E['act'].ops:
                f(e)

        @block.vector
        def _(e):
            for f in P.E['dve'].ops:
                f(e)

        @block.gpsimd
        def _(e):
            for f in P.E['pool'].ops:
                f(e)

        @block.sync
        def _(e):
            for f in P.E['sp'].ops:
                f(e)
    return nc


def make_in_maps(x_prompt, x_sample, c, state_hgrn, c_ctx, w_mod, b_mod, norm_mix, norm_mlp,
                 hgrn_w_in, hgrn_lb_fwd, hgrn_lb_bwd, hgrn_g_norm, hgrn_w_out,
                 conv_w_pw1, conv_w_dw, conv_b_dw, conv_ln_g, conv_ln_b, conv_w_pw2,
                 mlp_w1, mlp_w2, final_norm):
    f = lambda a: np.ascontiguousarray(np.asarray(a, dtype=np.float32))
    x_prompt, x_sample, c, state_hgrn, c_ctx = map(f, (x_prompt, x_sample, c, state_hgrn, c_ctx))
    hgrn_w_in = f(hgrn_w_in)
    w5 = hgrn_w_in.reshape(2, D, 5, NH, 128)
    whg = {}
    for half in range(2):
        order = [0, 1, 2, 3, 4] if half == 0 else [0, 1, 3, 2, 4]
        whg[half] = np.ascontiguousarray(w5[:, :, order].transpose(0, 3, 1, 2, 4).reshape(2, NH, D, 640))
    shared = dict(w_mod=f(w_mod), w_out=f(hgrn_w_out), w_pw1=f(conv_w_pw1), w_pw2=f(conv_w_pw2), w1=f(mlp_w1), w2=f(mlp_w2))
    in_maps = []
    for core in range(8):
        p, half = core // 2, core % 2
        xs = x_sample[p, half * TS:(half + 1) * TS]
        xp = x_prompt[4 * core:4 * core + 4]
        if half == 1:
            xs = xs[::-1]
            xp = xp[:, ::-1]
        vecs = np.zeros((128, D), np.float32)
        vecs[R_NMIX:R_NMIX + 4] = norm_mix
        vecs[R_NMLP:R_NMLP + 4] = norm_mlp
        lbf, lbb = (hgrn_lb_fwd, hgrn_lb_bwd) if half == 0 else (hgrn_lb_bwd, hgrn_lb_fwd)
        vecs[R_LB1:R_LB1 + 2] = lbf
        vecs[R_LB2:R_LB2 + 2] = lbb
        vecs[R_GN:R_GN + 2] = hgrn_g_norm
        vecs[R_BDW:R_BDW + 2] = conv_b_dw
        vecs[R_LNG:R_LNG + 2] = conv_ln_g
        vecs[R_LNB:R_LNB + 2] = conv_ln_b
        vecs[R_FN] = final_norm
        vecs[R_CV] = c_ctx
        vecs[R_CV + 1] = c[p]
        vecs[R_BMOD:R_BMOD + 24] = np.asarray(b_mod, np.float32).reshape(24, D)
        wdw = np.asarray(conv_w_dw, np.float32)
        if half == 1:
            wdw = wdw[:, ::-1]
        vecs[R_WDW:R_WDW + 62] = wdw.reshape(62, D)
        pinfo = np.zeros((128, 8), np.float32)
        sr = 1.0 if half == 0 else -1.0
        r0 = 0.0 if half == 0 else 63.0
        pinfo[:, 0] = sr
        pinfo[:, 1] = r0
        for tb in range(4):
            pinfo[:, 2 + tb] = r0 + sr * 8 * tb
        pinfo[:, 6] = 1.0 if half == 1 else 0.0
        pinfo[:, 7] = 1.0 if half == 0 else 0.0
        m = dict(shared)
        m.update(xs=np.ascontiguousarray(xs), xp=np.ascontiguousarray(xp.reshape(TP, D)), vecs=vecs,
                 s_init=np.ascontiguousarray(state_hgrn[p, :, half]), pinfo=pinfo, w_hg=whg[half])
        in_maps.append(m)
    return in_maps


def assemble(results):
    y_prompt = np.zeros((32, 256, D), np.float32)
    y_sample = np.zeros((4, 4096, D), np.float32)
    new_state = np.zeros((32, 2, 2, NH, 128, 128), np.float32)
    for core in range(8):
        p, half = core // 2, core % 2
        r = results[core]
        ys = np.asarray(r["ys"])
        yp = np.asarray(r["yp"]).reshape(4, 256, D)
        ns = np.asarray(r["ns"])
        if half == 1:
            ys = ys[::-1]
            yp = yp[:, ::-1]
            ns = ns[:, :, ::-1]
        y_sample[p, half * TS:(half + 1) * TS] = ys
        y_prompt[4 * core:4 * core + 4] = yp
        new_state[4 * core:4 * core + 4] = ns
    return y_prompt, y_sample, new_state


def kernel(**inputs):
    in_maps = make_in_maps(**inputs)
    nc = build_program()
    res = run_bass_kernel_spmd(nc, in_maps, core_ids=list(range(8)))
    return assemble(res.results)
```
